# Optimizing a Trainium2 kernel written in Bass

```python
import math
import jax
import jax.numpy as jnp
from jax import lax
import numpy as np

D_MODEL = 2048
BATCH = 4
SEQ = 2048
DEPTH = 2

CHUNK = 64
D_MIX = D_MODEL
N_MIXERS = 4
GROUP_W = D_MIX // N_MIXERS
NORM_EPS = 1e-6

GLA_HEADS = 4
GLA_DV = GROUP_W // GLA_HEADS
GLA_DK = GLA_DV // 2
GLA_GATE_RANK = 16
GLA_GATE_NORMALIZER = 16.0

RWKV_HEAD = 64
RWKV_HEADS = GROUP_W // RWKV_HEAD
RWKV_W_RANK = 64
RWKV_A_RANK = 64
RWKV_V_RANK = 32
RWKV_LN_EPS = 64e-5
RWKV_DECAY_OFFSET = 0.5

SSD_HEADDIM = 64
SSD_HEADS = GROUP_W // SSD_HEADDIM
SSD_GROUPS = 2
SSD_STATE = 128
SSD_CONV = 4
SSD_XBC = GROUP_W + 2 * SSD_GROUPS * SSD_STATE

MLSTM_HEADS = 4
MLSTM_HEAD = GROUP_W // MLSTM_HEADS
MLSTM_CONV = 4

kernel_name = 'hybrid_gla_rwkv7_ssd_mlstm_heads'


def rwkv_shift_width(layer):
    return 3 * GROUP_W + RWKV_W_RANK + RWKV_A_RANK + (RWKV_V_RANK if layer > 0 else 0)


def in_layout(layer):
    return (('gla_q', GLA_HEADS * GLA_DK), ('gla_k', GLA_HEADS * GLA_DK), ('gla_v', GROUP_W),
            ('gla_gk', GLA_GATE_RANK), ('gla_z', GROUP_W),
            ('rwkv_shift', rwkv_shift_width(layer)), ('rwkv_z', GROUP_W),
            ('ssd_xbc', SSD_XBC), ('ssd_dt', SSD_HEADS), ('ssd_z', GROUP_W),
            ('mlstm_qk', 2 * GROUP_W), ('mlstm_v', GROUP_W), ('mlstm_i', MLSTM_HEADS),
            ('mlstm_f', MLSTM_HEADS), ('mlstm_o', GROUP_W), ('mlstm_z', GROUP_W))


def split_cols(proj, layout):
    idx = np.cumsum([w for _, w in layout])[:-1].tolist()
    parts = jnp.split(proj, idx, axis=-1)
    return {name: part for (name, _), part in zip(layout, parts)}


def rms_norm(x, g, eps=NORM_EPS):
    xf = x.astype(jnp.float32)
    y = xf * lax.rsqrt(jnp.mean(xf * xf, axis=-1, keepdims=True) + eps)
    return (y * g.astype(jnp.float32)).astype(x.dtype)


def head_rms_norm(y, g, eps=NORM_EPS):
    y = y * lax.rsqrt(jnp.mean(y * y, axis=-1, keepdims=True) + eps)
    return y * g.astype(jnp.float32).reshape(y.shape[-2:])


def head_layer_norm(y, g, eps):
    mu = jnp.mean(y, axis=-1, keepdims=True)
    yc = y - mu
    y = yc * lax.rsqrt(jnp.mean(yc * yc, axis=-1, keepdims=True) + eps)
    return y * g.astype(jnp.float32).reshape(y.shape[-2:])


def causal_depthwise_conv(x, w, b):
    k = w.shape[0]
    y = lax.conv_general_dilated(x, w[:, None, :].astype(x.dtype), window_strides=(1,),
                                 padding=[(k - 1, 0)], dimension_numbers=('NWC', 'WIO', 'NWC'),
                                 feature_group_count=x.shape[-1])
    return y + b.astype(x.dtype)


def token_shift(x):
    return jnp.pad(x, ((0, 0), (1, 0), (0, 0)))[:, :-1]


def causal_mask():
    return jnp.tril(jnp.ones((CHUNK, CHUNK), dtype=bool))


def chunk_state_scan(decay, inc):
    def step(state, xs):
        d, u = xs
        return d * state + u, state
    _, prev = lax.scan(step, jnp.zeros_like(inc[0]), (decay, inc))
    return prev


def gla_mixer(q, k, v, gk_low, z, gk_w2, gk_b, norm_g):
    f32 = jnp.float32
    B, S, _ = q.shape
    nc = S // CHUNK
    q = q.astype(f32).reshape(B, nc, CHUNK, GLA_HEADS, GLA_DK) * GLA_DK ** -0.5
    k = k.astype(f32).reshape(B, nc, CHUNK, GLA_HEADS, GLA_DK)
    v = v.astype(f32).reshape(B, nc, CHUNK, GLA_HEADS, GLA_DV)
    gk = gk_low.astype(f32) @ gk_w2.astype(f32) + gk_b.astype(f32)
    log_a = jax.nn.log_sigmoid(gk).reshape(B, nc, CHUNK, GLA_HEADS, GLA_DK) / GLA_GATE_NORMALIZER
    cum = jnp.cumsum(log_a, axis=2)
    last = cum[:, :, -1:]
    qg = q * jnp.exp(cum)
    kg = k * jnp.exp(-cum)
    kd = k * jnp.exp(last - cum)
    att = jnp.where(causal_mask(), jnp.einsum('bnlhd,bnshd->bnhls', qg, kg), 0.0)
    o = jnp.einsum('bnhls,bnshv->bnlhv', att, v)
    inc = jnp.einsum('bnshd,bnshv->nbhdv', kd, v)
    decay = jnp.exp(last[:, :, 0]).transpose(1, 0, 2, 3)[..., None]
    s_prev = chunk_state_scan(decay, inc)
    o = o + jnp.einsum('bnlhd,nbhdv->bnlhv', qg, s_prev)
    o = head_rms_norm(o.reshape(B, S, GLA_HEADS, GLA_DV), norm_g)
    return o.reshape(B, S, GROUP_W) * jax.nn.silu(z.astype(f32))


def rwkv7_mixer(feats, z, v_first, mu, w0, w2, a0, a2, k_k, k_a, r_k, ln_g, ln_b, v0, v2):
    f32 = jnp.float32
    B, S, _ = feats.shape
    f = feats.astype(f32)
    f = f + mu.astype(f32) * (token_shift(f) - f)
    widths = [GROUP_W, GROUP_W, GROUP_W, RWKV_W_RANK, RWKV_A_RANK] + ([] if v0 is None else [RWKV_V_RANK])
    parts = jnp.split(f, np.cumsum(widths)[:-1].tolist(), axis=-1)
    r, k, v, wl, al = parts[:5]
    w_log = -jax.nn.softplus(-(w0 + jnp.tanh(wl) @ w2)) - RWKV_DECAY_OFFSET
    decay = jnp.exp(-jnp.exp(w_log))
    a = jax.nn.sigmoid(a0 + al @ a2)
    if v0 is None:
        v_first = v
    else:
        v = v + (v_first - v) * jax.nn.sigmoid(v0 + parts[5] @ v2)
    hs = lambda t: t.reshape(B, S, RWKV_HEADS, RWKV_HEAD)
    kk = hs(k * k_k)
    kk = kk / jnp.maximum(jnp.sqrt(jnp.sum(kk * kk, axis=-1, keepdims=True)), 1e-12)
    k = k * (1.0 + (a - 1.0) * k_a)
    r, k, v, decay, a = hs(r), hs(k), hs(v), hs(decay), hs(a)
    b_vec = kk * a

    def step(state, xs):
        r_t, k_t, v_t, w_t, kk_t, b_t = xs
        sa = jnp.einsum('bhij,bhj->bhi', state, -kk_t)
        state = (state * w_t[:, :, None, :] + sa[..., None] * b_t[:, :, None, :]
                 + v_t[..., None] * k_t[:, :, None, :])
        return state, jnp.einsum('bhij,bhj->bhi', state, r_t)

    xs = tuple(t.transpose(1, 0, 2, 3) for t in (r, k, v, decay, kk, b_vec))
    init = jnp.zeros((B, RWKV_HEADS, RWKV_HEAD, RWKV_HEAD), f32)
    _, y = lax.scan(step, init, xs)
    y = y.transpose(1, 0, 2, 3)
    y = head_layer_norm(y, ln_g, RWKV_LN_EPS) + ln_b.astype(f32).reshape(RWKV_HEADS, RWKV_HEAD)
    bonus = jnp.sum(r * k * r_k.astype(f32), axis=-1, keepdims=True) * v
    y = (y + bonus).reshape(B, S, GROUP_W) * jax.nn.silu(z.astype(f32))
    return y, v_first


def ssd_mixer(xbc, dt_raw, z, conv_w, conv_b, dt_bias, a_log, d_skip, norm_g):
    f32 = jnp.float32
    B, S, _ = xbc.shape
    nc = S // CHUNK
    E = SSD_HEADS // SSD_GROUPS
    xbc = jax.nn.silu(causal_depthwise_conv(xbc.astype(f32), conv_w.astype(f32), conv_b.astype(f32)))
    x, bm, cm = jnp.split(xbc, [GROUP_W, GROUP_W + SSD_GROUPS * SSD_STATE], axis=-1)
    dt = jax.nn.softplus(dt_raw.astype(f32) + dt_bias.astype(f32))
    a = -jnp.exp(a_log.astype(f32))
    x = x.reshape(B, S, SSD_HEADS, SSD_HEADDIM)
    xdt = (x * dt[..., None]).reshape(B, nc, CHUNK, SSD_GROUPS, E, SSD_HEADDIM)
    bm = bm.reshape(B, nc, CHUNK, SSD_GROUPS, SSD_STATE)
    cm = cm.reshape(B, nc, CHUNK, SSD_GROUPS, SSD_STATE)
    cum = jnp.cumsum((dt * a).reshape(B, nc, CHUNK, SSD_GROUPS, E), axis=2)
    cum_t = cum.transpose(0, 1, 3, 4, 2)
    seg = cum_t[..., :, None] - cum_t[..., None, :]
    L = jnp.exp(jnp.where(causal_mask(), seg, -jnp.inf))
    cb = jnp.einsum('bnlgs,bnmgs->bnglm', cm, bm)
    y = jnp.einsum('bnglm,bngelm,bnmgep->bnlgep', cb, L, xdt)
    last = cum[:, :, -1:]
    inc = jnp.einsum('bnmgs,bnmge,bnmgep->nbgeps', bm, jnp.exp(last - cum), xdt)
    decay = jnp.exp(last[:, :, 0]).transpose(1, 0, 2, 3)[..., None, None]
    s_prev = chunk_state_scan(decay, inc)
    y = y + jnp.einsum('bnlgs,nbgeps,bnlge->bnlgep', cm, s_prev, jnp.exp(cum))
    y = y.reshape(B, S, SSD_HEADS, SSD_HEADDIM) + d_skip.astype(f32)[:, None] * x
    y = y.reshape(B, S, GROUP_W) * jax.nn.silu(z.astype(f32))
    return rms_norm(y, norm_g)


def mlstm_mixer(qk, v, i_pre, f_pre, o_pre, z, conv_w, conv_b, ig_b, fg_b, norm_g):
    f32 = jnp.float32
    B, S, _ = qk.shape
    nc = S // CHUNK
    H, Dh = MLSTM_HEADS, MLSTM_HEAD
    qk = jax.nn.silu(causal_depthwise_conv(qk.astype(f32), conv_w.astype(f32), conv_b.astype(f32)))
    q, k = jnp.split(qk, 2, axis=-1)
    q = q.reshape(B, nc, CHUNK, H, Dh)
    k = k.reshape(B, nc, CHUNK, H, Dh) * Dh ** -0.5
    v = v.astype(f32).reshape(B, nc, CHUNK, H, Dh)
    log_i = (i_pre.astype(f32) + ig_b.astype(f32)).reshape(B, nc, CHUNK, H)
    log_f = jax.nn.log_sigmoid(f_pre.astype(f32) + fg_b.astype(f32)).reshape(B, nc, CHUNK, H)
    cum = jnp.cumsum(log_f, axis=2)
    last = cum[:, :, -1]
    g = last[:, :, None] - cum + log_i
    g_max = jnp.max(g, axis=2)
    w_end = jnp.exp(g - g_max[:, :, None])
    c_loc = jnp.einsum('bnsh,bnshd,bnshe->nbhde', w_end, k, v)
    n_loc = jnp.einsum('bnsh,bnshd->nbhd', w_end, k)

    def step(carry, xs):
        c, n, m = carry
        lf, gm, cl, nl = xs
        m_new = jnp.maximum(lf + m, gm)
        a_old = jnp.exp(lf + m - m_new)
        a_new = jnp.exp(gm - m_new)
        c_new = a_old[..., None, None] * c + a_new[..., None, None] * cl
        n_new = a_old[..., None] * n + a_new[..., None] * nl
        return (c_new, n_new, m_new), (c, n, m)

    init = (jnp.zeros((B, H, Dh, Dh), f32), jnp.zeros((B, H, Dh), f32), jnp.zeros((B, H), f32))
    _, (c_prev, n_prev, m_prev) = lax.scan(
        step, init, (last.transpose(1, 0, 2), g_max.transpose(1, 0, 2), c_loc, n_loc))
    cum_t = cum.transpose(0, 1, 3, 2)
    log_d = cum_t[..., :, None] - cum_t[..., None, :] + log_i.transpose(0, 1, 3, 2)[..., None, :]
    log_d = jnp.where(causal_mask(), log_d, -jnp.inf)
    m_inter = cum_t + m_prev.transpose(1, 0, 2)[..., None]
    m_l = jnp.maximum(m_inter, jnp.max(log_d, axis=-1))
    wqk = jnp.einsum('bnlhd,bnshd->bnhls', q, k) * jnp.exp(log_d - m_l[..., None])
    w_inter = jnp.exp(m_inter - m_l)
    num = (jnp.einsum('bnhls,bnshe->bnlhe', wqk, v)
           + w_inter.transpose(0, 1, 3, 2)[..., None] * jnp.einsum('bnlhd,nbhde->bnlhe', q, c_prev))
    den = jnp.sum(wqk, axis=-1) + w_inter * jnp.einsum('bnlhd,nbhd->bnhl', q, n_prev)
    den = jnp.maximum(jnp.abs(den), jnp.exp(-m_l))
    h = num / den.transpose(0, 1, 3, 2)[..., None]
    h = h.reshape(B, S, H, Dh) * jax.nn.sigmoid(o_pre.astype(f32)).reshape(B, S, H, Dh)
    h = head_layer_norm(h, norm_g, NORM_EPS)
    return h.reshape(B, S, GROUP_W) * jax.nn.silu(z.astype(f32))


def hybrid_layer(x, v_first, layer, p):
    h = rms_norm(x, p['norm_g'])
    c = split_cols(h @ p['w_in'], in_layout(layer))
    y_gla = gla_mixer(c['gla_q'], c['gla_k'], c['gla_v'], c['gla_gk'], c['gla_z'],
                      p['gla_gk_w2'], p['gla_gk_b'], p['gla_norm_g'])
    y_rwkv, v_first = rwkv7_mixer(c['rwkv_shift'], c['rwkv_z'], v_first, p['rwkv_mu'],
                                  p['rwkv_w0'], p['rwkv_w2'], p['rwkv_a0'], p['rwkv_a2'],
                                  p['rwkv_k_k'], p['rwkv_k_a'], p['rwkv_r_k'],
                                  p['rwkv_ln_g'], p['rwkv_ln_b'], p['rwkv_v0'], p['rwkv_v2'])
    y_ssd = ssd_mixer(c['ssd_xbc'], c['ssd_dt'], c['ssd_z'], p['ssd_conv_w'], p['ssd_conv_b'],
                      p['ssd_dt_bias'], p['ssd_a_log'], p['ssd_d'], p['ssd_norm_g'])
    y_mlstm = mlstm_mixer(c['mlstm_qk'], c['mlstm_v'], c['mlstm_i'], c['mlstm_f'], c['mlstm_o'],
                          c['mlstm_z'], p['mlstm_conv_w'], p['mlstm_conv_b'], p['mlstm_ig_b'],
                          p['mlstm_fg_b'], p['mlstm_norm_g'])
    y = jnp.concatenate([y_gla, y_rwkv, y_ssd, y_mlstm], axis=-1).astype(x.dtype)
    return x + y @ p['w_out'], v_first


def setup_inputs(seed: int = 0) -> dict:
    key = jax.random.key(seed)
    keys = iter(jax.random.split(key, 128))
    normal = lambda shape, scale: scale * jax.random.normal(next(keys), shape, jnp.float32)
    gain = lambda n: 1.0 + normal((n,), 0.02)
    inputs = {'x': normal((BATCH, SEQ, D_MODEL), 1.0)}
    for l in range(DEPTH):
        n_in = sum(w for _, w in in_layout(l))
        p = {}
        p['norm_g'] = gain(D_MODEL)
        p['w_in'] = normal((D_MODEL, n_in), D_MODEL ** -0.5)
        p['w_out'] = normal((D_MIX, D_MODEL), 0.5 * D_MIX ** -0.5)
        p['gla_gk_w2'] = normal((GLA_GATE_RANK, GLA_HEADS * GLA_DK), GLA_GATE_RANK ** -0.5)
        p['gla_gk_b'] = normal((GLA_HEADS * GLA_DK,), 0.1)
        p['gla_norm_g'] = gain(GROUP_W)
        p['rwkv_mu'] = jax.random.uniform(next(keys), (rwkv_shift_width(l),), jnp.float32)
        p['rwkv_w0'] = normal((GROUP_W,), 0.5)
        p['rwkv_w2'] = normal((RWKV_W_RANK, GROUP_W), 0.1)
        p['rwkv_a0'] = normal((GROUP_W,), 0.1)
        p['rwkv_a2'] = normal((RWKV_A_RANK, GROUP_W), 0.1)
        if l > 0:
            p['rwkv_v0'] = normal((GROUP_W,), 0.1)
            p['rwkv_v2'] = normal((RWKV_V_RANK, GROUP_W), 0.1)
        p['rwkv_k_k'] = 0.85 + normal((GROUP_W,), 0.02)
        p['rwkv_k_a'] = 1.0 + normal((GROUP_W,), 0.02)
        p['rwkv_r_k'] = normal((RWKV_HEADS, RWKV_HEAD), 0.1)
        p['rwkv_ln_g'] = gain(GROUP_W)
        p['rwkv_ln_b'] = normal((GROUP_W,), 0.02)
        p['ssd_conv_w'] = normal((SSD_CONV, SSD_XBC), SSD_CONV ** -0.5)
        p['ssd_conv_b'] = normal((SSD_XBC,), 0.02)
        dt0 = jnp.exp(jax.random.uniform(next(keys), (SSD_HEADS,), jnp.float32,
                                         math.log(1e-3), math.log(1e-1)))
        p['ssd_dt_bias'] = dt0 + jnp.log(-jnp.expm1(-dt0))
        p['ssd_a_log'] = jnp.log(jax.random.uniform(next(keys), (SSD_HEADS,), jnp.float32, 1.0, 16.0))
        p['ssd_d'] = 1.0 + normal((SSD_HEADS,), 0.1)
        p['ssd_norm_g'] = gain(GROUP_W)
        p['mlstm_conv_w'] = normal((MLSTM_CONV, 2 * GROUP_W), MLSTM_CONV ** -0.5)
        p['mlstm_conv_b'] = normal((2 * GROUP_W,), 0.02)
        p['mlstm_ig_b'] = normal((MLSTM_HEADS,), 0.1)
        p['mlstm_fg_b'] = jnp.linspace(3.0, 6.0, MLSTM_HEADS, dtype=jnp.float32) + normal((MLSTM_HEADS,), 0.1)
        p['mlstm_norm_g'] = gain(GROUP_W)
        for name, val in p.items():
            inputs[name + '_' + str(l)] = val
    inputs['final_norm_g'] = gain(D_MODEL)
    return inputs


def reference(x,
              norm_g_0, w_in_0, w_out_0, gla_gk_w2_0, gla_gk_b_0, gla_norm_g_0,
              rwkv_mu_0, rwkv_w0_0, rwkv_w2_0, rwkv_a0_0, rwkv_a2_0,
              rwkv_k_k_0, rwkv_k_a_0, rwkv_r_k_0, rwkv_ln_g_0, rwkv_ln_b_0,
              ssd_conv_w_0, ssd_conv_b_0, ssd_dt_bias_0, ssd_a_log_0, ssd_d_0, ssd_norm_g_0,
              mlstm_conv_w_0, mlstm_conv_b_0, mlstm_ig_b_0, mlstm_fg_b_0, mlstm_norm_g_0,
              norm_g_1, w_in_1, w_out_1, gla_gk_w2_1, gla_gk_b_1, gla_norm_g_1,
              rwkv_mu_1, rwkv_w0_1, rwkv_w2_1, rwkv_a0_1, rwkv_a2_1, rwkv_v0_1, rwkv_v2_1,
              rwkv_k_k_1, rwkv_k_a_1, rwkv_r_k_1, rwkv_ln_g_1, rwkv_ln_b_1,
              ssd_conv_w_1, ssd_conv_b_1, ssd_dt_bias_1, ssd_a_log_1, ssd_d_1, ssd_norm_g_1,
              mlstm_conv_w_1, mlstm_conv_b_1, mlstm_ig_b_1, mlstm_fg_b_1, mlstm_norm_g_1,
              final_norm_g):
    layers = [
        dict(norm_g=norm_g_0, w_in=w_in_0, w_out=w_out_0, gla_gk_w2=gla_gk_w2_0, gla_gk_b=gla_gk_b_0,
             gla_norm_g=gla_norm_g_0, rwkv_mu=rwkv_mu_0, rwkv_w0=rwkv_w0_0, rwkv_w2=rwkv_w2_0,
             rwkv_a0=rwkv_a0_0, rwkv_a2=rwkv_a2_0, rwkv_v0=None, rwkv_v2=None,
             rwkv_k_k=rwkv_k_k_0, rwkv_k_a=rwkv_k_a_0, rwkv_r_k=rwkv_r_k_0, rwkv_ln_g=rwkv_ln_g_0,
             rwkv_ln_b=rwkv_ln_b_0, ssd_conv_w=ssd_conv_w_0, ssd_conv_b=ssd_conv_b_0,
             ssd_dt_bias=ssd_dt_bias_0, ssd_a_log=ssd_a_log_0, ssd_d=ssd_d_0, ssd_norm_g=ssd_norm_g_0,
             mlstm_conv_w=mlstm_conv_w_0, mlstm_conv_b=mlstm_conv_b_0, mlstm_ig_b=mlstm_ig_b_0,
             mlstm_fg_b=mlstm_fg_b_0, mlstm_norm_g=mlstm_norm_g_0),
        dict(norm_g=norm_g_1, w_in=w_in_1, w_out=w_out_1, gla_gk_w2=gla_gk_w2_1, gla_gk_b=gla_gk_b_1,
             gla_norm_g=gla_norm_g_1, rwkv_mu=rwkv_mu_1, rwkv_w0=rwkv_w0_1, rwkv_w2=rwkv_w2_1,
             rwkv_a0=rwkv_a0_1, rwkv_a2=rwkv_a2_1, rwkv_v0=rwkv_v0_1, rwkv_v2=rwkv_v2_1,
             rwkv_k_k=rwkv_k_k_1, rwkv_k_a=rwkv_k_a_1, rwkv_r_k=rwkv_r_k_1, rwkv_ln_g=rwkv_ln_g_1,
             rwkv_ln_b=rwkv_ln_b_1, ssd_conv_w=ssd_conv_w_1, ssd_conv_b=ssd_conv_b_1,
             ssd_dt_bias=ssd_dt_bias_1, ssd_a_log=ssd_a_log_1, ssd_d=ssd_d_1, ssd_norm_g=ssd_norm_g_1,
             mlstm_conv_w=mlstm_conv_w_1, mlstm_conv_b=mlstm_conv_b_1, mlstm_ig_b=mlstm_ig_b_1,
             mlstm_fg_b=mlstm_fg_b_1, mlstm_norm_g=mlstm_norm_g_1),
    ]
    v_first = None
    for layer in range(DEPTH):
        x, v_first = hybrid_layer(x, v_first, layer, layers[layer])
    return rms_norm(x, final_norm_g)
```

```python
import numpy as np
import concourse.bass as bass
import concourse.mybir as mybir
from concourse.bass_utils import run_bass_kernel_spmd

F32 = mybir.dt.float32
F32R = mybir.dt.float32r
BF16 = mybir.dt.bfloat16
AF = mybir.ActivationFunctionType
ALU = mybir.AluOpType

D = 2048
SEQ = 2048
T = 512
NSUB = T // 128
PADC = 4
CW = T + PADC
GW = 512
NIN = (7840, 7872)
EPS = 1e-6


def in_layout(l):
    rs = 3 * GW + 64 + 64 + (32 if l > 0 else 0)
    return (('gla_q', 256), ('gla_k', 256), ('gla_v', 512), ('gla_gk', 16), ('gla_z', 512),
            ('rwkv_shift', rs), ('rwkv_z', 512), ('ssd_xbc', 1024), ('ssd_dt', 8), ('ssd_z', 512),
            ('mlstm_qk', 1024), ('mlstm_v', 512), ('mlstm_i', 4), ('mlstm_f', 4), ('mlstm_o', 512),
            ('mlstm_z', 512))


def col_offsets(l):
    off = {}
    c = 0
    for n, w in in_layout(l):
        off[n] = c
        c += w
    return off


_o = [0]


def _al(n):
    r = _o[0]
    _o[0] += n
    return r


P_NG = _al(16); P_GLAG = _al(4); P_RMU = _al(15); P_RA0 = _al(4); P_RKK = _al(4); P_RKA = _al(4); P_RRK = _al(4)
P_RLNG = _al(4); P_RLNB = _al(4); P_RV0 = _al(4); P_SCW = _al(32); P_SCB = _al(8); P_SDTB = _al(1); P_SALOG = _al(1)
P_SDCH = _al(4); P_SNG = _al(4); P_MCW = _al(32); P_MCB = _al(8); P_MIFB = _al(1); P_MNG = _al(4); P_FNG = _al(16)
NPRM = _o[0]
C_IDN, C_ONES, C_TRI, C_SGT, C_MU64, C_MSU64, C_MSL64, C_BONES, NCST = 0, 128, 256, 384, 512, 640, 768, 896, 1024


def cols128(v):
    v = np.asarray(v, np.float32).reshape(-1)
    n = (len(v) + 127) // 128
    o = np.zeros((n * 128,), np.float32)
    o[:len(v)] = v
    return o.reshape(n, 128).T


def pack_params(inp, l):
    g = lambda k: np.asarray(inp["%s_%d" % (k, l)], np.float32)
    prm = np.zeros((128, NPRM), np.float32)
    prm[:, P_NG:P_NG + 16] = cols128(g("norm_g"))
    prm[:, P_GLAG:P_GLAG + 4] = cols128(g("gla_norm_g"))
    mu = g("rwkv_mu")
    prm[:, P_RMU:P_RMU + 12] = cols128(mu[:1536])
    prm[:64, P_RMU + 12] = mu[1536:1600]
    prm[:64, P_RMU + 13] = mu[1600:1664]
    if l > 0:
        prm[:32, P_RMU + 14] = mu[1664:1696]
        prm[:, P_RV0:P_RV0 + 4] = cols128(g("rwkv_v0"))
    prm[:, P_RA0:P_RA0 + 4] = cols128(g("rwkv_a0"))
    prm[:, P_RKK:P_RKK + 4] = cols128(g("rwkv_k_k"))
    prm[:, P_RKA:P_RKA + 4] = cols128(g("rwkv_k_a"))
    prm[:, P_RRK:P_RRK + 4] = cols128(g("rwkv_r_k"))
    prm[:, P_RLNG:P_RLNG + 4] = cols128(g("rwkv_ln_g"))
    prm[:, P_RLNB:P_RLNB + 4] = cols128(g("rwkv_ln_b"))
    cw = g("ssd_conv_w")
    for j in range(4):
        prm[:, P_SCW + j * 8:P_SCW + j * 8 + 8] = cols128(cw[j])
    prm[:, P_SCB:P_SCB + 8] = cols128(g("ssd_conv_b"))
    prm[:8, P_SDTB] = g("ssd_dt_bias")
    prm[:8, P_SALOG] = g("ssd_a_log")
    prm[:, P_SDCH:P_SDCH + 4] = cols128(np.repeat(g("ssd_d"), 64))
    prm[:, P_SNG:P_SNG + 4] = cols128(g("ssd_norm_g"))
    cw = g("mlstm_conv_w")
    for j in range(4):
        prm[:, P_MCW + j * 8:P_MCW + j * 8 + 8] = cols128(cw[j])
    prm[:, P_MCB:P_MCB + 8] = cols128(g("mlstm_conv_b"))
    prm[:4, P_MIFB] = g("mlstm_ig_b")
    prm[4:8, P_MIFB] = g("mlstm_fg_b")
    prm[:, P_MNG:P_MNG + 4] = cols128(g("mlstm_norm_g"))
    prm[:, P_FNG:P_FNG + 16] = cols128(np.asarray(inp["final_norm_g"], np.float32))
    return prm


def make_consts():
    c = np.zeros((128, NCST), np.float32)
    i = np.arange(128)
    r, cc = i[:, None], i[None, :]
    same = (r // 64) == (cc // 64)
    c[:, C_IDN:C_IDN + 128] = (r == cc)
    c[:, C_ONES:C_ONES + 128] = 1.0
    c[:, C_TRI:C_TRI + 128] = (r <= cc)
    c[:, C_SGT:C_SGT + 128] = (r > cc)
    c[:, C_MU64:C_MU64 + 128] = same & (r <= cc)
    c[:, C_MSU64:C_MSU64 + 128] = same & (r < cc)
    c[:, C_MSL64:C_MSL64 + 128] = same & (r > cc)
    c[:, C_BONES:C_BONES + 128] = same
    return c


class V:
    def __init__(self, ap, keys, excl=False):
        self.ap = ap
        self.keys = tuple(keys)
        self.excl = excl

    def __call__(self, f):
        v = V(f(self.ap), self.keys, self.excl)
        if hasattr(self, "base"):
            v.base = self.base
        return v


class Prog:
    ENGS = ("sync", "scalar", "vector", "gpsimd", "tensor")

    def __init__(self, nc):
        self.nc = nc
        self.ops = {e: [] for e in self.ENGS}
        self.count = {e: 0 for e in self.ENGS}
        self.waited = {e: {} for e in self.ENGS}
        self.lastw = {}
        self.readers = {}
        self.dma_cnt = {}
        self.sems = {}
        self.ctx = []
        self.nops = 0
        self.limit = None
        self.log = []

    def sem(self, key):
        if key not in self.sems:
            nm = "s_" + "".join(ch for ch in str(key) if ch.isalnum())
            cm = self.nc.semaphore(nm)
            self.sems[key] = cm.__enter__()
            self.ctx.append(cm)
        return self.sems[key]

    def _deps(self, reads, writes):
        deps = {}

        def add(d):
            if d is not None and deps.get(d[0], 0) < d[1]:
                deps[d[0]] = d[1]
        for k in reads:
            add(self.lastw.get(k))
        for k in writes:
            add(self.lastw.get(k))
            for r in self.readers.get(k, ()):
                add(r)
        return deps

    def _emit_waits(self, eng, deps):
        w = self.waited[eng]
        for k, v in deps.items():
            if eng == "tensor" and k == "tensor":
                continue
            if w.get(k, 0) >= v:
                continue
            w[k] = v
            s = self.sem(k)
            self.ops[eng].append(lambda e, s=s, v=v: e.wait_ge(s, v))

    def _commit(self, reads, writes, tok):
        for k in reads:
            self.readers.setdefault(k, []).append(tok)
        for k in writes:
            self.lastw[k] = tok
            self.readers[k] = []

    def op(self, eng, fn, reads=(), writes=()):
        if self.limit is not None and self.nops >= self.limit:
            return
        self._emit_waits(eng, self._deps(reads, writes))
        self.count[eng] += 1
        tok = (eng, self.count[eng])
        s = self.sem(eng)
        self.ops[eng].append(lambda e, fn=fn, s=s: fn(e).then_inc(s, 1))
        self._commit(reads, writes, tok)
        self.nops += 1
        if self.limit is not None:
            import sys as _s
            f = _s._getframe(1)
            ln = []
            while f is not None and len(ln) < 3:
                ln.append(f.f_lineno)
                f = f.f_back
            self.log.append((self.nops, eng, ln))

    def dma(self, eng, slot, out, in_, reads=(), writes=(), **kw):
        if self.limit is not None and self.nops >= self.limit:
            return
        self._emit_waits(eng, self._deps(reads, writes))
        key = ("dma", slot)
        self.dma_cnt[key] = self.dma_cnt.get(key, 0) + 16
        tok = (key, self.dma_cnt[key])
        s = self.sem(key)
        self.ops[eng].append(lambda e, s=s: e.dma_start(out=out, in_=in_, **kw).then_inc(s, 16))
        self._commit(reads, writes, tok)

    def wait_all(self, eng, keys):
        deps = {}
        for k in keys:
            d = self.lastw.get(k)
            if d is not None and deps.get(d[0], 0) < d[1]:
                deps[d[0]] = d[1]
        self._emit_waits(eng, deps)

    def emit(self):
        with self.nc.Block() as block:
            def mk(name):
                def body(e):
                    for f in self.ops[name]:
                        f(e)
                return body
            block.sync(mk("sync"))
            block.scalar(mk("scalar"))
            block.vector(mk("vector"))
            block.gpsimd(mk("gpsimd"))
            block.tensor(mk("tensor"))

    def close(self):
        for cm in reversed(self.ctx):
            cm.__exit__(None, None, None)

    @staticmethod
    def _rw(out, ins):
        r, w = (), out.keys
        for v in ins:
            if isinstance(v, V):
                if v.excl:
                    w = w + v.keys
                else:
                    r = r + v.keys
        return r, w

    @staticmethod
    def _a(x):
        return x.ap if isinstance(x, V) else x

    def mm(self, out, lhsT, rhs, start=True, stop=True):
        r, w = self._rw(out, (lhsT, rhs))
        self.op("tensor", lambda e: e.matmul(out.ap, lhsT.ap, rhs.ap, start=start, stop=stop), reads=r, writes=w)

    def tr(self, out, in_, idn):
        r, w = self._rw(out, (in_, idn))
        self.op("tensor", lambda e: e.transpose(out.ap, in_.ap, idn.ap), reads=r, writes=w)

    def act(self, out, in_, func, bias=None, scale=None):
        kw = {}
        if bias is not None:
            kw["bias"] = self._a(bias)
        if scale is not None:
            kw["scale"] = scale
        r, w = self._rw(out, (in_, bias))
        self.op("scalar", lambda e: e.activation(out.ap, in_.ap, func, **kw), reads=r, writes=w)

    def tt(self, out, a, b, op, eng="vector"):
        r, w = self._rw(out, (a, b))
        self.op(eng, lambda e: e.tensor_tensor(out.ap, a.ap, b.ap, op), reads=r, writes=w)

    def ts(self, out, a, s1, s2=None, op0=ALU.mult, op1=None, eng="vector"):
        r, w = self._rw(out, (a, s1, s2))
        s1, s2 = self._a(s1), self._a(s2)
        if op1 is None:
            self.op(eng, lambda e: e.tensor_scalar(out.ap, a.ap, s1, None, op0), reads=r, writes=w)
        else:
            self.op(eng, lambda e: e.tensor_scalar(out.ap, a.ap, s1, s2, op0, op1), reads=r, writes=w)

    def stt(self, out, a, s, b, op0, op1, eng="vector"):
        r, w = self._rw(out, (a, s, b))
        s = self._a(s)
        self.op(eng, lambda e: e.scalar_tensor_tensor(out.ap, a.ap, s, b.ap, op0, op1), reads=r, writes=w)

    def cp(self, out, a, eng="vector"):
        r, w = self._rw(out, (a,))
        if eng == "scalar":
            self.op("scalar", lambda e: e.copy(out.ap, a.ap), reads=r, writes=w)
        else:
            self.op(eng, lambda e: e.tensor_copy(out.ap, a.ap), reads=r, writes=w)

    def recip(self, out, a):
        r, w = self._rw(out, (a,))
        self.op("vector", lambda e: e.reciprocal(out.ap, a.ap), reads=r, writes=w)

    def memset(self, out, val, eng="vector"):
        self.op(eng, lambda e: e.memset(out.ap, val), writes=out.keys)


class Pool:
    def __init__(self, name, views):
        self.name = name
        self.free = list(views)

    def get(self):
        if not self.free:
            raise RuntimeError("pool %s exhausted" % self.name)
        return self.free.pop(0)

    def put(self, v):
        self.free.append(v)


class DerivedPool:
    def __init__(self, base, fn):
        self.base = base
        self.fn = fn
        self.name = base.name

    def get(self):
        b = self.base.get()
        v = V(self.fn(b.ap), b.keys, b.excl)
        v.base = b
        return v

    def put(self, v):
        self.base.put(v.base)


class Scope:
    def __init__(self):
        self.items = []

    def take(self, pool):
        v = pool.get()
        self.items.append((pool, v))
        return v

    def __enter__(self):
        return self

    def __exit__(self, *a):
        for pool, v in self.items:
            pool.put(v)
        self.items = []
        return False


MIXN = ("gla", "rwkv", "ssd", "mlstm")
NCH = 21


def build_program(n_tiles=4, depth=2, mixers=MIXN, debug=False, upto='all', limit=None):
    nc = bass.Bass("TRN2", target_bir_lowering=False)
    nc.dge_precook = False
    x_d = nc.dram_tensor("x", [SEQ, D], F32, kind="ExternalInput").ap()
    cst_d = nc.dram_tensor("cst", [128, NCST], F32, kind="ExternalInput").ap()
    win_d, wout_d, prm_d, gkw_d, rw2_d, ra2_d = [], [], [], [], [], []
    for l in range(2):
        win_d.append(nc.dram_tensor("win%d" % l, [D, NIN[l]], F32R, kind="ExternalInput").ap())
        wout_d.append(nc.dram_tensor("wout%d" % l, [D, D], F32R, kind="ExternalInput").ap())
        prm_d.append(nc.dram_tensor("prm%d" % l, [128, NPRM], F32, kind="ExternalInput").ap())
        gkw_d.append(nc.dram_tensor("gkw%d" % l, [17, 256], F32, kind="ExternalInput").ap())
        rw2_d.append(nc.dram_tensor("rw2%d" % l, [65, 512], F32, kind="ExternalInput").ap())
        ra2_d.append(nc.dram_tensor("ra2%d" % l, [65, 1024], F32, kind="ExternalInput").ap())
    out_d = nc.dram_tensor("out", [SEQ, D], F32, kind="ExternalOutput").ap()
    dbg_d = nc.dram_tensor("dbg", [2, D, SEQ], F32, kind="ExternalOutput").ap() if debug else None

    P = Prog(nc)
    P.limit = limit
    cms = []

    def sb(name, shape, dt=F32):
        cm = nc.sbuf_tensor("sb_" + name, shape, dt)
        cms.append(cm)
        return cm.__enter__()

    def ps(name, shape, dt=F32):
        cm = nc.psum_tensor("ps_" + name, shape, dt)
        cms.append(cm)
        return cm.__enter__()

    cst = sb("cst", [128, NCST])
    prm = [sb("prm%d" % l, [128, NPRM]) for l in range(2)]
    gkw1 = sb("gkw", [17, 256])
    rw21 = sb("rw2", [65, 512])
    ra21 = sb("ra2", [65, 1024])
    gkw, rw2, ra2 = [gkw1, gkw1], [rw21, rw21], [ra21, ra21]
    xT = sb("xT", [128, 16, T])
    hT = sb("hT", [128, 16, T], F32R)
    NW = 2
    wbuf = [sb("wbuf%d" % i, [128, 2048], F32R) for i in range(NW)]
    cbuf = sb("cbuf", [128, NCH, CW])
    yT = sb("yT", [128, 4, T], F32R)
    vfirst = sb("vfirst", [128, 4, T])
    bd = [sb("bd%d" % q, [128, 512], BF16) for q in range(5)]
    idnb = sb("idnb", [128, 128], BF16)
    negA = [sb("negA%d" % l, [8, 1]) for l in range(2)]
    Sgla = [sb("Sgla%d" % l, [128, 256]) for l in range(2)]
    Sglab = [sb("Sglab%d" % l, [128, 256], BF16) for l in range(2)]
    Sssd = [sb("Sssd%d" % l, [128, 512]) for l in range(2)]
    Sssdb = [sb("Sssdb%d" % l, [128, 512], BF16) for l in range(2)]
    Cml = [sb("Cml%d" % l, [128, 512]) for l in range(2)]
    Cmlb = [sb("Cmlb%d" % l, [128, 512], BF16) for l in range(2)]
    nml = [sb("nml%d" % l, [128, 4]) for l in range(2)]
    nmlb = [sb("nmlb%d" % l, [128, 4], BF16) for l in range(2)]
    onesb = sb("onesb", [128, 128], BF16)
    Trw = [sb("Trw%d" % l, [128, 512]) for l in range(2)]
    Trwb = [sb("Trwb%d" % l, [128, 512], BF16) for l in range(2)]
    car_s = [sb("car_s%d" % l, [128, 8, 4]) for l in range(2)]
    car_m = [sb("car_m%d" % l, [128, 8, 4]) for l in range(2)]
    car_r = [sb("car_r%d" % l, [128, 15, 4]) for l in range(2)]
    NBIG, NSM, NU = 6, 24, 16
    arena = sb("arena", [128, NU * 512], BF16)
    BIG = Pool("big", [V(sb("big%d" % i, [128, T])[:], [("big", i)]) for i in range(NBIG)])
    SM = Pool("sm", [V(arena[:, (i // 2) * 512 + (i % 2) * 256:(i // 2) * 512 + (i % 2) * 256 + 256].bitcast(F32),
                       [("ar", i // 2)]) for i in range(NSM)])
    SMB = Pool("smb", [V(arena[:, u * 512:(u + 1) * 512], [("ar", u)]) for u in range(NU)])
    TK = Pool("tok8", [V(sb("tok8_%d" % i, [128, 16])[:], [("tok8", i)]) for i in range(6)])
    pb_t = [ps("pb%d" % i, [128, 512]) for i in range(8)]
    PB = Pool("pb", [V(pb_t[i][:], [("pb", i)], excl=True) for i in range(8)])
    PH = DerivedPool(PB, lambda a: a[:, 0:256])
    PQ = DerivedPool(PB, lambda a: a[:, 0:128])

    st = {"ev": 0}

    def evac_eng():
        st["ev"] += 1
        return "scalar" if st["ev"] % 2 else "vector"

    def C(off, n=128, p0=0, p1=128):
        return V(cst[p0:p1, off:off + n], ["cst"])

    IDN, ONES, TRI, SGT = C(C_IDN), C(C_ONES), C(C_TRI), C(C_SGT)
    MU64, MSU64, MSL64, BONES = C(C_MU64), C(C_MSU64), C(C_MSL64), C(C_BONES)

    def SEL(h, k):
        return V(cst[0:k, C_IDN + h:C_IDN + h + 1].to_broadcast([k, 128]), ["cst"])

    def prmc(l, col, p1=128, p0=0):
        return V(prm[l][p0:p1, col:col + 1], [("prm", l)])

    def cb(j, p1=128, p0=0):
        return V(cbuf[p0:p1, j, PADC:CW], [("c", j)])

    def cbs(j, s, p0=0, p1=128):
        return V(cbuf[p0:p1, j, PADC + s * 128:PADC + (s + 1) * 128], [("c", j)])

    P.dma("gpsimd", "cst", cst[:], cst_d, writes=["cst"])
    for l in range(depth):
        P.dma("gpsimd", "prm%d" % l, prm[l][:], prm_d[l], writes=[("prm", l)])
    P.memset(V(cbuf[:], [("c", j) for j in range(NCH)]), 0.0)
    for q in range(5):
        P.memset(V(bd[q][:], [("bd", q)]), 0.0)
    P.cp(V(idnb[:], ["idnb"]), IDN)
    P.memset(V(onesb[:], ["onesb"]), 1.0)
    for l in range(depth):
        P.memset(V(Sgla[l][:], [("Sgla", l)]), 0.0)
        P.memset(V(Sglab[l][:], [("Sglab", l)]), 0.0)
        P.memset(V(Sssd[l][:], [("Sssd", l)]), 0.0)
        P.memset(V(Sssdb[l][:], [("Sssdb", l)]), 0.0)
        P.memset(V(Cml[l][:], [("Cml", l)]), 0.0)
        P.memset(V(Cmlb[l][:], [("Cmlb", l)]), 0.0)
        P.memset(V(nml[l][:], [("nml", l)]), 0.0)
        P.memset(V(nmlb[l][:], [("nmlb", l)]), 0.0)
        P.memset(V(Trw[l][:], [("Trw", l)]), 0.0)
        P.memset(V(Trwb[l][:], [("Trwb", l)]), 0.0)
        P.memset(V(car_s[l][:], [("car_s", l)]), 0.0)
        P.memset(V(car_m[l][:], [("car_m", l)]), 0.0)
        P.memset(V(car_r[l][:], [("car_r", l)]), 0.0)
        P.act(V(negA[l][:], [("negA", l)]), prmc(l, P_SALOG, 8), AF.Exp)
        P.ts(V(negA[l][:], [("negA", l)]), V(negA[l][:], [("negA", l)]), -1.0)

    def plan(l, m):
        off = col_offsets(l)
        it = []
        if m == "gla":
            for j in range(2):
                it.append((off['gla_q'] + 128 * j, 128, j, 0))
            for j in range(2):
                it.append((off['gla_k'] + 128 * j, 128, 2 + j, 0))
            for j in range(4):
                it.append((off['gla_z'] + 128 * j, 128, 4 + j, 0))
            for j in range(4):
                it.append((off['gla_v'] + 128 * j, 128, 8 + j, 0))
            it.append((off['gla_gk'], 16, 12, 0))
        elif m == "rwkv":
            b = off['rwkv_shift']
            for j in range(12):
                it.append((b + 128 * j, 128, j, 0))
            it.append((b + 1536, 64, 12, 0))
            it.append((b + 1600, 64, 13, 0))
            if l > 0:
                it.append((b + 1664, 32, 14, 0))
            for j in range(4):
                it.append((off['rwkv_z'] + 128 * j, 128, 15 + j, 0))
        elif m == "ssd":
            for j in range(8):
                it.append((off['ssd_xbc'] + 128 * j, 128, j, 0))
            for j in range(4):
                it.append((off['ssd_z'] + 128 * j, 128, 8 + j, 0))
            it.append((off['ssd_dt'], 8, 12, 0))
        elif m == "mlstm":
            for j in range(8):
                it.append((off['mlstm_qk'] + 128 * j, 128, j, 0))
            for j in range(4):
                it.append((off['mlstm_o'] + 128 * j, 128, 8 + j, 0))
            for j in range(4):
                it.append((off['mlstm_z'] + 128 * j, 128, 12 + j, 0))
            for j in range(4):
                it.append((off['mlstm_v'] + 128 * j, 128, 16 + j, 0))
            it.append((off['mlstm_i'], 8, 20, 0))
        return it

    plans = {(l, m): plan(l, m) for l in range(depth) for m in mixers}

    def groups_of(l, m):
        gs = []
        for it in plans[(l, m)]:
            c0, n = it[0], it[1]
            if gs and gs[-1][0] + gs[-1][1] == c0 and gs[-1][1] + n <= 512 and len(gs[-1][2]) < 4:
                gs[-1][2].append((gs[-1][1],) + tuple(it[1:]))
                gs[-1][1] += n
            else:
                gs.append([c0, n, [(0,) + tuple(it[1:])]])
        return gs

    gplans = {(l, m): groups_of(l, m) for l in range(depth) for m in mixers}
    wq = []
    for t in range(n_tiles):
        for l in range(depth):
            for m in mixers:
                for (c0, n, chunks) in gplans[(l, m)]:
                    for kg in range(4):
                        wq.append((win_d[l][kg * 512:(kg + 1) * 512, c0:c0 + n].rearrange("(k p) c -> p k c", p=128), (4, n)))
                mi = MIXN.index(m)
                for q in range(4):
                    wq.append((wout_d[l][mi * 512:(mi + 1) * 512, q * 512:(q + 1) * 512]
                               .rearrange("(k p) c -> p k c", p=128), (4, 512)))
    wstate = {"issued": 0, "used": 0}

    def w_issue():
        i = wstate["issued"]
        src, (k, n) = wq[i]
        slot = i % NW
        dst = wbuf[slot][:, 0:k * n].rearrange("p (k n) -> p k n", k=k)
        P.dma("sync", "w%d" % slot, dst, src, writes=[("wbuf", slot)])
        wstate["issued"] += 1

    def w_next():
        i = wstate["used"]
        while wstate["issued"] < min(i + NW, len(wq)):
            w_issue()
        wstate["used"] += 1
        return wbuf[i % NW], ("wbuf", i % NW)

    def inproj(l, m):
        for (c0, n, chunks) in gplans[(l, m)]:
            with Scope() as sc:
                pps = [sc.take(PB)(lambda a, pb=pb, cn=cn: a[pb:pb + cn, :]) for (o, cn, j, pb) in chunks]
                for kg in range(4):
                    wt, wkey = w_next()
                    w3 = wt[:, 0:4 * n].rearrange("p (k n) -> p k n", k=4)
                    for ci, (o, cn, j, pb) in enumerate(chunks):
                        for k in range(4):
                            P.mm(pps[ci], V(w3[:, k, o:o + cn], [wkey]), V(hT[:, kg * 4 + k, :], [("hT", kg * 4 + k)]),
                                 start=(kg == 0 and k == 0), stop=(kg == 3 and k == 3))
                for ci, (o, cn, j, pb) in enumerate(chunks):
                    P.cp(cb(j, p0=pb, p1=pb + cn), pps[ci], eng=evac_eng())

    def outproj(l, m):
        for q in range(4):
            wt, wkey = w_next()
            w3 = wt[:, 0:4 * 512].rearrange("p (k n) -> p k n", k=4)
            for oc in range(4):
                with Scope() as sc:
                    pp = sc.take(PB)
                    for k in range(4):
                        P.mm(pp, V(w3[:, k, oc * 128:(oc + 1) * 128], [wkey]), V(yT[:, k, :], [("y", k)]),
                             start=(k == 0), stop=(k == 3))
                    j = q * 4 + oc
                    xv = V(xT[:, j, :], [("xT", j)])
                    P.tt(xv, xv, pp, ALU.add)

    def rmsnorm(gcol, l, dst_fn):
        with Scope() as sc:
            pp = sc.take(PB)
            for j in range(16):
                with Scope() as s2:
                    sq = s2.take(BIG)
                    P.act(sq, V(xT[:, j, :], [("xT", j)]), AF.Square)
                    P.mm(pp, ONES, sq, start=(j == 0), stop=(j == 15))
            rs = sc.take(BIG)
            P.act(rs, pp, AF.Ln, bias=EPS, scale=1.0 / D)
            P.act(rs, rs, AF.Exp, scale=-0.5)
            for j in range(16):
                P.stt(dst_fn(j), V(xT[:, j, :], [("xT", j)]), prmc(l, gcol + j), rs, ALU.mult, ALU.mult)

    def conv_silu(l, car, cname, wcol, bcol, nch):
        ck = (cname, l)
        for j in range(nch):
            ce = "vector"
            with Scope() as sc:
                P.cp(V(cbuf[:, j, 1:4], [("c", j)]), V(car[l][:, j, 0:3], [ck]))
                acc = sc.take(BIG)
                P.ts(acc, V(cbuf[:, j, 1:1 + T], [("c", j)]), prmc(l, wcol + j), eng=ce)
                for tap in range(1, 4):
                    P.stt(acc, V(cbuf[:, j, 1 + tap:1 + tap + T], [("c", j)]), prmc(l, wcol + tap * 8 + j), acc,
                          ALU.mult, ALU.add, eng=ce)
                P.cp(V(car[l][:, j, 0:3], [ck]), V(cbuf[:, j, T + 1:T + 4], [("c", j)]), eng="scalar")
                P.act(cb(j), acc, AF.Silu, bias=prmc(l, bcol + j))

    def gla_pre(l):
        P.memset(V(cbuf[0:32, 12, :], [("c", 12)]), 1.0)

    def gla(l):
        for j in range(4):
            P.act(cb(4 + j), cb(4 + j), AF.Silu)
        g4 = lambda v: v(lambda a: a.rearrange("p (g t) -> p g t", g=4))
        g2 = lambda v: v(lambda a: a.rearrange("p (g t) -> p g t", g=2))
        Sv = g2(V(Sgla[l][:], [("Sgla", l)]))
        Sb = g2(V(Sglab[l][:], [("Sglab", l)]))
        for s in range(NSUB):
            sl = slice(s * 128, (s + 1) * 128)
            csl = slice(PADC + s * 128, PADC + (s + 1) * 128)
            ck = lambda j0, n: [("c", j0 + i) for i in range(n)]
            qv = V(cbuf[:, 0:2, csl], ck(0, 2))
            kv = V(cbuf[:, 2:4, csl], ck(2, 2))
            zv = V(cbuf[:, 4:8, csl], ck(4, 4))
            vv = V(cbuf[:, 8:12, csl], ck(8, 4))
            with Scope() as sc:
                vt = g4(sc.take(SMB))
                with Scope() as s2:
                    pvt = g4(s2.take(PB))
                    for h in range(4):
                        P.tr(pvt(lambda a: a[:, h, :]), vv(lambda a: a[:, h, :]), IDN)
                    P.cp(vt, pvt, eng="scalar")
                nb = sc.take(BIG)
                nlsv = nb(lambda a: a[:, 0:256])
                ercv = nb(lambda a: a[:, 256:512])
                with Scope() as s2:
                    pg = s2.take(PH)
                    P.mm(pg, cbs(12, s, 0, 17), V(gkw[l][:], ["gkw"]))
                    P.act(nlsv, pg, AF.Exp, scale=-1.0)
                    P.act(nlsv, nlsv, AF.Ln, bias=1.0)
                eb = sc.take(BIG)
                eq = g2(eb(lambda a: a[:, 0:256]))
                gb = sc.take(BIG)
                qg = g2(gb(lambda a: a[:, 0:256]))
                kg = g2(gb(lambda a: a[:, 256:512]))
                with Scope() as s2:
                    pc = g2(s2.take(PH))
                    for j in range(2):
                        P.mm(pc(lambda a: a[:, j, :]), nlsv(lambda a: a[:, j * 128:(j + 1) * 128]), TRI)
                    ek = g2(eb(lambda a: a[:, 256:512]))
                    P.act(eq, pc, AF.Exp, scale=-1.0 / 16)
                    P.act(ek, pc, AF.Exp, scale=1.0 / 16)
                    P.stt(qg, qv, 0.125, eq, ALU.mult, ALU.mult)
                    P.tt(kg, kv, ek, ALU.mult)
                kb = sc.take(SMB)
                kdv = kb(lambda a: a[:, 0:256])
                with Scope() as s2:
                    pr = s2.take(PH)
                    P.mm(pr, SGT, nlsv)
                    P.act(ercv, pr, AF.Exp, scale=-1.0 / 16)
                    pt = s2.take(PH)
                    for j in range(2):
                        P.tr(pt(lambda a: a[:, j * 128:(j + 1) * 128]), kv(lambda a: a[:, j, :]), IDN)
                    P.tt(kdv, pt, ercv, ALU.mult)
                attm = g4(sc.take(SMB))
                with Scope() as s2:
                    pa2 = [g2(s2.take(PH)), g2(s2.take(PH))]
                    for h in range(4):
                        j, pb = h // 2, (h % 2) * 64
                        P.mm(pa2[h % 2](lambda a: a[:, j, :]), kg(lambda a: a[pb:pb + 64, j, :]), qg(lambda a: a[pb:pb + 64, j, :]))
                    for hh in range(2):
                        P.tt(attm(lambda a: a[:, hh::2, :]), pa2[hh], TRI(lambda a: a.unsqueeze(1).to_broadcast([128, 2, 128])), ALU.mult)
                with Scope() as s2:
                    po4 = g4(s2.take(PB))
                    for h in range(4):
                        j, pb = h // 2, (h % 2) * 64
                        P.mm(po4(lambda a: a[:, h, :]), vt(lambda a: a[:, h, :]), attm(lambda a: a[:, h, :]), start=True, stop=False)
                        P.mm(po4(lambda a: a[:, h, :]), Sv(lambda a: a[pb:pb + 64, j, :]), qg(lambda a: a[pb:pb + 64, j, :]),
                             start=False, stop=True)
                    osq = g4(s2.take(BIG))
                    P.act(osq, po4, AF.Square)
                    pss = g4(s2.take(PB))
                    for h in range(4):
                        P.mm(pss(lambda a: a[:, h, :]), ONES, osq(lambda a: a[:, h, :]))
                    P.act(osq, pss, AF.Ln, bias=EPS, scale=1.0 / 128)
                    P.act(osq, osq, AF.Exp, scale=-0.5)
                    P.tt(osq, po4, osq, ALU.mult)
                    P.tt(osq, osq, V(prm[l][:, P_GLAG:P_GLAG + 4].unsqueeze(2).to_broadcast([128, 4, 128]), [("prm", l)]), ALU.mult)
                    P.tt(V(yT[:, :, sl], [("y", k) for k in range(4)]), osq, zv, ALU.mult)
                with Scope() as s2:
                    pi = s2.take(PB)
                    for j in range(2):
                        P.mm(pi(lambda a: a[:, j * 256:(j + 1) * 256]), kdv(lambda a: a[:, j * 128:(j + 1) * 128]),
                             vt(lambda a: a[:, 2 * j:2 * j + 2, :].rearrange("p g t -> p (g t)")))
                    pi3 = pi(lambda a: a.rearrange("p (j w) -> p j w", j=2))
                    for hh in range(2):
                        pb = hh * 64
                        sv = Sv(lambda a: a[pb:pb + 64, :, :])
                        P.tt(sv, sv, eq(lambda a: a[pb:pb + 64, :, 127:128].to_broadcast([64, 2, 128])), ALU.mult)
                        P.tt(sv, sv, pi3(lambda a: a[pb:pb + 64, :, hh * 128:(hh + 1) * 128]), ALU.add)

    def ssd(l):
        conv_silu(l, car_s, "car_s", P_SCW, P_SCB, 8)
        for j in range(4):
            P.act(cb(8 + j), cb(8 + j), AF.Silu)
        g4 = lambda v: v(lambda a: a.rearrange("p (g t) -> p g t", g=4))
        Sv = V(Sssd[l][:], [("Sssd", l)])
        Sb = V(Sssdb[l][:], [("Sssdb", l)])

        def bc4(v):
            return v(lambda a: a.unsqueeze(1).to_broadcast([128, 4, 128]))

        with Scope() as st_:
            dt_ = st_.take(BIG)(lambda a: a[0:8, :])
            dta_ = st_.take(BIG)(lambda a: a[0:8, :])
            P.act(dt_, cb(12, p1=8), AF.Exp, bias=prmc(l, P_SDTB, 8))
            P.act(dt_, dt_, AF.Ln, bias=1.0)
            P.ts(dta_, dt_, V(negA[l][:], [("negA", l)]))
            for s in range(NSUB):
                sl = slice(s * 128, (s + 1) * 128)
                csl = slice(PADC + s * 128, PADC + (s + 1) * 128)
                ck = lambda j0, n: [("c", j0 + i) for i in range(n)]
                xv = V(cbuf[:, 0:4, csl], ck(0, 4))
                zv = V(cbuf[:, 8:12, csl], ck(8, 4))
                with Scope() as sc:
                    dtk = sc.take(TK)
                    with Scope() as s2:
                        p1 = s2.take(PQ)
                        P.tr(p1(lambda a: a[:, 0:8]), dt_(lambda a: a[:, sl]), C(C_IDN, 8, 0, 8))
                        P.tr(p1(lambda a: a[:, 8:16]), dta_(lambda a: a[:, sl]), C(C_IDN, 8, 0, 8))
                        P.cp(dtk, p1(lambda a: a[:, 0:16]))
                    dt_tok = dtk(lambda a: a[:, 0:8])
                    dta_tok = dtk(lambda a: a[:, 8:16])
                    cumT = sc.take(BIG)(lambda a: a[0:8, 0:128])
                    ex = sc.take(TK)
                    with Scope() as s2:
                        p2 = s2.take(PQ)
                        P.mm(p2(lambda a: a[0:8, :]), dta_tok, TRI)
                        P.cp(cumT, p2(lambda a: a[0:8, :]), eng="scalar")
                        p3 = s2.take(PQ)
                        P.mm(p3(lambda a: a[:, 0:8]), ONES, dta_tok)
                        P.mm(p3(lambda a: a[:, 8:16]), SGT, dta_tok)
                        P.act(ex, p3(lambda a: a[:, 0:16]), AF.Exp)
                    dec, erev = ex(lambda a: a[:, 0:8]), ex(lambda a: a[:, 8:16])
                    xdtv, xdtwv = sc.take(SMB), sc.take(SMB)
                    h8 = lambda v: v(lambda a: a.rearrange("p (h e) -> p h e", h=8))
                    with Scope() as s2:
                        pt = s2.take(PB)
                        for jc in range(4):
                            P.tr(pt(lambda a: a[:, jc * 128:(jc + 1) * 128]), cbs(jc, s), IDN)
                        P.tt(h8(xdtv), h8(pt), dt_tok(lambda a: a.unsqueeze(2).to_broadcast([128, 8, 64])), ALU.mult)
                    P.tt(h8(xdtwv), h8(xdtv), erev(lambda a: a.unsqueeze(2).to_broadcast([128, 8, 64])), ALU.mult)
                    bc_t = sc.take(SMB)
                    btok = bc_t(lambda a: a[:, 0:256].rearrange("p (g t) -> p g t", g=2))
                    cbm = bc_t(lambda a: a[:, 256:512].rearrange("p (g t) -> p g t", g=2))
                    with Scope() as s2:
                        pt = s2.take(PH)(lambda a: a.rearrange("p (g t) -> p g t", g=2))
                        pc = s2.take(PH)(lambda a: a.rearrange("p (g t) -> p g t", g=2))
                        for g in range(2):
                            P.tr(pt(lambda a: a[:, g, :]), cbs(4 + g, s), IDN)
                            P.mm(pc(lambda a: a[:, g, :]), cbs(4 + g, s), cbs(6 + g, s))
                        P.cp(btok, pt, eng="scalar")
                        P.tt(cbm, pc, TRI(lambda a: a.unsqueeze(1).to_broadcast([128, 2, 128])), ALU.mult)
                    pyb = sc.take(PB)
                    py4 = g4(pyb)
                    for g in range(2):
                        with Scope() as s2:
                            R = g4(s2.take(BIG))
                            P.tt(R, bc4(SGT), dta_tok(lambda a: a[:, 4 * g:4 * g + 4].unsqueeze(2).to_broadcast([128, 4, 128])), ALU.mult)
                            WT = g4(s2.take(SMB))
                            with Scope() as s3:
                                pg = g4(s3.take(PB))
                                for hl in range(4):
                                    P.mm(pg(lambda a: a[:, hl, :]), R(lambda a: a[:, hl, :]), TRI)
                                LT = g4(s3.take(BIG))
                                P.act(LT, pg, AF.Exp)
                                P.tt(WT, LT, cbm(lambda a: a[:, g:g + 1, :].to_broadcast([128, 4, 128])), ALU.mult)
                            Cw = g4(s2.take(SMB))
                            with Scope() as s3:
                                pbc = g4(s3.take(PB))
                                for hl in range(4):
                                    P.mm(pbc(lambda a: a[:, hl, :]), SEL(4 * g + hl, 8), cumT)
                                ec = g4(s3.take(BIG))
                                P.act(ec, pbc, AF.Exp)
                                P.tt(Cw, cbs(6 + g, s)(lambda a: a.unsqueeze(1).to_broadcast([128, 4, 128])), ec, ALU.mult)
                            for hl in range(4):
                                h = 4 * g + hl
                                jc, pb = h // 2, (h % 2) * 64
                                po = py4(lambda a: a[pb:pb + 64, jc, :])
                                P.mm(po, xdtv(lambda a: a[:, h * 64:(h + 1) * 64]), WT(lambda a: a[:, hl, :]), start=True, stop=False)
                                P.mm(po, Sb(lambda a: a[:, h * 64:(h + 1) * 64]), Cw(lambda a: a[:, hl, :]), start=False, stop=True)
                    with Scope() as s2:
                        y2 = g4(s2.take(BIG))
                        sq = g4(s2.take(BIG))
                        P.tt(y2, xv, V(prm[l][:, P_SDCH:P_SDCH + 4].unsqueeze(2).to_broadcast([128, 4, 128]), [("prm", l)]), ALU.mult)
                        P.tt(y2, y2, py4, ALU.add)
                        P.tt(y2, y2, zv, ALU.mult)
                        P.act(sq, y2, AF.Square)
                        pss = s2.take(PQ)
                        for jc in range(4):
                            P.mm(pss, ONES, sq(lambda a: a[:, jc, :]), start=(jc == 0), stop=(jc == 3))
                        rs = s2.take(BIG)(lambda a: a[:, 0:128])
                        P.act(rs, pss, AF.Ln, bias=EPS, scale=1.0 / 512)
                        P.act(rs, rs, AF.Exp, scale=-0.5)
                        P.tt(y2, y2, V(prm[l][:, P_SNG:P_SNG + 4].unsqueeze(2).to_broadcast([128, 4, 128]), [("prm", l)]), ALU.mult)
                        P.tt(V(yT[:, :, sl], [("y", k) for k in range(4)]), y2, bc4(rs), ALU.mult)
                    with Scope() as s2:
                        pi = s2.take(PB)
                        for g in range(2):
                            P.mm(pi(lambda a: a[:, g * 256:(g + 1) * 256]), btok(lambda a: a[:, g, :]),
                                 xdtwv(lambda a: a[:, g * 256:(g + 1) * 256]))
                        P.tt(h8(Sv), h8(Sv), dec(lambda a: a.unsqueeze(2).to_broadcast([128, 8, 64])), ALU.mult)
                        P.tt(Sv, Sv, pi, ALU.add)
                        P.cp(Sb, Sv, eng="scalar")

    def mlstm(l):
        conv_silu(l, car_m, "car_m", P_MCW, P_MCB, 8)
        for j in range(4):
            P.act(cb(8 + j), cb(8 + j), AF.Sigmoid)
        for j in range(4):
            P.act(cb(12 + j), cb(12 + j), AF.Silu)
        P.ts(cb(20, p1=8), cb(20, p1=8), prmc(l, P_MIFB, 8), None, ALU.add)
        isq = float(1.0 / np.sqrt(128.0))
        g4 = lambda v: v(lambda a: a.rearrange("p (g t) -> p g t", g=4))
        C4 = g4(V(Cml[l][:], [("Cml", l)]))
        C4b = g4(V(Cmlb[l][:], [("Cmlb", l)]))
        n4 = V(nml[l][:], [("nml", l)])
        n4b = V(nmlb[l][:], [("nmlb", l)])
        ONESb = V(onesb[:], ["onesb"])

        def bc4(v):
            return v(lambda a: a.unsqueeze(1).to_broadcast([128, 4, 128]))

        def bch(v):
            return v(lambda a: a.unsqueeze(2).to_broadcast([128, 4, 128]))

        def pbank(sc):
            return g4(sc.take(PB))

        for s in range(NSUB):
            sl = slice(s * 128, (s + 1) * 128)
            csl = slice(PADC + s * 128, PADC + (s + 1) * 128)
            ck4 = lambda j0: [("c", j0 + i) for i in range(4)]
            qv = V(cbuf[:, 0:4, csl], ck4(0))
            kv = V(cbuf[:, 4:8, csl], ck4(4))
            ov = V(cbuf[:, 8:12, csl], ck4(8))
            zv = V(cbuf[:, 12:16, csl], ck4(12))
            vv = V(cbuf[:, 16:20, csl], ck4(16))
            with Scope() as sc:
                tk = sc.take(TK)
                with Scope() as s2:
                    p0 = s2.take(PQ)
                    P.tr(p0(lambda a: a[:, 0:8]), cbs(20, s, 0, 8), C(C_IDN, 8, 0, 8))
                    P.cp(tk(lambda a: a[:, 0:4]), p0(lambda a: a[:, 0:4]))
                    P.act(tk(lambda a: a[:, 4:8]), p0(lambda a: a[:, 4:8]), AF.Exp, scale=-1.0)
                    P.act(tk(lambda a: a[:, 4:8]), tk(lambda a: a[:, 4:8]), AF.Ln, bias=1.0)
                li = tk(lambda a: a[:, 0:4])
                nl = tk(lambda a: a[:, 4:8])
                cumT = sc.take(BIG)(lambda a: a[0:4, 0:128])
                with Scope() as s2:
                    p1 = s2.take(PQ)
                    P.mm(p1(lambda a: a[0:4, :]), nl, TRI)
                    P.ts(cumT, p1(lambda a: a[0:4, :]), -1.0)
                    p2 = s2.take(PQ)
                    P.mm(p2(lambda a: a[:, 0:4]), ONES, nl)
                    P.mm(p2(lambda a: a[:, 4:8]), SGT, nl)
                    P.act(tk(lambda a: a[:, 8:12]), p2(lambda a: a[:, 0:4]), AF.Exp, scale=-1.0)
                    gt = s2.take(TK)
                    P.stt(gt(lambda a: a[:, 0:4]), p2(lambda a: a[:, 4:8]), -1.0, li, ALU.mult, ALU.add)
                    P.act(tk(lambda a: a[:, 12:16]), gt(lambda a: a[:, 0:4]), AF.Exp)
                dec, wend = tk(lambda a: a[:, 8:12]), tk(lambda a: a[:, 12:16])
                vt = g4(sc.take(SMB))
                with Scope() as s2:
                    pvt = pbank(s2)
                    for h in range(4):
                        P.tr(pvt(lambda a: a[:, h, :]), vv(lambda a: a[:, h, :]), IDN)
                    P.cp(vt, pvt, eng="scalar")
                ed = g4(sc.take(SMB))
                with Scope() as s2:
                    t_ = g4(s2.take(BIG))
                    rh = g4(s2.take(BIG))
                    P.stt(t_, bc4(SGT), -1.0, bch(nl), ALU.mult, ALU.mult)
                    P.tt(rh, bc4(IDN), bch(li), ALU.mult)
                    P.tt(rh, rh, t_, ALU.add)
                    pl = pbank(s2)
                    for h in range(4):
                        P.mm(pl(lambda a: a[:, h, :]), rh(lambda a: a[:, h, :]), TRI)
                    P.act(ed, pl, AF.Exp)
                wq_ = g4(sc.take(SMB))
                with Scope() as s2:
                    pk = pbank(s2)
                    for h in range(4):
                        P.mm(pk(lambda a: a[:, h, :]), kv(lambda a: a[:, h, :]), qv(lambda a: a[:, h, :]))
                    m1 = g4(s2.take(SMB))
                    P.stt(m1, pk, isq, bc4(TRI), ALU.mult, ALU.mult)
                    P.tt(wq_, m1, ed, ALU.mult)
                qw = g4(sc.take(SMB))
                with Scope() as s2:
                    pbc = pbank(s2)
                    for h in range(4):
                        P.mm(pbc(lambda a: a[:, h, :]), SEL(h, 4), cumT)
                    wi = g4(s2.take(BIG))
                    P.act(wi, pbc, AF.Exp)
                    P.tt(qw, qv, wi, ALU.mult)
                hh_ = g4(sc.take(BIG))
                with Scope() as s2:
                    da = g4(s2.take(BIG))
                    pd = pbank(s2)
                    for h in range(4):
                        P.mm(pd(lambda a: a[:, h, :]), ONESb, wq_(lambda a: a[:, h, :]), start=True, stop=False)
                        P.mm(pd(lambda a: a[:, h, :]), n4b(lambda a: a[:, h:h + 1].to_broadcast([128, 128])),
                             qw(lambda a: a[:, h, :]), start=False, stop=True)
                    P.act(da, pd, AF.Abs)
                    P.ts(da, da, 1.0, None, ALU.max)
                    P.recip(da, da)
                    pn = pbank(s2)
                    for h in range(4):
                        P.mm(pn(lambda a: a[:, h, :]), vt(lambda a: a[:, h, :]), wq_(lambda a: a[:, h, :]), start=True, stop=False)
                        P.mm(pn(lambda a: a[:, h, :]), C4b(lambda a: a[:, h, :]), qw(lambda a: a[:, h, :]), start=False, stop=True)
                    P.tt(hh_, pn, da, ALU.mult)
                P.tt(hh_, hh_, ov, ALU.mult)
                with Scope() as s2:
                    hc = g4(s2.take(BIG))
                    sq = g4(s2.take(BIG))
                    pm = pbank(s2)
                    for h in range(4):
                        P.mm(pm(lambda a: a[:, h, :]), ONES, hh_(lambda a: a[:, h, :]))
                    P.stt(hc, pm, -1.0 / 128, hh_, ALU.mult, ALU.add)
                    P.act(sq, hc, AF.Square)
                    pv = pbank(s2)
                    for h in range(4):
                        P.mm(pv(lambda a: a[:, h, :]), ONES, sq(lambda a: a[:, h, :]))
                    P.act(sq, pv, AF.Ln, bias=EPS, scale=1.0 / 128)
                    P.act(sq, sq, AF.Exp, scale=-0.5)
                    P.tt(hc, hc, sq, ALU.mult)
                    P.tt(hc, hc, V(prm[l][:, P_MNG:P_MNG + 4].unsqueeze(2).to_broadcast([128, 4, 128]), [("prm", l)]), ALU.mult)
                    P.tt(V(yT[:, :, sl], [("y", k) for k in range(4)]), hc, zv, ALU.mult)
                with Scope() as s2:
                    kw = g4(s2.take(SMB))
                    pt = pbank(s2)
                    for h in range(4):
                        P.tr(pt(lambda a: a[:, h, :]), kv(lambda a: a[:, h, :]), IDN)
                    P.stt(kw, pt, isq, bch(wend), ALU.mult, ALU.mult)
                    pi = pbank(s2)
                    pi2 = s2.take(PQ)
                    for h in range(4):
                        P.mm(pi(lambda a: a[:, h, :]), kw(lambda a: a[:, h, :]), vt(lambda a: a[:, h, :]))
                        P.mm(pi2(lambda a: a[:, h:h + 1]), kw(lambda a: a[:, h, :]), ONESb(lambda a: a[:, 0:1]))
                    P.tt(C4, C4, bch(dec), ALU.mult)
                    P.tt(C4, C4, pi, ALU.add)
                    P.cp(C4b, C4, eng="scalar")
                    P.tt(n4, n4, dec, ALU.mult)
                    P.tt(n4, n4, pi2(lambda a: a[:, 0:4]), ALU.add)
                    P.cp(n4b, n4, eng="scalar")

    def rwkv_pre(l):
        pass

    def rwkv(l):
        ck = ("car_r", l)
        for j in range(15 if l > 0 else 14):
            p1 = 128 if j < 12 else (64 if j < 14 else 32)
            with Scope() as sc:
                P.cp(V(cbuf[0:p1, j, 3:4], [("c", j)]), V(car_r[l][0:p1, j, 0:1], [ck]), eng="scalar")
                d = sc.take(BIG)(lambda a: a[0:p1, :])
                ce = "vector"
                P.tt(d, V(cbuf[0:p1, j, 3:3 + T], [("c", j)]), V(cbuf[0:p1, j, 4:4 + T], [("c", j)]), ALU.subtract, eng=ce)
                P.cp(V(car_r[l][0:p1, j, 0:1], [ck]), V(cbuf[0:p1, j, T + 3:T + 4], [("c", j)]), eng="scalar")
                P.stt(cb(j, p1=p1), d, prmc(l, P_RMU + j, p1), cb(j, p1=p1), ALU.mult, ALU.add, eng=ce)
        for j in range(4):
            P.act(cb(15 + j), cb(15 + j), AF.Silu)
        P.act(cb(12, p1=64), cb(12, p1=64), AF.Tanh)
        P.memset(V(cbuf[64:65, 12, :], [("c", 12)]), 1.0)
        P.memset(V(cbuf[64:65, 13, :], [("c", 13)]), 1.0)
        vfv = lambda jc: V(vfirst[:, jc, :], [("vfirst", jc)])
        for jc in range(4):
            if l == 0:
                P.cp(vfv(jc), cb(8 + jc), eng="scalar")
            else:
                with Scope() as sc:
                    pv = sc.take(PB)
                    P.mm(pv, V(ra2[l][0:32, 512 + jc * 128:512 + (jc + 1) * 128], ["ra2"]), cb(14, p1=32))
                    sg = sc.take(BIG)
                    P.act(sg, pv, AF.Sigmoid, bias=prmc(l, P_RV0 + jc))
                    d = sc.take(BIG)
                    P.tt(d, vfv(jc), cb(8 + jc), ALU.subtract)
                    P.tt(d, d, sg, ALU.mult)
                    P.tt(cb(8 + jc), cb(8 + jc), d, ALU.add)
        LD = 0.6065306597126334
        g4 = lambda v: v(lambda a: a.rearrange("p (g t) -> p g t", g=4))
        AtT, RtT, BtT, KtT, VTb = [g4(V(bd[q][:], [("bd", q)])) for q in range(5)]
        IDNb = V(idnb[:], ["idnb"])
        Tv = g4(V(Trw[l][:], [("Trw", l)]))
        Tvb = g4(V(Trwb[l][:], [("Trwb", l)]))

        def bc4(v):
            return v(lambda a: a.unsqueeze(1).to_broadcast([128, 4, 128]))

        def prmb(col):
            return V(prm[l][:, col:col + 4].unsqueeze(2).to_broadcast([128, 4, 128]), [("prm", l)])

        def pbank(sc, bf=False):
            pp = sc.take(PB)
            if bf:
                return pp(lambda a: a.bitcast(BF16)[:, 0:512].rearrange("p (g t) -> p g t", g=4))
            return g4(pp)

        for s in range(NSUB):
            sl = slice(s * 128, (s + 1) * 128)
            csl = slice(PADC + s * 128, PADC + (s + 1) * 128)
            ck4 = lambda j0: [("c", j0 + i) for i in range(4)]
            rview = V(cbuf[:, 0:4, csl], ck4(0))
            kview = V(cbuf[:, 4:8, csl], ck4(4))
            vview = V(cbuf[:, 8:12, csl], ck4(8))
            zview = V(cbuf[:, 15:19, csl], ck4(15))
            with Scope() as so:
              yr = g4(so.take(BIG))
              with Scope() as sc:
                sgv = sc.take(BIG)
                with Scope() as s2:
                    pw = s2.take(PB)
                    P.mm(pw, cbs(12, s, 0, 65), V(rw2[l][:], ["rw2"]))
                    P.act(sgv, pw, AF.Sigmoid)
                a_ = g4(sc.take(BIG))
                kk = g4(sc.take(BIG))
                with Scope() as s2:
                    pa = pbank(s2)
                    for jc in range(4):
                        P.mm(pa(lambda a: a[:, jc, :]), V(ra2[l][0:65, jc * 128:(jc + 1) * 128], ["ra2"]), cbs(13, s, 0, 65))
                    P.act(a_, pa, AF.Sigmoid)
                P.tt(kk, kview, prmb(P_RKK), ALU.mult)
                with Scope() as s2:
                    sq = g4(s2.take(BIG))
                    P.act(sq, kk, AF.Square)
                    pn = pbank(s2)
                    for jc in range(4):
                        P.mm(pn(lambda a: a[:, jc, :]), BONES, sq(lambda a: a[:, jc, :]))
                    P.ts(sq, pn, 1e-24, None, ALU.max)
                    P.act(sq, sq, AF.Ln)
                    P.act(sq, sq, AF.Exp, scale=-0.5)
                    P.tt(kk, kk, sq, ALU.mult)
                    P.ts(sq, a_, -1.0, None, ALU.add)
                    P.tt(sq, sq, prmb(P_RKA), ALU.mult)
                    P.stt(kview, sq, 1.0, kview, ALU.add, ALU.mult)
                P.tt(a_, kk, a_, ALU.mult)
                Wt = g4(sc.take(BIG))
                iW, Wx = g4(sc.take(SMB)), g4(sc.take(SMB))
                with Scope() as s2:
                    pc1, pc2 = pbank(s2), pbank(s2)
                    for jc in range(4):
                        P.mm(pc1(lambda a: a[:, jc, :]), sgv(lambda a: a[:, jc * 128:(jc + 1) * 128]), MU64)
                        P.mm(pc2(lambda a: a[:, jc, :]), sgv(lambda a: a[:, jc * 128:(jc + 1) * 128]), MSU64)
                    P.act(Wt, pc1, AF.Exp, scale=-LD)
                    P.act(iW, pc1, AF.Exp, scale=LD)
                    P.act(Wx, pc2, AF.Exp, scale=-LD)
                for cc in range(2):
                    wsl = slice(cc * 64, (cc + 1) * 64)
                    with Scope() as sk:
                        for hh in range(2):
                            pr = slice(hh * 64, (hh + 1) * 64)
                            src = lambda v: v(lambda a: a[pr, :, wsl])
                            dst = lambda v: v(lambda a: a[pr, :, hh * 64:(hh + 1) * 64])
                            P.stt(dst(AtT), src(kk), -1.0, src(Wx), ALU.mult, ALU.mult)
                            P.tt(dst(RtT), src(rview), src(Wt), ALU.mult)
                            P.tt(dst(BtT), src(a_), src(iW), ALU.mult)
                            P.tt(dst(KtT), src(kview), src(iW), ALU.mult)
                            P.cp(dst(VTb), src(vview), eng="scalar")

                        def trans(srcv):
                            o = g4(sk.take(SMB))
                            with Scope() as s2:
                                pt = pbank(s2, bf=True)
                                for jc in range(4):
                                    P.tr(pt(lambda a: a[:, jc, :]), srcv(lambda a: a[:, jc, :]), IDNb)
                                P.cp(o, pt, eng=evac_eng())
                            return o
                        Vb, Btk, Ktk = trans(VTb), trans(BtT), trans(KtT)

                        def mm4(pp, lhsT, rhs, **kw):
                            for jc in range(4):
                                P.mm(pp(lambda a: a[:, jc, :]), lhsT(lambda a: a[:, jc, :]), rhs(lambda a: a[:, jc, :]), **kw)

                        def acc4(pp, terms):
                            for jc in range(4):
                                for ti, (l_, r_) in enumerate(terms):
                                    P.mm(pp(lambda a: a[:, jc, :]), l_(lambda a: a[:, jc, :]), r_(lambda a: a[:, jc, :]),
                                         start=(ti == 0), stop=(ti == len(terms) - 1))

                        inv = Scope()

                        def mmask(lhsT, rhs, mask, own=None):
                            o = g4((own or sk).take(SMB))
                            with Scope() as s2:
                                pp = pbank(s2)
                                mm4(pp, lhsT, rhs)
                                P.tt(o, pp, bc4(mask), ALU.mult)
                            return o
                        X = mmask(BtT, AtT, MSU64, own=inv)
                        Xt = mmask(AtT, BtT, MSL64, own=inv)
                        AakT = mmask(KtT, AtT, MSU64)
                        RbT = mmask(BtT, RtT, MU64)
                        RkT = mmask(KtT, RtT, MU64)
                        N = g4(inv.take(SMB))
                        P.tt(N, bc4(IDN), X, ALU.add)
                        Pm, Pt = X, Xt
                        for k in range(1, 6):
                            nxt = Scope()
                            with Scope() as s2:
                                pp = pbank(s2)
                                mm4(pp, Pt, Pm)
                                P2 = g4(nxt.take(SMB))
                                P.cp(P2, pp, eng=evac_eng())
                                pp2 = pbank(s2)
                                mm4(pp2, Pm, Pt)
                                Pt2 = g4(nxt.take(SMB))
                                P.cp(Pt2, pp2, eng=evac_eng())
                                pn_ = pbank(s2)
                                mm4(pn_, Pt2, N)
                                N2 = g4(nxt.take(SMB))
                                P.tt(N2, N, pn_, ALU.add)
                            inv.__exit__()
                            inv = nxt
                            Pm, Pt, N = P2, Pt2, N2
                        with Scope() as s2:
                            pr0 = pbank(s2)
                            acc4(pr0, [(AtT, Tvb), (AakT, Vb)])
                            R0 = g4(sk.take(SMB))
                            P.cp(R0, pr0, eng="scalar")
                            pu = pbank(s2)
                            mm4(pu, N, R0)
                            U = g4(sk.take(SMB))
                            P.cp(U, pu, eng="scalar")
                        inv.__exit__()
                        with Scope() as s2:
                            py = pbank(s2)
                            acc4(py, [(Tvb, RtT), (U, RbT), (Vb, RkT)])
                            for hh in range(2):
                                pr = slice(hh * 64, (hh + 1) * 64)
                                P.cp(yr(lambda a: a[pr, :, wsl]), py(lambda a: a[pr, :, hh * 64:(hh + 1) * 64]), eng="scalar")
                            pS = pbank(s2)
                            acc4(pS, [(Btk, U), (Ktk, Vb)])
                            P.tt(Tv, Tv, pS, ALU.add)
                            P.tt(Tv, Tv, Wt(lambda a: a[:, :, cc * 64 + 63:cc * 64 + 64].to_broadcast([128, 4, 128])), ALU.mult)
                            P.cp(Tvb, Tv, eng="scalar")
              if True:
                with Scope() as s2:
                    pm = pbank(s2)
                    for jc in range(4):
                        P.mm(pm(lambda a: a[:, jc, :]), BONES, yr(lambda a: a[:, jc, :]))
                    hc = g4(s2.take(BIG))
                    P.stt(hc, pm, -1.0 / 64, yr, ALU.mult, ALU.add)
                    sq = g4(s2.take(BIG))
                    P.act(sq, hc, AF.Square)
                    pv = pbank(s2)
                    for jc in range(4):
                        P.mm(pv(lambda a: a[:, jc, :]), BONES, sq(lambda a: a[:, jc, :]))
                    P.act(sq, pv, AF.Ln, bias=64e-5, scale=1.0 / 64)
                    P.act(sq, sq, AF.Exp, scale=-0.5)
                    P.tt(hc, hc, sq, ALU.mult)
                    P.tt(hc, hc, prmb(P_RLNG), ALU.mult)
                    P.tt(hc, hc, prmb(P_RLNB), ALU.add)
                    P.tt(sq, rview, prmb(P_RRK), ALU.mult)
                    P.tt(sq, sq, kview, ALU.mult)
                    pb_ = pbank(s2)
                    for jc in range(4):
                        P.mm(pb_(lambda a: a[:, jc, :]), BONES, sq(lambda a: a[:, jc, :]))
                    P.tt(sq, pb_, vview, ALU.mult)
                    P.tt(hc, hc, sq, ALU.add)
                    P.tt(V(yT[:, :, sl], [("y", k) for k in range(4)]), hc, zview, ALU.mult)

    MIX = {"gla": gla, "rwkv": rwkv, "ssd": ssd, "mlstm": mlstm}

    def finish():
        fin = {}
        for key, cnt in P.dma_cnt.items():
            if key[1].startswith("out_") or key[1].startswith("dbg_"):
                fin[key] = cnt
        P._emit_waits("gpsimd", fin)
        P.emit()
        P.close()
        for cm in reversed(cms):
            cm.__exit__(None, None, None)
        return nc, P

    if upto == 'init':
        return finish()
    for tile in range(n_tiles):
        tok0 = tile * T
        for s in range(NSUB):
            for jq in range(4):
                with Scope() as sc:
                    xi = sc.take(BIG)
                    r0 = tok0 + s * 128
                    P.dma("gpsimd", "xin_%s" % str(xi.keys[0][1]), xi.ap, x_d[r0:r0 + 128, jq * 512:(jq + 1) * 512],
                          writes=xi.keys)
                    pp = sc.take(PB)
                    for q in range(4):
                        P.tr(pp(lambda a: a[:, q * 128:(q + 1) * 128]), xi(lambda a: a[:, q * 128:(q + 1) * 128]), IDN)
                    P.cp(V(xT[:, jq * 4:(jq + 1) * 4, s * 128:(s + 1) * 128], [("xT", jq * 4 + q) for q in range(4)]),
                         pp(lambda a: a.rearrange("p (q t) -> p q t", q=4)), eng=evac_eng())
        if upto == 'xload':
            return finish()
        for l in range(depth):
            P.dma("gpsimd", "gkw", gkw1[:], gkw_d[l], writes=["gkw"])
            P.dma("gpsimd", "rw2", rw21[:], rw2_d[l], writes=["rw2"])
            P.dma("gpsimd", "ra2", ra21[:], ra2_d[l], writes=["ra2"])
            rmsnorm(P_NG, l, lambda j: V(hT[:, j, :], [("hT", j)]))
            for m in mixers:
                if m == "gla":
                    gla_pre(l)
                inproj(l, m)
                if upto == 'inproj':
                    return finish()
                MIX[m](l)
                if debug:
                    mi = MIXN.index(m)
                    for k in range(4):
                        with Scope() as sc:
                            yk = sc.take(BIG)
                            P.cp(yk, V(yT[:, k, :], [("y", k)]))
                            P.dma("gpsimd", "dbg_%s" % str(yk.keys[0][1]),
                                  dbg_d[l, mi * 512 + k * 128:mi * 512 + (k + 1) * 128, tok0:tok0 + T],
                                  yk.ap, reads=yk.keys, writes=["dbgout"])
                outproj(l, m)
        if upto == 'norm':
            return finish()
        rmsnorm(P_FNG, depth - 1, lambda j: cb(j))
        if upto == 'fnorm':
            return finish()
        for s in range(NSUB):
            for jq in range(4):
                with Scope() as sc:
                    pp = sc.take(PB)
                    for q in range(4):
                        P.tr(pp(lambda a: a[:, q * 128:(q + 1) * 128]), cbs(jq * 4 + q, s), IDN)
                    xo = sc.take(BIG)
                    P.cp(xo, pp, eng=evac_eng())
                    r0 = tok0 + s * 128
                    P.dma("gpsimd", "out_%s" % str(xo.keys[0][1]), out_d[r0:r0 + 128, jq * 512:(jq + 1) * 512], xo.ap,
                          reads=xo.keys, writes=["outd"])
    return finish()


def make_in_maps(inputs, batch_of_core):
    cst = make_consts()
    shared = {"cst": cst}
    for l in range(2):
        g = lambda k: np.ascontiguousarray(np.asarray(inputs["%s_%d" % (k, l)], np.float32))
        shared["win%d" % l] = g("w_in")
        shared["wout%d" % l] = g("w_out")
        shared["prm%d" % l] = pack_params(inputs, l)
        shared["gkw%d" % l] = np.concatenate([g("gla_gk_w2"), g("gla_gk_b")[None, :]], axis=0)
        shared["rw2%d" % l] = np.concatenate([g("rwkv_w2"), g("rwkv_w0")[None, :]], axis=0)
        shared["ra2%d" % l] = np.concatenate([np.concatenate([g("rwkv_a2"), g("rwkv_a0")[None, :]], axis=0), np.concatenate([g("rwkv_v2") if l > 0 else np.zeros((32, 512), np.float32), np.zeros((33, 512), np.float32)], axis=0)], axis=1)
    x = np.asarray(inputs["x"], np.float32)
    maps = []
    for b in batch_of_core:
        d = dict(shared)
        d["x"] = np.ascontiguousarray(x[b])
        maps.append(d)
    return maps


_INPUT_NAMES = (
    'x',
    'norm_g_0',
    'w_in_0',
    'w_out_0',
    'gla_gk_w2_0',
    'gla_gk_b_0',
    'gla_norm_g_0',
    'rwkv_mu_0',
    'rwkv_w0_0',
    'rwkv_w2_0',
    'rwkv_a0_0',
    'rwkv_a2_0',
    'rwkv_k_k_0',
    'rwkv_k_a_0',
    'rwkv_r_k_0',
    'rwkv_ln_g_0',
    'rwkv_ln_b_0',
    'ssd_conv_w_0',
    'ssd_conv_b_0',
    'ssd_dt_bias_0',
    'ssd_a_log_0',
    'ssd_d_0',
    'ssd_norm_g_0',
    'mlstm_conv_w_0',
    'mlstm_conv_b_0',
    'mlstm_ig_b_0',
    'mlstm_fg_b_0',
    'mlstm_norm_g_0',
    'norm_g_1',
    'w_in_1',
    'w_out_1',
    'gla_gk_w2_1',
    'gla_gk_b_1',
    'gla_norm_g_1',
    'rwkv_mu_1',
    'rwkv_w0_1',
    'rwkv_w2_1',
    'rwkv_a0_1',
    'rwkv_a2_1',
    'rwkv_v0_1',
    'rwkv_v2_1',
    'rwkv_k_k_1',
    'rwkv_k_a_1',
    'rwkv_r_k_1',
    'rwkv_ln_g_1',
    'rwkv_ln_b_1',
    'ssd_conv_w_1',
    'ssd_conv_b_1',
    'ssd_dt_bias_1',
    'ssd_a_log_1',
    'ssd_d_1',
    'ssd_norm_g_1',
    'mlstm_conv_w_1',
    'mlstm_conv_b_1',
    'mlstm_ig_b_1',
    'mlstm_fg_b_1',
    'mlstm_norm_g_1',
    'final_norm_g',
)


_CACHE = {}


def kernel(**inputs):
    inputs = {n: inputs[n] for n in _INPUT_NAMES}
    if "nc" not in _CACHE:
        _CACHE["nc"] = build_program()[0]
    nc = _CACHE["nc"]
    batch_of_core = [i // 2 for i in range(8)]
    maps = make_in_maps(inputs, batch_of_core)
    res = run_bass_kernel_spmd(nc, maps, core_ids=list(range(8)))
    out = np.stack([res.results[2 * b]["out"] for b in range(4)], axis=0)
    return out.astype(np.float32)
```

```python
import numpy as np
import concourse.bass as bass
import concourse.mybir as mybir
from concourse.bass_utils import run_bass_kernel_spmd

F32 = mybir.dt.float32
F32R = mybir.dt.float32r
BF16 = mybir.dt.bfloat16
AF = mybir.ActivationFunctionType
ALU = mybir.AluOpType

D = 2048
SEQ = 2048
T = 512
NSUB = T // 128
PADC = 4
CW = T + PADC
GW = 512
NIN = (7840, 7872)
EPS = 1e-6


def in_layout(l):
    rs = 3 * GW + 64 + 64 + (32 if l > 0 else 0)
    return (('gla_q', 256), ('gla_k', 256), ('gla_v', 512), ('gla_gk', 16), ('gla_z', 512),
            ('rwkv_shift', rs), ('rwkv_z', 512), ('ssd_xbc', 1024), ('ssd_dt', 8), ('ssd_z', 512),
            ('mlstm_qk', 1024), ('mlstm_v', 512), ('mlstm_i', 4), ('mlstm_f', 4), ('mlstm_o', 512),
            ('mlstm_z', 512))


def col_offsets(l):
    off = {}
    c = 0
    for n, w in in_layout(l):
        off[n] = c
        c += w
    return off


_o = [0]


def _al(n):
    r = _o[0]
    _o[0] += n
    return r


P_NG = _al(16); P_GLAG = _al(4); P_RMU = _al(15); P_RA0 = _al(4); P_RKK = _al(4); P_RKA = _al(4); P_RRK = _al(4)
P_RLNG = _al(4); P_RLNB = _al(4); P_RV0 = _al(4); P_SCW = _al(32); P_SCB = _al(8); P_SDTB = _al(1); P_SALOG = _al(1)
P_SDCH = _al(4); P_SNG = _al(4); P_MCW = _al(32); P_MCB = _al(8); P_MIFB = _al(1); P_MNG = _al(4); P_FNG = _al(16)
NPRM = _o[0]
C_IDN, C_ONES, C_TRI, C_SGT, C_MU64, C_MSU64, C_MSL64, C_BONES, NCST = 0, 128, 256, 384, 512, 640, 768, 896, 1024


def cols128(v):
    v = np.asarray(v, np.float32).reshape(-1)
    n = (len(v) + 127) // 128
    o = np.zeros((n * 128,), np.float32)
    o[:len(v)] = v
    return o.reshape(n, 128).T


def pack_params(inp, l):
    g = lambda k: np.asarray(inp["%s_%d" % (k, l)], np.float32)
    prm = np.zeros((128, NPRM), np.float32)
    prm[:, P_NG:P_NG + 16] = cols128(g("norm_g"))
    prm[:, P_GLAG:P_GLAG + 4] = cols128(g("gla_norm_g"))
    mu = g("rwkv_mu")
    prm[:, P_RMU:P_RMU + 12] = cols128(mu[:1536])
    prm[:64, P_RMU + 12] = mu[1536:1600]
    prm[:64, P_RMU + 13] = mu[1600:1664]
    if l > 0:
        prm[:32, P_RMU + 14] = mu[1664:1696]
        prm[:, P_RV0:P_RV0 + 4] = cols128(g("rwkv_v0"))
    prm[:, P_RA0:P_RA0 + 4] = cols128(g("rwkv_a0"))
    prm[:, P_RKK:P_RKK + 4] = cols128(g("rwkv_k_k"))
    prm[:, P_RKA:P_RKA + 4] = cols128(g("rwkv_k_a"))
    prm[:, P_RRK:P_RRK + 4] = cols128(g("rwkv_r_k"))
    prm[:, P_RLNG:P_RLNG + 4] = cols128(g("rwkv_ln_g"))
    prm[:, P_RLNB:P_RLNB + 4] = cols128(g("rwkv_ln_b"))
    cw = g("ssd_conv_w")
    for j in range(4):
        prm[:, P_SCW + j * 8:P_SCW + j * 8 + 8] = cols128(cw[j])
    prm[:, P_SCB:P_SCB + 8] = cols128(g("ssd_conv_b"))
    prm[:8, P_SDTB] = g("ssd_dt_bias")
    prm[:8, P_SALOG] = g("ssd_a_log")
    prm[:, P_SDCH:P_SDCH + 4] = cols128(np.repeat(g("ssd_d"), 64))
    prm[:, P_SNG:P_SNG + 4] = cols128(g("ssd_norm_g"))
    cw = g("mlstm_conv_w")
    for j in range(4):
        prm[:, P_MCW + j * 8:P_MCW + j * 8 + 8] = cols128(cw[j])
    prm[:, P_MCB:P_MCB + 8] = cols128(g("mlstm_conv_b"))
    prm[:4, P_MIFB] = g("mlstm_ig_b")
    prm[4:8, P_MIFB] = g("mlstm_fg_b")
    prm[:, P_MNG:P_MNG + 4] = cols128(g("mlstm_norm_g"))
    prm[:, P_FNG:P_FNG + 16] = cols128(np.asarray(inp["final_norm_g"], np.float32))
    return prm


def make_consts():
    c = np.zeros((128, NCST), np.float32)
    i = np.arange(128)
    r, cc = i[:, None], i[None, :]
    same = (r // 64) == (cc // 64)
    c[:, C_IDN:C_IDN + 128] = (r == cc)
    c[:, C_ONES:C_ONES + 128] = 1.0
    c[:, C_TRI:C_TRI + 128] = (r <= cc)
    c[:, C_SGT:C_SGT + 128] = (r > cc)
    c[:, C_MU64:C_MU64 + 128] = same & (r <= cc)
    c[:, C_MSU64:C_MSU64 + 128] = same & (r < cc)
    c[:, C_MSL64:C_MSL64 + 128] = same & (r > cc)
    c[:, C_BONES:C_BONES + 128] = same
    return c


class V:
    def __init__(self, ap, keys, excl=False):
        self.ap = ap
        self.keys = tuple(keys)
        self.excl = excl

    def __call__(self, f):
        v = V(f(self.ap), self.keys, self.excl)
        if hasattr(self, "base"):
            v.base = self.base
        return v


class Prog:
    ENGS = ("sync", "scalar", "vector", "gpsimd", "tensor")

    def __init__(self, nc):
        self.nc = nc
        self.ops = {e: [] for e in self.ENGS}
        self.count = {e: 0 for e in self.ENGS}
        self.waited = {e: {} for e in self.ENGS}
        self.lastw = {}
        self.readers = {}
        self.dma_cnt = {}
        self.sems = {}
        self.ctx = []
        self.nops = 0
        self.limit = None
        self.log = []

    def sem(self, key):
        if key not in self.sems:
            nm = "s_" + "".join(ch for ch in str(key) if ch.isalnum())
            cm = self.nc.semaphore(nm)
            self.sems[key] = cm.__enter__()
            self.ctx.append(cm)
        return self.sems[key]

    def _deps(self, reads, writes):
        deps = {}

        def add(d):
            if d is not None and deps.get(d[0], 0) < d[1]:
                deps[d[0]] = d[1]
        for k in reads:
            add(self.lastw.get(k))
        for k in writes:
            add(self.lastw.get(k))
            for r in self.readers.get(k, ()):
                add(r)
        return deps

    def _emit_waits(self, eng, deps):
        w = self.waited[eng]
        for k, v in deps.items():
            if eng == "tensor" and k == "tensor":
                continue
            if w.get(k, 0) >= v:
                continue
            w[k] = v
            s = self.sem(k)
            self.ops[eng].append(lambda e, s=s, v=v: e.wait_ge(s, v))

    def _commit(self, reads, writes, tok):
        for k in reads:
            self.readers.setdefault(k, []).append(tok)
        for k in writes:
            self.lastw[k] = tok
            self.readers[k] = []

    def op(self, eng, fn, reads=(), writes=()):
        if self.limit is not None and self.nops >= self.limit:
            return
        self._emit_waits(eng, self._deps(reads, writes))
        self.count[eng] += 1
        tok = (eng, self.count[eng])
        s = self.sem(eng)
        self.ops[eng].append(lambda e, fn=fn, s=s: fn(e).then_inc(s, 1))
        self._commit(reads, writes, tok)
        self.nops += 1
        if self.limit is not None:
            import sys as _s
            f = _s._getframe(1)
            ln = []
            while f is not None and len(ln) < 3:
                ln.append(f.f_lineno)
                f = f.f_back
            self.log.append((self.nops, eng, ln))

    def dma(self, eng, slot, out, in_, reads=(), writes=(), **kw):
        if self.limit is not None and self.nops >= self.limit:
            return
        self._emit_waits(eng, self._deps(reads, writes))
        key = ("dma", slot)
        self.dma_cnt[key] = self.dma_cnt.get(key, 0) + 16
        tok = (key, self.dma_cnt[key])
        s = self.sem(key)
        self.ops[eng].append(lambda e, s=s: e.dma_start(out=out, in_=in_, **kw).then_inc(s, 16))
        self._commit(reads, writes, tok)

    def wait_all(self, eng, keys):
        deps = {}
        for k in keys:
            d = self.lastw.get(k)
            if d is not None and deps.get(d[0], 0) < d[1]:
                deps[d[0]] = d[1]
        self._emit_waits(eng, deps)

    def emit(self):
        with self.nc.Block() as block:
            def mk(name):
                def body(e):
                    for f in self.ops[name]:
                        f(e)
                return body
            block.sync(mk("sync"))
            block.scalar(mk("scalar"))
            block.vector(mk("vector"))
            block.gpsimd(mk("gpsimd"))
            block.tensor(mk("tensor"))

    def close(self):
        for cm in reversed(self.ctx):
            cm.__exit__(None, None, None)

    @staticmethod
    def _rw(out, ins):
        r, w = (), out.keys
        for v in ins:
            if isinstance(v, V):
                if v.excl:
                    w = w + v.keys
                else:
                    r = r + v.keys
        return r, w

    @staticmethod
    def _a(x):
        return x.ap if isinstance(x, V) else x

    def mm(self, out, lhsT, rhs, start=True, stop=True):
        r, w = self._rw(out, (lhsT, rhs))
        self.op("tensor", lambda e: e.matmul(out.ap, lhsT.ap, rhs.ap, start=start, stop=stop), reads=r, writes=w)

    def tr(self, out, in_, idn):
        r, w = self._rw(out, (in_, idn))
        self.op("tensor", lambda e: e.transpose(out.ap, in_.ap, idn.ap), reads=r, writes=w)

    def act(self, out, in_, func, bias=None, scale=None):
        kw = {}
        if bias is not None:
            kw["bias"] = self._a(bias)
        if scale is not None:
            kw["scale"] = scale
        r, w = self._rw(out, (in_, bias))
        self.op("scalar", lambda e: e.activation(out.ap, in_.ap, func, **kw), reads=r, writes=w)

    def tt(self, out, a, b, op, eng="vector"):
        r, w = self._rw(out, (a, b))
        self.op(eng, lambda e: e.tensor_tensor(out.ap, a.ap, b.ap, op), reads=r, writes=w)

    def ts(self, out, a, s1, s2=None, op0=ALU.mult, op1=None, eng="vector"):
        r, w = self._rw(out, (a, s1, s2))
        s1, s2 = self._a(s1), self._a(s2)
        if op1 is None:
            self.op(eng, lambda e: e.tensor_scalar(out.ap, a.ap, s1, None, op0), reads=r, writes=w)
        else:
            self.op(eng, lambda e: e.tensor_scalar(out.ap, a.ap, s1, s2, op0, op1), reads=r, writes=w)

    def stt(self, out, a, s, b, op0, op1, eng="vector"):
        r, w = self._rw(out, (a, s, b))
        s = self._a(s)
        self.op(eng, lambda e: e.scalar_tensor_tensor(out.ap, a.ap, s, b.ap, op0, op1), reads=r, writes=w)

    def cp(self, out, a, eng="vector"):
        r, w = self._rw(out, (a,))
        if eng == "scalar":
            self.op("scalar", lambda e: e.copy(out.ap, a.ap), reads=r, writes=w)
        else:
            self.op(eng, lambda e: e.tensor_copy(out.ap, a.ap), reads=r, writes=w)

    def recip(self, out, a):
        r, w = self._rw(out, (a,))
        self.op("vector", lambda e: e.reciprocal(out.ap, a.ap), reads=r, writes=w)

    def memset(self, out, val, eng="vector"):
        self.op(eng, lambda e: e.memset(out.ap, val), writes=out.keys)


class Pool:
    def __init__(self, name, views):
        self.name = name
        self.free = list(views)

    def get(self):
        if not self.free:
            raise RuntimeError("pool %s exhausted" % self.name)
        return self.free.pop(0)

    def put(self, v):
        self.free.append(v)


class DerivedPool:
    def __init__(self, base, fn):
        self.base = base
        self.fn = fn
        self.name = base.name

    def get(self):
        b = self.base.get()
        v = V(self.fn(b.ap), b.keys, b.excl)
        v.base = b
        return v

    def put(self, v):
        self.base.put(v.base)


class Scope:
    def __init__(self):
        self.items = []

    def take(self, pool):
        v = pool.get()
        self.items.append((pool, v))
        return v

    def __enter__(self):
        return self

    def __exit__(self, *a):
        for pool, v in self.items:
            pool.put(v)
        self.items = []
        return False


MIXN = ("gla", "rwkv", "ssd", "mlstm")
NCH = 21


def build_program(n_tiles=4, depth=2, mixers=MIXN, debug=False, upto='all', limit=None):
    nc = bass.Bass("TRN2", target_bir_lowering=False)
    nc.dge_precook = False
    x_d = nc.dram_tensor("x", [SEQ, D], F32, kind="ExternalInput").ap()
    cst_d = nc.dram_tensor("cst", [128, NCST], F32, kind="ExternalInput").ap()
    win_d, wout_d, prm_d, gkw_d, rw2_d, ra2_d = [], [], [], [], [], []
    for l in range(2):
        win_d.append(nc.dram_tensor("win%d" % l, [D, NIN[l]], F32R, kind="ExternalInput").ap())
        wout_d.append(nc.dram_tensor("wout%d" % l, [D, D], F32R, kind="ExternalInput").ap())
        prm_d.append(nc.dram_tensor("prm%d" % l, [128, NPRM], F32, kind="ExternalInput").ap())
        gkw_d.append(nc.dram_tensor("gkw%d" % l, [17, 256], F32, kind="ExternalInput").ap())
        rw2_d.append(nc.dram_tensor("rw2%d" % l, [65, 512], F32, kind="ExternalInput").ap())
        ra2_d.append(nc.dram_tensor("ra2%d" % l, [65, 1024], F32, kind="ExternalInput").ap())
    out_d = nc.dram_tensor("out", [SEQ, D], F32, kind="ExternalOutput").ap()
    dbg_d = nc.dram_tensor("dbg", [2, D, SEQ], F32, kind="ExternalOutput").ap() if debug else None

    P = Prog(nc)
    P.limit = limit
    cms = []

    def sb(name, shape, dt=F32):
        cm = nc.sbuf_tensor("sb_" + name, shape, dt)
        cms.append(cm)
        return cm.__enter__()

    def ps(name, shape, dt=F32):
        cm = nc.psum_tensor("ps_" + name, shape, dt)
        cms.append(cm)
        return cm.__enter__()

    cst = sb("cst", [128, NCST])
    prm = [sb("prm%d" % l, [128, NPRM]) for l in range(2)]
    gkw1 = sb("gkw", [17, 256])
    rw21 = sb("rw2", [65, 512])
    ra21 = sb("ra2", [65, 1024])
    gkw, rw2, ra2 = [gkw1, gkw1], [rw21, rw21], [ra21, ra21]
    xT = sb("xT", [128, 16, T])
    hT = sb("hT", [128, 16, T], F32R)
    NW = 4
    WB = 256
    wbuf = [sb("wbuf%d" % i, [128, 4 * WB], F32R) for i in range(NW)]
    cbuf = sb("cbuf", [128, NCH, CW])
    yT = sb("yT", [128, 4, T], F32R)
    vfirst = sb("vfirst", [128, 4, T])
    bd = [sb("bd%d" % q, [128, 512], BF16) for q in range(5)]
    idnb = sb("idnb", [128, 128], BF16)
    negA = [sb("negA%d" % l, [8, 1]) for l in range(2)]
    Sgla = [sb("Sgla%d" % l, [128, 256]) for l in range(2)]
    Sssd = [sb("Sssd%d" % l, [128, 512]) for l in range(2)]
    Sssdb = [sb("Sssdb%d" % l, [128, 512], BF16) for l in range(2)]
    Cml = [sb("Cml%d" % l, [128, 512]) for l in range(2)]
    Cmlb = [sb("Cmlb%d" % l, [128, 512], BF16) for l in range(2)]
    nml = [sb("nml%d" % l, [128, 4]) for l in range(2)]
    nmlb = [sb("nmlb%d" % l, [128, 4], BF16) for l in range(2)]
    onesb = sb("onesb", [128, 128], BF16)
    Trw = [sb("Trw%d" % l, [128, 512]) for l in range(2)]
    Trwb = [sb("Trwb%d" % l, [128, 512], BF16) for l in range(2)]
    car_s = [sb("car_s%d" % l, [128, 8, 4]) for l in range(2)]
    car_m = [sb("car_m%d" % l, [128, 8, 4]) for l in range(2)]
    car_r = [sb("car_r%d" % l, [128, 15, 4]) for l in range(2)]
    NBIG, NSM, NU = 6, 24, 16
    arena = sb("arena", [128, NU * 512], BF16)
    BIG = Pool("big", [V(sb("big%d" % i, [128, T])[:], [("big", i)]) for i in range(NBIG)])
    SM = Pool("sm", [V(arena[:, (i // 2) * 512 + (i % 2) * 256:(i // 2) * 512 + (i % 2) * 256 + 256].bitcast(F32),
                       [("ar", i // 2)]) for i in range(NSM)])
    SMB = Pool("smb", [V(arena[:, u * 512:(u + 1) * 512], [("ar", u)]) for u in range(NU)])
    TK = Pool("tok8", [V(sb("tok8_%d" % i, [128, 16])[:], [("tok8", i)]) for i in range(6)])
    pb_t = [ps("pb%d" % i, [128, 512]) for i in range(8)]
    PB = Pool("pb", [V(pb_t[i][:], [("pb", i)], excl=True) for i in range(8)])
    PH = DerivedPool(PB, lambda a: a[:, 0:256])
    PQ = DerivedPool(PB, lambda a: a[:, 0:128])

    st = {"ev": 0}

    def evac_eng():
        st["ev"] += 1
        return "scalar" if st["ev"] % 2 else "vector"

    def C(off, n=128, p0=0, p1=128):
        return V(cst[p0:p1, off:off + n], ["cst"])

    IDN, ONES, TRI, SGT = C(C_IDN), C(C_ONES), C(C_TRI), C(C_SGT)
    MU64, MSU64, MSL64, BONES = C(C_MU64), C(C_MSU64), C(C_MSL64), C(C_BONES)

    def SEL(h, k):
        return V(cst[0:k, C_IDN + h:C_IDN + h + 1].to_broadcast([k, 128]), ["cst"])

    def prmc(l, col, p1=128, p0=0):
        return V(prm[l][p0:p1, col:col + 1], [("prm", l)])

    def cb(j, p1=128, p0=0):
        return V(cbuf[p0:p1, j, PADC:CW], [("c", j)])

    def cbs(j, s, p0=0, p1=128):
        return V(cbuf[p0:p1, j, PADC + s * 128:PADC + (s + 1) * 128], [("c", j)])

    P.dma("gpsimd", "cst", cst[:], cst_d, writes=["cst"])
    for l in range(depth):
        P.dma("gpsimd", "prm%d" % l, prm[l][:], prm_d[l], writes=[("prm", l)])
    P.memset(V(cbuf[:], [("c", j) for j in range(NCH)]), 0.0)
    for q in range(5):
        P.memset(V(bd[q][:], [("bd", q)]), 0.0)
    P.cp(V(idnb[:], ["idnb"]), IDN)
    P.memset(V(onesb[:], ["onesb"]), 1.0)
    for l in range(depth):
        P.memset(V(Sgla[l][:], [("Sgla", l)]), 0.0)
        P.memset(V(Sssd[l][:], [("Sssd", l)]), 0.0)
        P.memset(V(Sssdb[l][:], [("Sssdb", l)]), 0.0)
        P.memset(V(Cml[l][:], [("Cml", l)]), 0.0)
        P.memset(V(Cmlb[l][:], [("Cmlb", l)]), 0.0)
        P.memset(V(nml[l][:], [("nml", l)]), 0.0)
        P.memset(V(nmlb[l][:], [("nmlb", l)]), 0.0)
        P.memset(V(Trw[l][:], [("Trw", l)]), 0.0)
        P.memset(V(Trwb[l][:], [("Trwb", l)]), 0.0)
        P.memset(V(car_s[l][:], [("car_s", l, j) for j in range(8)]), 0.0)
        P.memset(V(car_m[l][:], [("car_m", l, j) for j in range(8)]), 0.0)
        P.memset(V(car_r[l][:], [("car_r", l, j) for j in range(15)]), 0.0)
        P.act(V(negA[l][:], [("negA", l)]), prmc(l, P_SALOG, 8), AF.Exp)
        P.ts(V(negA[l][:], [("negA", l)]), V(negA[l][:], [("negA", l)]), -1.0)

    def plan(l, m):
        off = col_offsets(l)
        it = []
        if m == "gla":
            for j in range(2):
                it.append((off['gla_q'] + 128 * j, 128, j, 0))
            for j in range(2):
                it.append((off['gla_k'] + 128 * j, 128, 2 + j, 0))
            for j in range(4):
                it.append((off['gla_z'] + 128 * j, 128, 4 + j, 0))
            for j in range(4):
                it.append((off['gla_v'] + 128 * j, 128, 8 + j, 0))
            it.append((off['gla_gk'], 16, 12, 0))
        elif m == "rwkv":
            b = off['rwkv_shift']
            for j in range(12):
                it.append((b + 128 * j, 128, j, 0))
            it.append((b + 1536, 64, 12, 0))
            it.append((b + 1600, 64, 13, 0))
            if l > 0:
                it.append((b + 1664, 32, 14, 0))
            for j in range(4):
                it.append((off['rwkv_z'] + 128 * j, 128, 15 + j, 0))
        elif m == "ssd":
            for j in range(8):
                it.append((off['ssd_xbc'] + 128 * j, 128, j, 0))
            for j in range(4):
                it.append((off['ssd_z'] + 128 * j, 128, 8 + j, 0))
            it.append((off['ssd_dt'], 8, 12, 0))
        elif m == "mlstm":
            for j in range(8):
                it.append((off['mlstm_qk'] + 128 * j, 128, j, 0))
            for j in range(4):
                it.append((off['mlstm_o'] + 128 * j, 128, 8 + j, 0))
            for j in range(4):
                it.append((off['mlstm_z'] + 128 * j, 128, 12 + j, 0))
            for j in range(4):
                it.append((off['mlstm_v'] + 128 * j, 128, 16 + j, 0))
            it.append((off['mlstm_i'], 8, 20, 0))
        return it

    plans = {(l, m): plan(l, m) for l in range(depth) for m in mixers}

    def groups_of(l, m):
        gs = []
        for it in plans[(l, m)]:
            c0, n = it[0], it[1]
            if gs and gs[-1][0] + gs[-1][1] == c0 and gs[-1][1] + n <= WB:
                gs[-1][2].append((gs[-1][1],) + tuple(it[1:]))
                gs[-1][1] += n
            else:
                gs.append([c0, n, [(0,) + tuple(it[1:])]])
        return gs

    gplans = {(l, m): groups_of(l, m) for l in range(depth) for m in mixers}
    wq = []
    for t in range(n_tiles):
        for l in range(depth):
            for m in mixers:
                for (c0, n, chunks) in gplans[(l, m)]:
                    for kg in range(4):
                        wq.append((win_d[l][kg * 512:(kg + 1) * 512, c0:c0 + n].rearrange("(k p) c -> p k c", p=128), (4, n)))
                mi = MIXN.index(m)
                for q in range(2048 // WB):
                    wq.append((wout_d[l][mi * 512:(mi + 1) * 512, q * WB:(q + 1) * WB]
                               .rearrange("(k p) c -> p k c", p=128), (4, WB)))
    wstate = {"issued": 0, "used": 0}

    def w_issue():
        i = wstate["issued"]
        src, (k, n) = wq[i]
        slot = i % NW
        dst = wbuf[slot][:, 0:k * n].rearrange("p (k n) -> p k n", k=k)
        P.dma("sync", "w%d" % slot, dst, src, writes=[("wbuf", slot)])
        wstate["issued"] += 1

    def w_next():
        i = wstate["used"]
        while wstate["issued"] < min(i + NW, len(wq)):
            w_issue()
        wstate["used"] += 1
        return wbuf[i % NW], ("wbuf", i % NW)

    def inproj(l, m):
        hooks = post_hooks(l, m)
        for (c0, n, chunks) in gplans[(l, m)]:
            with Scope() as sc:
                pps = [sc.take(PB)(lambda a, pb=pb, cn=cn: a[pb:pb + cn, :]) for (o, cn, j, pb) in chunks]
                for kg in range(4):
                    wt, wkey = w_next()
                    w3 = wt[:, 0:4 * n].rearrange("p (k n) -> p k n", k=4)
                    for ci, (o, cn, j, pb) in enumerate(chunks):
                        for k in range(4):
                            P.mm(pps[ci], V(w3[:, k, o:o + cn], [wkey]), V(hT[:, kg * 4 + k, :], [("hT", kg * 4 + k)]),
                                 start=(kg == 0 and k == 0), stop=(kg == 3 and k == 3))
                for ci, (o, cn, j, pb) in enumerate(chunks):
                    P.cp(cb(j, p0=pb, p1=pb + cn), pps[ci], eng=evac_eng())
            for ci, (o, cn, j, pb) in enumerate(chunks):
                if j in hooks:
                    hooks[j]()

    def outproj(l, m):
        for q in range(2048 // WB):
            wt, wkey = w_next()
            w3 = wt[:, 0:4 * WB].rearrange("p (k n) -> p k n", k=4)
            for oc in range(WB // 128):
                with Scope() as sc:
                    pp = sc.take(PB)
                    for k in range(4):
                        P.mm(pp, V(w3[:, k, oc * 128:(oc + 1) * 128], [wkey]), V(yT[:, k, :], [("y", k)]),
                             start=(k == 0), stop=(k == 3))
                    j = q * (WB // 128) + oc
                    xv = V(xT[:, j, :], [("xT", j)])
                    P.tt(xv, xv, pp, ALU.add)

    def rmsnorm(gcol, l, dst_fn):
        with Scope() as sc:
            pp = sc.take(PB)
            for j in range(16):
                with Scope() as s2:
                    sq = s2.take(BIG)
                    P.act(sq, V(xT[:, j, :], [("xT", j)]), AF.Square)
                    P.mm(pp, ONES, sq, start=(j == 0), stop=(j == 15))
            rs = sc.take(BIG)
            P.act(rs, pp, AF.Ln, bias=EPS, scale=1.0 / D)
            P.act(rs, rs, AF.Exp, scale=-0.5)
            for j in range(16):
                P.stt(dst_fn(j), V(xT[:, j, :], [("xT", j)]), prmc(l, gcol + j), rs, ALU.mult, ALU.mult)

    def conv_one(l, car, cname, wcol, bcol, j):
        ck = (cname, l, j)
        with Scope() as sc:
            P.cp(V(cbuf[:, j, 1:4], [("c", j)]), V(car[l][:, j, 0:3], [ck]))
            acc = sc.take(BIG)
            P.ts(acc, V(cbuf[:, j, 1:1 + T], [("c", j)]), prmc(l, wcol + j))
            for tap in range(1, 4):
                P.stt(acc, V(cbuf[:, j, 1 + tap:1 + tap + T], [("c", j)]), prmc(l, wcol + tap * 8 + j), acc,
                      ALU.mult, ALU.add)
            P.cp(V(car[l][:, j, 0:3], [ck]), V(cbuf[:, j, T + 1:T + 4], [("c", j)]), eng="scalar")
            P.act(cb(j), acc, AF.Silu, bias=prmc(l, bcol + j))

    def shift_one(l, j):
        ck = ("car_r", l, j)
        p1 = 128 if j < 12 else (64 if j < 14 else 32)
        with Scope() as sc:
            P.cp(V(cbuf[0:p1, j, 3:4], [("c", j)]), V(car_r[l][0:p1, j, 0:1], [ck]), eng="scalar")
            d = sc.take(BIG)(lambda a: a[0:p1, :])
            P.tt(d, V(cbuf[0:p1, j, 3:3 + T], [("c", j)]), V(cbuf[0:p1, j, 4:4 + T], [("c", j)]), ALU.subtract)
            P.cp(V(car_r[l][0:p1, j, 0:1], [ck]), V(cbuf[0:p1, j, T + 3:T + 4], [("c", j)]), eng="scalar")
            P.stt(cb(j, p1=p1), d, prmc(l, P_RMU + j, p1), cb(j, p1=p1), ALU.mult, ALU.add)

    def post_hooks(l, m):
        h = {}
        if m == "gla":
            for j in range(4):
                h[4 + j] = lambda j=j: P.act(cb(4 + j), cb(4 + j), AF.Silu)
        elif m == "ssd":
            for j in range(8):
                h[j] = lambda j=j: conv_one(l, car_s, "car_s", P_SCW, P_SCB, j)
            for j in range(4):
                h[8 + j] = lambda j=j: P.act(cb(8 + j), cb(8 + j), AF.Silu)
        elif m == "mlstm":
            for j in range(8):
                h[j] = lambda j=j: conv_one(l, car_m, "car_m", P_MCW, P_MCB, j)
            for j in range(4):
                h[8 + j] = lambda j=j: P.act(cb(8 + j), cb(8 + j), AF.Sigmoid)
                h[12 + j] = lambda j=j: P.act(cb(12 + j), cb(12 + j), AF.Silu)
            h[20] = lambda: P.ts(cb(20, p1=8), cb(20, p1=8), prmc(l, P_MIFB, 8), None, ALU.add)
        elif m == "rwkv":
            for j in range(15):
                h[j] = lambda j=j: shift_one(l, j)

            def h12():
                shift_one(l, 12)
                P.act(cb(12, p1=64), cb(12, p1=64), AF.Tanh)
                P.memset(V(cbuf[64:65, 12, :], [("c", 12)]), 1.0)

            def h13():
                shift_one(l, 13)
                P.memset(V(cbuf[64:65, 13, :], [("c", 13)]), 1.0)
            h[12], h[13] = h12, h13
            for j in range(4):
                h[15 + j] = lambda j=j: P.act(cb(15 + j), cb(15 + j), AF.Silu)
        return h

    def gla_pre(l):
        P.memset(V(cbuf[0:32, 12, :], [("c", 12)]), 1.0)

    def gla(l):
        g4 = lambda v: v(lambda a: a.rearrange("p (g t) -> p g t", g=4))
        g2 = lambda v: v(lambda a: a.rearrange("p (g t) -> p g t", g=2))
        Sv = g2(V(Sgla[l][:], [("Sgla", l)]))
        for s in range(NSUB):
            sl = slice(s * 128, (s + 1) * 128)
            csl = slice(PADC + s * 128, PADC + (s + 1) * 128)
            ck = lambda j0, n: [("c", j0 + i) for i in range(n)]
            qv = V(cbuf[:, 0:2, csl], ck(0, 2))
            kv = V(cbuf[:, 2:4, csl], ck(2, 2))
            zv = V(cbuf[:, 4:8, csl], ck(4, 4))
            vv = V(cbuf[:, 8:12, csl], ck(8, 4))
            with Scope() as sc:
                vt = g4(sc.take(SMB))
                with Scope() as s2:
                    pvt = g4(s2.take(PB))
                    for h in range(4):
                        P.tr(pvt(lambda a: a[:, h, :]), vv(lambda a: a[:, h, :]), IDN)
                    P.cp(vt, pvt, eng="scalar")
                nb = sc.take(BIG)
                nlsv = nb(lambda a: a[:, 0:256])
                ercv = nb(lambda a: a[:, 256:512])
                with Scope() as s2:
                    pg = s2.take(PH)
                    P.mm(pg, cbs(12, s, 0, 17), V(gkw[l][:], ["gkw"]))
                    P.act(nlsv, pg, AF.Exp, scale=-1.0)
                    P.act(nlsv, nlsv, AF.Ln, bias=1.0)
                eb = sc.take(BIG)
                eq = g2(eb(lambda a: a[:, 0:256]))
                gb = sc.take(BIG)
                qg = g2(gb(lambda a: a[:, 0:256]))
                kg = g2(gb(lambda a: a[:, 256:512]))
                with Scope() as s2:
                    pc = g2(s2.take(PH))
                    for j in range(2):
                        P.mm(pc(lambda a: a[:, j, :]), nlsv(lambda a: a[:, j * 128:(j + 1) * 128]), TRI)
                    ek = g2(eb(lambda a: a[:, 256:512]))
                    P.act(eq, pc, AF.Exp, scale=-1.0 / 16)
                    P.act(ek, pc, AF.Exp, scale=1.0 / 16)
                    P.stt(qg, qv, 0.125, eq, ALU.mult, ALU.mult)
                    P.tt(kg, kv, ek, ALU.mult)
                kb = sc.take(SMB)
                kdv = kb(lambda a: a[:, 0:256])
                with Scope() as s2:
                    pr = s2.take(PH)
                    P.mm(pr, SGT, nlsv)
                    P.act(ercv, pr, AF.Exp, scale=-1.0 / 16)
                    pt = s2.take(PH)
                    for j in range(2):
                        P.tr(pt(lambda a: a[:, j * 128:(j + 1) * 128]), kv(lambda a: a[:, j, :]), IDN)
                    P.tt(kdv, pt, ercv, ALU.mult)
                attm = g4(sc.take(SMB))
                with Scope() as s2:
                    pa2 = [g2(s2.take(PH)), g2(s2.take(PH))]
                    for h in range(4):
                        j, pb = h // 2, (h % 2) * 64
                        P.mm(pa2[h % 2](lambda a: a[:, j, :]), kg(lambda a: a[pb:pb + 64, j, :]), qg(lambda a: a[pb:pb + 64, j, :]))
                    for hh in range(2):
                        P.tt(attm(lambda a: a[:, hh::2, :]), pa2[hh], TRI(lambda a: a.unsqueeze(1).to_broadcast([128, 2, 128])), ALU.mult)
                with Scope() as s2:
                    po4 = g4(s2.take(PB))
                    for h in range(4):
                        j, pb = h // 2, (h % 2) * 64
                        P.mm(po4(lambda a: a[:, h, :]), vt(lambda a: a[:, h, :]), attm(lambda a: a[:, h, :]), start=True, stop=False)
                        P.mm(po4(lambda a: a[:, h, :]), Sv(lambda a: a[pb:pb + 64, j, :]), qg(lambda a: a[pb:pb + 64, j, :]),
                             start=False, stop=True)
                    osq = g4(s2.take(BIG))
                    P.act(osq, po4, AF.Square)
                    pss = g4(s2.take(PB))
                    for h in range(4):
                        P.mm(pss(lambda a: a[:, h, :]), ONES, osq(lambda a: a[:, h, :]))
                    P.act(osq, pss, AF.Ln, bias=EPS, scale=1.0 / 128)
                    P.act(osq, osq, AF.Exp, scale=-0.5)
                    P.tt(osq, po4, osq, ALU.mult)
                    P.tt(osq, osq, V(prm[l][:, P_GLAG:P_GLAG + 4].unsqueeze(2).to_broadcast([128, 4, 128]), [("prm", l)]), ALU.mult)
                    P.tt(V(yT[:, :, sl], [("y", k) for k in range(4)]), osq, zv, ALU.mult)
                with Scope() as s2:
                    pi = s2.take(PB)
                    for j in range(2):
                        P.mm(pi(lambda a: a[:, j * 256:(j + 1) * 256]), kdv(lambda a: a[:, j * 128:(j + 1) * 128]),
                             vt(lambda a: a[:, 2 * j:2 * j + 2, :].rearrange("p g t -> p (g t)")))
                    pi3 = pi(lambda a: a.rearrange("p (j w) -> p j w", j=2))
                    for hh in range(2):
                        pb = hh * 64
                        sv = Sv(lambda a: a[pb:pb + 64, :, :])
                        P.tt(sv, sv, eq(lambda a: a[pb:pb + 64, :, 127:128].to_broadcast([64, 2, 128])), ALU.mult)
                        P.tt(sv, sv, pi3(lambda a: a[pb:pb + 64, :, hh * 128:(hh + 1) * 128]), ALU.add)

    def ssd(l):
        g4 = lambda v: v(lambda a: a.rearrange("p (g t) -> p g t", g=4))
        Sv = V(Sssd[l][:], [("Sssd", l)])
        Sb = V(Sssdb[l][:], [("Sssdb", l)])

        def bc4(v):
            return v(lambda a: a.unsqueeze(1).to_broadcast([128, 4, 128]))

        with Scope() as st_:
            dt_ = st_.take(BIG)(lambda a: a[0:8, :])
            dta_ = st_.take(BIG)(lambda a: a[0:8, :])
            P.act(dt_, cb(12, p1=8), AF.Exp, bias=prmc(l, P_SDTB, 8))
            P.act(dt_, dt_, AF.Ln, bias=1.0)
            P.ts(dta_, dt_, V(negA[l][:], [("negA", l)]))
            for s in range(NSUB):
                sl = slice(s * 128, (s + 1) * 128)
                csl = slice(PADC + s * 128, PADC + (s + 1) * 128)
                ck = lambda j0, n: [("c", j0 + i) for i in range(n)]
                xv = V(cbuf[:, 0:4, csl], ck(0, 4))
                zv = V(cbuf[:, 8:12, csl], ck(8, 4))
                with Scope() as sc:
                    dtk = sc.take(TK)
                    with Scope() as s2:
                        p1 = s2.take(PQ)
                        P.tr(p1(lambda a: a[:, 0:8]), dt_(lambda a: a[:, sl]), C(C_IDN, 8, 0, 8))
                        P.tr(p1(lambda a: a[:, 8:16]), dta_(lambda a: a[:, sl]), C(C_IDN, 8, 0, 8))
                        P.cp(dtk, p1(lambda a: a[:, 0:16]))
                    dt_tok = dtk(lambda a: a[:, 0:8])
                    dta_tok = dtk(lambda a: a[:, 8:16])
                    cumT = sc.take(BIG)(lambda a: a[0:8, 0:128])
                    ex = sc.take(TK)
                    with Scope() as s2:
                        p2 = s2.take(PQ)
                        P.mm(p2(lambda a: a[0:8, :]), dta_tok, TRI)
                        P.cp(cumT, p2(lambda a: a[0:8, :]), eng="scalar")
                        p3 = s2.take(PQ)
                        P.mm(p3(lambda a: a[:, 0:8]), ONES, dta_tok)
                        P.mm(p3(lambda a: a[:, 8:16]), SGT, dta_tok)
                        P.act(ex, p3(lambda a: a[:, 0:16]), AF.Exp)
                    dec, erev = ex(lambda a: a[:, 0:8]), ex(lambda a: a[:, 8:16])
                    xdtv, xdtwv = sc.take(SMB), sc.take(SMB)
                    h8 = lambda v: v(lambda a: a.rearrange("p (h e) -> p h e", h=8))
                    with Scope() as s2:
                        pt = s2.take(PB)
                        for jc in range(4):
                            P.tr(pt(lambda a: a[:, jc * 128:(jc + 1) * 128]), cbs(jc, s), IDN)
                        P.tt(h8(xdtv), h8(pt), dt_tok(lambda a: a.unsqueeze(2).to_broadcast([128, 8, 64])), ALU.mult)
                    P.tt(h8(xdtwv), h8(xdtv), erev(lambda a: a.unsqueeze(2).to_broadcast([128, 8, 64])), ALU.mult)
                    bc_t = sc.take(SMB)
                    btok = bc_t(lambda a: a[:, 0:256].rearrange("p (g t) -> p g t", g=2))
                    cbm = bc_t(lambda a: a[:, 256:512].rearrange("p (g t) -> p g t", g=2))
                    with Scope() as s2:
                        pt = s2.take(PH)(lambda a: a.rearrange("p (g t) -> p g t", g=2))
                        pc = s2.take(PH)(lambda a: a.rearrange("p (g t) -> p g t", g=2))
                        for g in range(2):
                            P.tr(pt(lambda a: a[:, g, :]), cbs(4 + g, s), IDN)
                            P.mm(pc(lambda a: a[:, g, :]), cbs(4 + g, s), cbs(6 + g, s))
                        P.cp(btok, pt, eng="scalar")
                        P.tt(cbm, pc, TRI(lambda a: a.unsqueeze(1).to_broadcast([128, 2, 128])), ALU.mult)
                    pyb = sc.take(PB)
                    py4 = g4(pyb)
                    for g in range(2):
                        with Scope() as s2:
                            R = g4(s2.take(BIG))
                            P.tt(R, bc4(SGT), dta_tok(lambda a: a[:, 4 * g:4 * g + 4].unsqueeze(2).to_broadcast([128, 4, 128])), ALU.mult)
                            WT = g4(s2.take(SMB))
                            with Scope() as s3:
                                pg = g4(s3.take(PB))
                                for hl in range(4):
                                    P.mm(pg(lambda a: a[:, hl, :]), R(lambda a: a[:, hl, :]), TRI)
                                LT = g4(s3.take(BIG))
                                P.act(LT, pg, AF.Exp)
                                P.tt(WT, LT, cbm(lambda a: a[:, g:g + 1, :].to_broadcast([128, 4, 128])), ALU.mult)
                            Cw = g4(s2.take(SMB))
                            with Scope() as s3:
                                pbc = g4(s3.take(PB))
                                for hl in range(4):
                                    P.mm(pbc(lambda a: a[:, hl, :]), SEL(4 * g + hl, 8), cumT)
                                ec = g4(s3.take(BIG))
                                P.act(ec, pbc, AF.Exp)
                                P.tt(Cw, cbs(6 + g, s)(lambda a: a.unsqueeze(1).to_broadcast([128, 4, 128])), ec, ALU.mult)
                            for hl in range(4):
                                h = 4 * g + hl
                                jc, pb = h // 2, (h % 2) * 64
                                po = py4(lambda a: a[pb:pb + 64, jc, :])
                                P.mm(po, xdtv(lambda a: a[:, h * 64:(h + 1) * 64]), WT(lambda a: a[:, hl, :]), start=True, stop=False)
                                P.mm(po, Sb(lambda a: a[:, h * 64:(h + 1) * 64]), Cw(lambda a: a[:, hl, :]), start=False, stop=True)
                    with Scope() as s2:
                        y2 = g4(s2.take(BIG))
                        sq = g4(s2.take(BIG))
                        P.tt(y2, xv, V(prm[l][:, P_SDCH:P_SDCH + 4].unsqueeze(2).to_broadcast([128, 4, 128]), [("prm", l)]), ALU.mult)
                        P.tt(y2, y2, py4, ALU.add)
                        P.tt(y2, y2, zv, ALU.mult)
                        P.act(sq, y2, AF.Square)
                        pss = s2.take(PQ)
                        for jc in range(4):
                            P.mm(pss, ONES, sq(lambda a: a[:, jc, :]), start=(jc == 0), stop=(jc == 3))
                        rs = s2.take(BIG)(lambda a: a[:, 0:128])
                        P.act(rs, pss, AF.Ln, bias=EPS, scale=1.0 / 512)
                        P.act(rs, rs, AF.Exp, scale=-0.5)
                        P.tt(y2, y2, V(prm[l][:, P_SNG:P_SNG + 4].unsqueeze(2).to_broadcast([128, 4, 128]), [("prm", l)]), ALU.mult)
                        P.tt(V(yT[:, :, sl], [("y", k) for k in range(4)]), y2, bc4(rs), ALU.mult)
                    with Scope() as s2:
                        pi = s2.take(PB)
                        for g in range(2):
                            P.mm(pi(lambda a: a[:, g * 256:(g + 1) * 256]), btok(lambda a: a[:, g, :]),
                                 xdtwv(lambda a: a[:, g * 256:(g + 1) * 256]))
                        P.tt(h8(Sv), h8(Sv), dec(lambda a: a.unsqueeze(2).to_broadcast([128, 8, 64])), ALU.mult)
                        P.tt(Sv, Sv, pi, ALU.add)
                        P.cp(Sb, Sv, eng="scalar")

    def mlstm(l):
        isq = float(1.0 / np.sqrt(128.0))
        g4 = lambda v: v(lambda a: a.rearrange("p (g t) -> p g t", g=4))
        C4 = g4(V(Cml[l][:], [("Cml", l)]))
        C4b = g4(V(Cmlb[l][:], [("Cmlb", l)]))
        n4 = V(nml[l][:], [("nml", l)])
        n4b = V(nmlb[l][:], [("nmlb", l)])
        ONESb = V(onesb[:], ["onesb"])

        def bc4(v):
            return v(lambda a: a.unsqueeze(1).to_broadcast([128, 4, 128]))

        def bch(v):
            return v(lambda a: a.unsqueeze(2).to_broadcast([128, 4, 128]))

        def pbank(sc):
            return g4(sc.take(PB))

        for s in range(NSUB):
            sl = slice(s * 128, (s + 1) * 128)
            csl = slice(PADC + s * 128, PADC + (s + 1) * 128)
            ck4 = lambda j0: [("c", j0 + i) for i in range(4)]
            qv = V(cbuf[:, 0:4, csl], ck4(0))
            kv = V(cbuf[:, 4:8, csl], ck4(4))
            ov = V(cbuf[:, 8:12, csl], ck4(8))
            zv = V(cbuf[:, 12:16, csl], ck4(12))
            vv = V(cbuf[:, 16:20, csl], ck4(16))
            with Scope() as sc:
                tk = sc.take(TK)
                with Scope() as s2:
                    p0 = s2.take(PQ)
                    P.tr(p0(lambda a: a[:, 0:8]), cbs(20, s, 0, 8), C(C_IDN, 8, 0, 8))
                    P.cp(tk(lambda a: a[:, 0:4]), p0(lambda a: a[:, 0:4]))
                    P.act(tk(lambda a: a[:, 4:8]), p0(lambda a: a[:, 4:8]), AF.Exp, scale=-1.0)
                    P.act(tk(lambda a: a[:, 4:8]), tk(lambda a: a[:, 4:8]), AF.Ln, bias=1.0)
                li = tk(lambda a: a[:, 0:4])
                nl = tk(lambda a: a[:, 4:8])
                cumT = sc.take(BIG)(lambda a: a[0:4, 0:128])
                with Scope() as s2:
                    p1 = s2.take(PQ)
                    P.mm(p1(lambda a: a[0:4, :]), nl, TRI)
                    P.ts(cumT, p1(lambda a: a[0:4, :]), -1.0)
                    p2 = s2.take(PQ)
                    P.mm(p2(lambda a: a[:, 0:4]), ONES, nl)
                    P.mm(p2(lambda a: a[:, 4:8]), SGT, nl)
                    P.act(tk(lambda a: a[:, 8:12]), p2(lambda a: a[:, 0:4]), AF.Exp, scale=-1.0)
                    gt = s2.take(TK)
                    P.stt(gt(lambda a: a[:, 0:4]), p2(lambda a: a[:, 4:8]), -1.0, li, ALU.mult, ALU.add)
                    P.act(tk(lambda a: a[:, 12:16]), gt(lambda a: a[:, 0:4]), AF.Exp)
                dec, wend = tk(lambda a: a[:, 8:12]), tk(lambda a: a[:, 12:16])
                vt = g4(sc.take(SMB))
                with Scope() as s2:
                    pvt = pbank(s2)
                    for h in range(4):
                        P.tr(pvt(lambda a: a[:, h, :]), vv(lambda a: a[:, h, :]), IDN)
                    P.cp(vt, pvt, eng="scalar")
                ed = g4(sc.take(SMB))
                with Scope() as s2:
                    t_ = g4(s2.take(BIG))
                    rh = g4(s2.take(BIG))
                    P.stt(t_, bc4(SGT), -1.0, bch(nl), ALU.mult, ALU.mult)
                    P.tt(rh, bc4(IDN), bch(li), ALU.mult)
                    P.tt(rh, rh, t_, ALU.add)
                    pl = pbank(s2)
                    for h in range(4):
                        P.mm(pl(lambda a: a[:, h, :]), rh(lambda a: a[:, h, :]), TRI)
                    P.act(ed, pl, AF.Exp)
                wq_ = g4(sc.take(SMB))
                with Scope() as s2:
                    pk = pbank(s2)
                    for h in range(4):
                        P.mm(pk(lambda a: a[:, h, :]), kv(lambda a: a[:, h, :]), qv(lambda a: a[:, h, :]))
                    m1 = g4(s2.take(SMB))
                    P.stt(m1, pk, isq, bc4(TRI), ALU.mult, ALU.mult)
                    P.tt(wq_, m1, ed, ALU.mult)
                qw = g4(sc.take(SMB))
                with Scope() as s2:
                    pbc = pbank(s2)
                    for h in range(4):
                        P.mm(pbc(lambda a: a[:, h, :]), SEL(h, 4), cumT)
                    wi = g4(s2.take(BIG))
                    P.act(wi, pbc, AF.Exp)
                    P.tt(qw, qv, wi, ALU.mult)
                hh_ = g4(sc.take(BIG))
                with Scope() as s2:
                    da = g4(s2.take(BIG))
                    pd = pbank(s2)
                    for h in range(4):
                        P.mm(pd(lambda a: a[:, h, :]), ONESb, wq_(lambda a: a[:, h, :]), start=True, stop=False)
                        P.mm(pd(lambda a: a[:, h, :]), n4b(lambda a: a[:, h:h + 1].to_broadcast([128, 128])),
                             qw(lambda a: a[:, h, :]), start=False, stop=True)
                    P.act(da, pd, AF.Abs)
                    P.ts(da, da, 1.0, None, ALU.max)
                    P.recip(da, da)
                    pn = pbank(s2)
                    for h in range(4):
                        P.mm(pn(lambda a: a[:, h, :]), vt(lambda a: a[:, h, :]), wq_(lambda a: a[:, h, :]), start=True, stop=False)
                        P.mm(pn(lambda a: a[:, h, :]), C4b(lambda a: a[:, h, :]), qw(lambda a: a[:, h, :]), start=False, stop=True)
                    P.tt(hh_, pn, da, ALU.mult)
                P.tt(hh_, hh_, ov, ALU.mult)
                with Scope() as s2:
                    hc = g4(s2.take(BIG))
                    sq = g4(s2.take(BIG))
                    pm = pbank(s2)
                    for h in range(4):
                        P.mm(pm(lambda a: a[:, h, :]), ONES, hh_(lambda a: a[:, h, :]))
                    P.stt(hc, pm, -1.0 / 128, hh_, ALU.mult, ALU.add)
                    P.act(sq, hc, AF.Square)
                    pv = pbank(s2)
                    for h in range(4):
                        P.mm(pv(lambda a: a[:, h, :]), ONES, sq(lambda a: a[:, h, :]))
                    P.act(sq, pv, AF.Ln, bias=EPS, scale=1.0 / 128)
                    P.act(sq, sq, AF.Exp, scale=-0.5)
                    P.tt(hc, hc, sq, ALU.mult)
                    P.tt(hc, hc, V(prm[l][:, P_MNG:P_MNG + 4].unsqueeze(2).to_broadcast([128, 4, 128]), [("prm", l)]), ALU.mult)
                    P.tt(V(yT[:, :, sl], [("y", k) for k in range(4)]), hc, zv, ALU.mult)
                with Scope() as s2:
                    kw = g4(s2.take(SMB))
                    pt = pbank(s2)
                    for h in range(4):
                        P.tr(pt(lambda a: a[:, h, :]), kv(lambda a: a[:, h, :]), IDN)
                    P.stt(kw, pt, isq, bch(wend), ALU.mult, ALU.mult)
                    pi = pbank(s2)
                    pi2 = s2.take(PQ)
                    for h in range(4):
                        P.mm(pi(lambda a: a[:, h, :]), kw(lambda a: a[:, h, :]), vt(lambda a: a[:, h, :]))
                        P.mm(pi2(lambda a: a[:, h:h + 1]), kw(lambda a: a[:, h, :]), ONESb(lambda a: a[:, 0:1]))
                    P.tt(C4, C4, bch(dec), ALU.mult)
                    P.tt(C4, C4, pi, ALU.add)
                    P.cp(C4b, C4, eng="scalar")
                    P.tt(n4, n4, dec, ALU.mult)
                    P.tt(n4, n4, pi2(lambda a: a[:, 0:4]), ALU.add)
                    P.cp(n4b, n4, eng="scalar")

    def rwkv_pre(l):
        pass

    def rwkv(l):
        vfv = lambda jc: V(vfirst[:, jc, :], [("vfirst", jc)])
        for jc in range(4):
            if l == 0:
                P.cp(vfv(jc), cb(8 + jc), eng="scalar")
            else:
                with Scope() as sc:
                    pv = sc.take(PB)
                    P.mm(pv, V(ra2[l][0:32, 512 + jc * 128:512 + (jc + 1) * 128], ["ra2"]), cb(14, p1=32))
                    sg = sc.take(BIG)
                    P.act(sg, pv, AF.Sigmoid, bias=prmc(l, P_RV0 + jc))
                    d = sc.take(BIG)
                    P.tt(d, vfv(jc), cb(8 + jc), ALU.subtract)
                    P.tt(d, d, sg, ALU.mult)
                    P.tt(cb(8 + jc), cb(8 + jc), d, ALU.add)
        LD = 0.6065306597126334
        g4 = lambda v: v(lambda a: a.rearrange("p (g t) -> p g t", g=4))
        AtT, RtT, BtT, KtT, VTb = [g4(V(bd[q][:], [("bd", q)])) for q in range(5)]
        IDNb = V(idnb[:], ["idnb"])
        Tv = g4(V(Trw[l][:], [("Trw", l)]))
        Tvb = g4(V(Trwb[l][:], [("Trwb", l)]))

        def bc4(v):
            return v(lambda a: a.unsqueeze(1).to_broadcast([128, 4, 128]))

        def prmb(col):
            return V(prm[l][:, col:col + 4].unsqueeze(2).to_broadcast([128, 4, 128]), [("prm", l)])

        def pbank(sc, bf=False):
            pp = sc.take(PB)
            if bf:
                return pp(lambda a: a.bitcast(BF16)[:, 0:512].rearrange("p (g t) -> p g t", g=4))
            return g4(pp)

        for s in range(NSUB):
            sl = slice(s * 128, (s + 1) * 128)
            csl = slice(PADC + s * 128, PADC + (s + 1) * 128)
            ck4 = lambda j0: [("c", j0 + i) for i in range(4)]
            rview = V(cbuf[:, 0:4, csl], ck4(0))
            kview = V(cbuf[:, 4:8, csl], ck4(4))
            vview = V(cbuf[:, 8:12, csl], ck4(8))
            zview = V(cbuf[:, 15:19, csl], ck4(15))
            with Scope() as so:
              yr = g4(so.take(BIG))
              with Scope() as sc:
                sgv = sc.take(BIG)
                with Scope() as s2:
                    pw = s2.take(PB)
                    P.mm(pw, cbs(12, s, 0, 65), V(rw2[l][:], ["rw2"]))
                    P.act(sgv, pw, AF.Sigmoid)
                a_ = g4(sc.take(BIG))
                kk = g4(sc.take(BIG))
                with Scope() as s2:
                    pa = pbank(s2)
                    for jc in range(4):
                        P.mm(pa(lambda a: a[:, jc, :]), V(ra2[l][0:65, jc * 128:(jc + 1) * 128], ["ra2"]), cbs(13, s, 0, 65))
                    P.act(a_, pa, AF.Sigmoid)
                P.tt(kk, kview, prmb(P_RKK), ALU.mult)
                with Scope() as s2:
                    sq = g4(s2.take(BIG))
                    P.act(sq, kk, AF.Square)
                    pn = pbank(s2)
                    for jc in range(4):
                        P.mm(pn(lambda a: a[:, jc, :]), BONES, sq(lambda a: a[:, jc, :]))
                    P.ts(sq, pn, 1e-24, None, ALU.max)
                    P.act(sq, sq, AF.Ln)
                    P.act(sq, sq, AF.Exp, scale=-0.5)
                    P.tt(kk, kk, sq, ALU.mult)
                    P.ts(sq, a_, -1.0, None, ALU.add)
                    P.tt(sq, sq, prmb(P_RKA), ALU.mult)
                    P.stt(kview, sq, 1.0, kview, ALU.add, ALU.mult)
                P.tt(a_, kk, a_, ALU.mult)
                Wt = g4(sc.take(BIG))
                iW, Wx = g4(sc.take(SMB)), g4(sc.take(SMB))
                with Scope() as s2:
                    pc1, pc2 = pbank(s2), pbank(s2)
                    for jc in range(4):
                        P.mm(pc1(lambda a: a[:, jc, :]), sgv(lambda a: a[:, jc * 128:(jc + 1) * 128]), MU64)
                        P.mm(pc2(lambda a: a[:, jc, :]), sgv(lambda a: a[:, jc * 128:(jc + 1) * 128]), MSU64)
                    P.act(Wt, pc1, AF.Exp, scale=-LD)
                    P.act(iW, pc1, AF.Exp, scale=LD)
                    P.act(Wx, pc2, AF.Exp, scale=-LD)
                for cc in range(2):
                    wsl = slice(cc * 64, (cc + 1) * 64)
                    with Scope() as sk:
                        for hh in range(2):
                            pr = slice(hh * 64, (hh + 1) * 64)
                            src = lambda v: v(lambda a: a[pr, :, wsl])
                            dst = lambda v: v(lambda a: a[pr, :, hh * 64:(hh + 1) * 64])
                            P.stt(dst(AtT), src(kk), -1.0, src(Wx), ALU.mult, ALU.mult)
                            P.tt(dst(RtT), src(rview), src(Wt), ALU.mult)
                            P.tt(dst(BtT), src(a_), src(iW), ALU.mult)
                            P.tt(dst(KtT), src(kview), src(iW), ALU.mult)
                            P.cp(dst(VTb), src(vview), eng="scalar")

                        def trans(srcv):
                            o = g4(sk.take(SMB))
                            with Scope() as s2:
                                pt = pbank(s2, bf=True)
                                for jc in range(4):
                                    P.tr(pt(lambda a: a[:, jc, :]), srcv(lambda a: a[:, jc, :]), IDNb)
                                P.cp(o, pt, eng=evac_eng())
                            return o
                        Vb, Btk, Ktk = trans(VTb), trans(BtT), trans(KtT)

                        def mm4(pp, lhsT, rhs, **kw):
                            for jc in range(4):
                                P.mm(pp(lambda a: a[:, jc, :]), lhsT(lambda a: a[:, jc, :]), rhs(lambda a: a[:, jc, :]), **kw)

                        def acc4(pp, terms):
                            for jc in range(4):
                                for ti, (l_, r_) in enumerate(terms):
                                    P.mm(pp(lambda a: a[:, jc, :]), l_(lambda a: a[:, jc, :]), r_(lambda a: a[:, jc, :]),
                                         start=(ti == 0), stop=(ti == len(terms) - 1))

                        inv = Scope()

                        def mmask(lhsT, rhs, mask, own=None):
                            o = g4((own or sk).take(SMB))
                            with Scope() as s2:
                                pp = pbank(s2)
                                mm4(pp, lhsT, rhs)
                                P.tt(o, pp, bc4(mask), ALU.mult)
                            return o
                        X = mmask(BtT, AtT, MSU64, own=inv)
                        Xt = mmask(AtT, BtT, MSL64, own=inv)
                        AakT = mmask(KtT, AtT, MSU64)
                        RbT = mmask(BtT, RtT, MU64)
                        RkT = mmask(KtT, RtT, MU64)
                        N = g4(inv.take(SMB))
                        P.tt(N, bc4(IDN), X, ALU.add)
                        Pm, Pt = X, Xt
                        for k in range(1, 6):
                            nxt = Scope()
                            with Scope() as s2:
                                pp = pbank(s2)
                                mm4(pp, Pt, Pm)
                                P2 = g4(nxt.take(SMB))
                                P.cp(P2, pp, eng=evac_eng())
                                pp2 = pbank(s2)
                                mm4(pp2, Pm, Pt)
                                Pt2 = g4(nxt.take(SMB))
                                P.cp(Pt2, pp2, eng=evac_eng())
                                pn_ = pbank(s2)
                                mm4(pn_, Pt2, N)
                                N2 = g4(nxt.take(SMB))
                                P.tt(N2, N, pn_, ALU.add)
                            inv.__exit__()
                            inv = nxt
                            Pm, Pt, N = P2, Pt2, N2
                        with Scope() as s2:
                            pr0 = pbank(s2)
                            acc4(pr0, [(AtT, Tvb), (AakT, Vb)])
                            R0 = g4(sk.take(SMB))
                            P.cp(R0, pr0, eng="scalar")
                            pu = pbank(s2)
                            mm4(pu, N, R0)
                            U = g4(sk.take(SMB))
                            P.cp(U, pu, eng="scalar")
                        inv.__exit__()
                        with Scope() as s2:
                            py = pbank(s2)
                            acc4(py, [(Tvb, RtT), (U, RbT), (Vb, RkT)])
                            for hh in range(2):
                                pr = slice(hh * 64, (hh + 1) * 64)
                                P.cp(yr(lambda a: a[pr, :, wsl]), py(lambda a: a[pr, :, hh * 64:(hh + 1) * 64]), eng="scalar")
                            pS = pbank(s2)
                            acc4(pS, [(Btk, U), (Ktk, Vb)])
                            P.tt(Tv, Tv, pS, ALU.add)
                            P.tt(Tv, Tv, Wt(lambda a: a[:, :, cc * 64 + 63:cc * 64 + 64].to_broadcast([128, 4, 128])), ALU.mult)
                            P.cp(Tvb, Tv, eng="scalar")
              if True:
                with Scope() as s2:
                    pm = pbank(s2)
                    for jc in range(4):
                        P.mm(pm(lambda a: a[:, jc, :]), BONES, yr(lambda a: a[:, jc, :]))
                    hc = g4(s2.take(BIG))
                    P.stt(hc, pm, -1.0 / 64, yr, ALU.mult, ALU.add)
                    sq = g4(s2.take(BIG))
                    P.act(sq, hc, AF.Square)
                    pv = pbank(s2)
                    for jc in range(4):
                        P.mm(pv(lambda a: a[:, jc, :]), BONES, sq(lambda a: a[:, jc, :]))
                    P.act(sq, pv, AF.Ln, bias=64e-5, scale=1.0 / 64)
                    P.act(sq, sq, AF.Exp, scale=-0.5)
                    P.tt(hc, hc, sq, ALU.mult)
                    P.tt(hc, hc, prmb(P_RLNG), ALU.mult)
                    P.tt(hc, hc, prmb(P_RLNB), ALU.add)
                    P.tt(sq, rview, prmb(P_RRK), ALU.mult)
                    P.tt(sq, sq, kview, ALU.mult)
                    pb_ = pbank(s2)
                    for jc in range(4):
                        P.mm(pb_(lambda a: a[:, jc, :]), BONES, sq(lambda a: a[:, jc, :]))
                    P.tt(sq, pb_, vview, ALU.mult)
                    P.tt(hc, hc, sq, ALU.add)
                    P.tt(V(yT[:, :, sl], [("y", k) for k in range(4)]), hc, zview, ALU.mult)

    MIX = {"gla": gla, "rwkv": rwkv, "ssd": ssd, "mlstm": mlstm}

    def finish():
        fin = {}
        for key, cnt in P.dma_cnt.items():
            if key[1].startswith("out_") or key[1].startswith("dbg_"):
                fin[key] = cnt
        P._emit_waits("gpsimd", fin)
        P.emit()
        P.close()
        for cm in reversed(cms):
            cm.__exit__(None, None, None)
        return nc, P

    if upto == 'init':
        return finish()
    for tile in range(n_tiles):
        tok0 = tile * T
        for s in range(NSUB):
            for jq in range(4):
                with Scope() as sc:
                    xi = sc.take(BIG)
                    r0 = tok0 + s * 128
                    P.dma("gpsimd", "xin_%s" % str(xi.keys[0][1]), xi.ap, x_d[r0:r0 + 128, jq * 512:(jq + 1) * 512],
                          writes=xi.keys)
                    pp = sc.take(PB)
                    for q in range(4):
                        P.tr(pp(lambda a: a[:, q * 128:(q + 1) * 128]), xi(lambda a: a[:, q * 128:(q + 1) * 128]), IDN)
                    P.cp(V(xT[:, jq * 4:(jq + 1) * 4, s * 128:(s + 1) * 128], [("xT", jq * 4 + q) for q in range(4)]),
                         pp(lambda a: a.rearrange("p (q t) -> p q t", q=4)), eng=evac_eng())
        if upto == 'xload':
            return finish()
        for l in range(depth):
            P.dma("gpsimd", "gkw", gkw1[:], gkw_d[l], writes=["gkw"])
            P.dma("gpsimd", "rw2", rw21[:], rw2_d[l], writes=["rw2"])
            P.dma("gpsimd", "ra2", ra21[:], ra2_d[l], writes=["ra2"])
            rmsnorm(P_NG, l, lambda j: V(hT[:, j, :], [("hT", j)]))
            for m in mixers:
                if m == "gla":
                    gla_pre(l)
                inproj(l, m)
                if upto == 'inproj':
                    return finish()
                MIX[m](l)
                if debug:
                    mi = MIXN.index(m)
                    for k in range(4):
                        with Scope() as sc:
                            yk = sc.take(BIG)
                            P.cp(yk, V(yT[:, k, :], [("y", k)]))
                            P.dma("gpsimd", "dbg_%s" % str(yk.keys[0][1]),
                                  dbg_d[l, mi * 512 + k * 128:mi * 512 + (k + 1) * 128, tok0:tok0 + T],
                                  yk.ap, reads=yk.keys, writes=["dbgout"])
                outproj(l, m)
        if upto == 'norm':
            return finish()
        rmsnorm(P_FNG, depth - 1, lambda j: cb(j))
        if upto == 'fnorm':
            return finish()
        for s in range(NSUB):
            for jq in range(4):
                with Scope() as sc:
                    pp = sc.take(PB)
                    for q in range(4):
                        P.tr(pp(lambda a: a[:, q * 128:(q + 1) * 128]), cbs(jq * 4 + q, s), IDN)
                    xo = sc.take(BIG)
                    P.cp(xo, pp, eng=evac_eng())
                    r0 = tok0 + s * 128
                    P.dma("gpsimd", "out_%s" % str(xo.keys[0][1]), out_d[r0:r0 + 128, jq * 512:(jq + 1) * 512], xo.ap,
                          reads=xo.keys, writes=["outd"])
    return finish()


def make_in_maps(inputs, batch_of_core):
    cst = make_consts()
    shared = {"cst": cst}
    for l in range(2):
        g = lambda k: np.ascontiguousarray(np.asarray(inputs["%s_%d" % (k, l)], np.float32))
        shared["win%d" % l] = g("w_in")
        shared["wout%d" % l] = g("w_out")
        shared["prm%d" % l] = pack_params(inputs, l)
        shared["gkw%d" % l] = np.concatenate([g("gla_gk_w2"), g("gla_gk_b")[None, :]], axis=0)
        shared["rw2%d" % l] = np.concatenate([g("rwkv_w2"), g("rwkv_w0")[None, :]], axis=0)
        shared["ra2%d" % l] = np.concatenate([np.concatenate([g("rwkv_a2"), g("rwkv_a0")[None, :]], axis=0), np.concatenate([g("rwkv_v2") if l > 0 else np.zeros((32, 512), np.float32), np.zeros((33, 512), np.float32)], axis=0)], axis=1)
    x = np.asarray(inputs["x"], np.float32)
    maps = []
    for b in batch_of_core:
        d = dict(shared)
        d["x"] = np.ascontiguousarray(x[b])
        maps.append(d)
    return maps


_INPUT_NAMES = (
    'x',
    'norm_g_0',
    'w_in_0',
    'w_out_0',
    'gla_gk_w2_0',
    'gla_gk_b_0',
    'gla_norm_g_0',
    'rwkv_mu_0',
    'rwkv_w0_0',
    'rwkv_w2_0',
    'rwkv_a0_0',
    'rwkv_a2_0',
    'rwkv_k_k_0',
    'rwkv_k_a_0',
    'rwkv_r_k_0',
    'rwkv_ln_g_0',
    'rwkv_ln_b_0',
    'ssd_conv_w_0',
    'ssd_conv_b_0',
    'ssd_dt_bias_0',
    'ssd_a_log_0',
    'ssd_d_0',
    'ssd_norm_g_0',
    'mlstm_conv_w_0',
    'mlstm_conv_b_0',
    'mlstm_ig_b_0',
    'mlstm_fg_b_0',
    'mlstm_norm_g_0',
    'norm_g_1',
    'w_in_1',
    'w_out_1',
    'gla_gk_w2_1',
    'gla_gk_b_1',
    'gla_norm_g_1',
    'rwkv_mu_1',
    'rwkv_w0_1',
    'rwkv_w2_1',
    'rwkv_a0_1',
    'rwkv_a2_1',
    'rwkv_v0_1',
    'rwkv_v2_1',
    'rwkv_k_k_1',
    'rwkv_k_a_1',
    'rwkv_r_k_1',
    'rwkv_ln_g_1',
    'rwkv_ln_b_1',
    'ssd_conv_w_1',
    'ssd_conv_b_1',
    'ssd_dt_bias_1',
    'ssd_a_log_1',
    'ssd_d_1',
    'ssd_norm_g_1',
    'mlstm_conv_w_1',
    'mlstm_conv_b_1',
    'mlstm_ig_b_1',
    'mlstm_fg_b_1',
    'mlstm_norm_g_1',
    'final_norm_g',
)


_CACHE = {}


def kernel(**inputs):
    inputs = {n: inputs[n] for n in _INPUT_NAMES}
    if "nc" not in _CACHE:
        _CACHE["nc"] = build_program()[0]
    nc = _CACHE["nc"]
    batch_of_core = [i // 2 for i in range(8)]
    maps = make_in_maps(inputs, batch_of_core)
    res = run_bass_kernel_spmd(nc, maps, core_ids=list(range(8)))
    out = np.stack([res.results[2 * b]["out"] for b in range(4)], axis=0)
    return out.astype(np.float32)
```

```python
import numpy as np
import concourse.bass as bass
import concourse.mybir as mybir
from concourse.bass_utils import run_bass_kernel_spmd

F32 = mybir.dt.float32
F32R = mybir.dt.float32r
BF16 = mybir.dt.bfloat16
AF = mybir.ActivationFunctionType
ALU = mybir.AluOpType

D = 2048
SEQ = 2048
T = 512
NSUB = T // 128
PADC = 4
CW = T + PADC
GW = 512
NIN = (7840, 7872)
EPS = 1e-6


def in_layout(l):
    rs = 3 * GW + 64 + 64 + (32 if l > 0 else 0)
    return (('gla_q', 256), ('gla_k', 256), ('gla_v', 512), ('gla_gk', 16), ('gla_z', 512),
            ('rwkv_shift', rs), ('rwkv_z', 512), ('ssd_xbc', 1024), ('ssd_dt', 8), ('ssd_z', 512),
            ('mlstm_qk', 1024), ('mlstm_v', 512), ('mlstm_i', 4), ('mlstm_f', 4), ('mlstm_o', 512),
            ('mlstm_z', 512))


def col_offsets(l):
    off = {}
    c = 0
    for n, w in in_layout(l):
        off[n] = c
        c += w
    return off


_o = [0]


def _al(n):
    r = _o[0]
    _o[0] += n
    return r


P_NG = _al(16); P_GLAG = _al(4); P_RMU = _al(15); P_RA0 = _al(4); P_RKK = _al(4); P_RKA = _al(4); P_RRK = _al(4)
P_RLNG = _al(4); P_RLNB = _al(4); P_RV0 = _al(4); P_SCW = _al(32); P_SCB = _al(8); P_SDTB = _al(1); P_SALOG = _al(1)
P_SDCH = _al(4); P_SNG = _al(4); P_MCW = _al(32); P_MCB = _al(8); P_MIFB = _al(1); P_MNG = _al(4); P_FNG = _al(16)
NPRM = _o[0]
C_IDN, C_ONES, C_TRI, C_SGT, C_MU64, C_MSU64, C_MSL64, C_BONES, NCST = 0, 128, 256, 384, 512, 640, 768, 896, 1024


def cols128(v):
    v = np.asarray(v, np.float32).reshape(-1)
    n = (len(v) + 127) // 128
    o = np.zeros((n * 128,), np.float32)
    o[:len(v)] = v
    return o.reshape(n, 128).T


def pack_params(inp, l):
    g = lambda k: np.asarray(inp["%s_%d" % (k, l)], np.float32)
    prm = np.zeros((128, NPRM), np.float32)
    prm[:, P_NG:P_NG + 16] = cols128(g("norm_g"))
    prm[:, P_GLAG:P_GLAG + 4] = cols128(g("gla_norm_g"))
    mu = g("rwkv_mu")
    prm[:, P_RMU:P_RMU + 12] = cols128(mu[:1536])
    prm[:64, P_RMU + 12] = mu[1536:1600]
    prm[:64, P_RMU + 13] = mu[1600:1664]
    if l > 0:
        prm[:32, P_RMU + 14] = mu[1664:1696]
        prm[:, P_RV0:P_RV0 + 4] = cols128(g("rwkv_v0"))
    prm[:, P_RA0:P_RA0 + 4] = cols128(g("rwkv_a0"))
    prm[:, P_RKK:P_RKK + 4] = cols128(g("rwkv_k_k"))
    prm[:, P_RKA:P_RKA + 4] = cols128(g("rwkv_k_a"))
    prm[:, P_RRK:P_RRK + 4] = cols128(g("rwkv_r_k"))
    prm[:, P_RLNG:P_RLNG + 4] = cols128(g("rwkv_ln_g"))
    prm[:, P_RLNB:P_RLNB + 4] = cols128(g("rwkv_ln_b"))
    cw = g("ssd_conv_w")
    for j in range(4):
        prm[:, P_SCW + j * 8:P_SCW + j * 8 + 8] = cols128(cw[j])
    prm[:, P_SCB:P_SCB + 8] = cols128(g("ssd_conv_b"))
    prm[:8, P_SDTB] = g("ssd_dt_bias")
    prm[:8, P_SALOG] = g("ssd_a_log")
    prm[:, P_SDCH:P_SDCH + 4] = cols128(np.repeat(g("ssd_d"), 64))
    prm[:, P_SNG:P_SNG + 4] = cols128(g("ssd_norm_g"))
    cw = g("mlstm_conv_w")
    for j in range(4):
        prm[:, P_MCW + j * 8:P_MCW + j * 8 + 8] = cols128(cw[j])
    prm[:, P_MCB:P_MCB + 8] = cols128(g("mlstm_conv_b"))
    prm[:4, P_MIFB] = g("mlstm_ig_b")
    prm[4:8, P_MIFB] = g("mlstm_fg_b")
    prm[:, P_MNG:P_MNG + 4] = cols128(g("mlstm_norm_g"))
    prm[:, P_FNG:P_FNG + 16] = cols128(np.asarray(inp["final_norm_g"], np.float32))
    return prm


def make_consts():
    c = np.zeros((128, NCST), np.float32)
    i = np.arange(128)
    r, cc = i[:, None], i[None, :]
    same = (r // 64) == (cc // 64)
    c[:, C_IDN:C_IDN + 128] = (r == cc)
    c[:, C_ONES:C_ONES + 128] = 1.0
    c[:, C_TRI:C_TRI + 128] = (r <= cc)
    c[:, C_SGT:C_SGT + 128] = (r > cc)
    c[:, C_MU64:C_MU64 + 128] = same & (r <= cc)
    c[:, C_MSU64:C_MSU64 + 128] = same & (r < cc)
    c[:, C_MSL64:C_MSL64 + 128] = same & (r > cc)
    c[:, C_BONES:C_BONES + 128] = same
    return c


class V:
    def __init__(self, ap, keys, excl=False):
        self.ap = ap
        self.keys = tuple(keys)
        self.excl = excl

    def __call__(self, f):
        v = V(f(self.ap), self.keys, self.excl)
        if hasattr(self, "base"):
            v.base = self.base
        return v


class Prog:
    ENGS = ("sync", "scalar", "vector", "gpsimd", "tensor")

    def __init__(self, nc):
        self.nc = nc
        self.ops = {e: [] for e in self.ENGS}
        self.count = {e: 0 for e in self.ENGS}
        self.waited = {e: {} for e in self.ENGS}
        self.lastw = {}
        self.readers = {}
        self.dma_cnt = {}
        self.sems = {}
        self.ctx = []
        self.nops = 0
        self.limit = None
        self.log = []

    def sem(self, key):
        if key not in self.sems:
            nm = "s_" + "".join(ch for ch in str(key) if ch.isalnum())
            cm = self.nc.semaphore(nm)
            self.sems[key] = cm.__enter__()
            self.ctx.append(cm)
        return self.sems[key]

    def _deps(self, reads, writes):
        deps = {}

        def add(d):
            if d is not None and deps.get(d[0], 0) < d[1]:
                deps[d[0]] = d[1]
        for k in reads:
            add(self.lastw.get(k))
        for k in writes:
            add(self.lastw.get(k))
            for r in self.readers.get(k, ()):
                add(r)
        return deps

    def _emit_waits(self, eng, deps):
        w = self.waited[eng]
        for k, v in deps.items():
            if eng == "tensor" and k == "tensor":
                continue
            if w.get(k, 0) >= v:
                continue
            w[k] = v
            s = self.sem(k)
            self.ops[eng].append(lambda e, s=s, v=v: e.wait_ge(s, v))

    def _commit(self, reads, writes, tok):
        for k in reads:
            self.readers.setdefault(k, []).append(tok)
        for k in writes:
            self.lastw[k] = tok
            self.readers[k] = []

    def op(self, eng, fn, reads=(), writes=()):
        if self.limit is not None and self.nops >= self.limit:
            return
        self._emit_waits(eng, self._deps(reads, writes))
        self.count[eng] += 1
        tok = (eng, self.count[eng])
        s = self.sem(eng)
        self.ops[eng].append(lambda e, fn=fn, s=s: fn(e).then_inc(s, 1))
        self._commit(reads, writes, tok)
        self.nops += 1
        if self.limit is not None:
            import sys as _s
            f = _s._getframe(1)
            ln = []
            while f is not None and len(ln) < 3:
                ln.append(f.f_lineno)
                f = f.f_back
            self.log.append((self.nops, eng, ln))

    def dma(self, eng, slot, out, in_, reads=(), writes=(), **kw):
        if self.limit is not None and self.nops >= self.limit:
            return
        self._emit_waits(eng, self._deps(reads, writes))
        key = ("dma", slot)
        self.dma_cnt[key] = self.dma_cnt.get(key, 0) + 16
        tok = (key, self.dma_cnt[key])
        s = self.sem(key)
        self.ops[eng].append(lambda e, s=s: e.dma_start(out=out, in_=in_, **kw).then_inc(s, 16))
        self._commit(reads, writes, tok)

    def wait_all(self, eng, keys):
        deps = {}
        for k in keys:
            d = self.lastw.get(k)
            if d is not None and deps.get(d[0], 0) < d[1]:
                deps[d[0]] = d[1]
        self._emit_waits(eng, deps)

    def emit(self):
        with self.nc.Block() as block:
            def mk(name):
                def body(e):
                    for f in self.ops[name]:
                        f(e)
                return body
            block.sync(mk("sync"))
            block.scalar(mk("scalar"))
            block.vector(mk("vector"))
            block.gpsimd(mk("gpsimd"))
            block.tensor(mk("tensor"))

    def close(self):
        for cm in reversed(self.ctx):
            cm.__exit__(None, None, None)

    @staticmethod
    def _rw(out, ins):
        r, w = (), out.keys
        for v in ins:
            if isinstance(v, V):
                if v.excl:
                    w = w + v.keys
                else:
                    r = r + v.keys
        return r, w

    @staticmethod
    def _a(x):
        return x.ap if isinstance(x, V) else x

    def mm(self, out, lhsT, rhs, start=True, stop=True):
        r, w = self._rw(out, (lhsT, rhs))
        self.op("tensor", lambda e: e.matmul(out.ap, lhsT.ap, rhs.ap, start=start, stop=stop), reads=r, writes=w)

    def tr(self, out, in_, idn):
        r, w = self._rw(out, (in_, idn))
        self.op("tensor", lambda e: e.transpose(out.ap, in_.ap, idn.ap), reads=r, writes=w)

    def act(self, out, in_, func, bias=None, scale=None):
        kw = {}
        if bias is not None:
            kw["bias"] = self._a(bias)
        if scale is not None:
            kw["scale"] = scale
        r, w = self._rw(out, (in_, bias))
        self.op("scalar", lambda e: e.activation(out.ap, in_.ap, func, **kw), reads=r, writes=w)

    def tt(self, out, a, b, op, eng="vector"):
        r, w = self._rw(out, (a, b))
        self.op(eng, lambda e: e.tensor_tensor(out.ap, a.ap, b.ap, op), reads=r, writes=w)

    def ts(self, out, a, s1, s2=None, op0=ALU.mult, op1=None, eng="vector"):
        r, w = self._rw(out, (a, s1, s2))
        s1, s2 = self._a(s1), self._a(s2)
        if op1 is None:
            self.op(eng, lambda e: e.tensor_scalar(out.ap, a.ap, s1, None, op0), reads=r, writes=w)
        else:
            self.op(eng, lambda e: e.tensor_scalar(out.ap, a.ap, s1, s2, op0, op1), reads=r, writes=w)

    def stt(self, out, a, s, b, op0, op1, eng="vector"):
        r, w = self._rw(out, (a, s, b))
        s = self._a(s)
        self.op(eng, lambda e: e.scalar_tensor_tensor(out.ap, a.ap, s, b.ap, op0, op1), reads=r, writes=w)

    def cp(self, out, a, eng="vector"):
        r, w = self._rw(out, (a,))
        if eng == "scalar":
            self.op("scalar", lambda e: e.copy(out.ap, a.ap), reads=r, writes=w)
        else:
            self.op(eng, lambda e: e.tensor_copy(out.ap, a.ap), reads=r, writes=w)

    def recip(self, out, a):
        r, w = self._rw(out, (a,))
        self.op("vector", lambda e: e.reciprocal(out.ap, a.ap), reads=r, writes=w)

    def memset(self, out, val, eng="vector"):
        self.op(eng, lambda e: e.memset(out.ap, val), writes=out.keys)


class Pool:
    def __init__(self, name, views):
        self.name = name
        self.free = list(views)

    def get(self):
        if not self.free:
            raise RuntimeError("pool %s exhausted" % self.name)
        return self.free.pop(0)

    def put(self, v):
        self.free.append(v)


class DerivedPool:
    def __init__(self, base, fn):
        self.base = base
        self.fn = fn
        self.name = base.name

    def get(self):
        b = self.base.get()
        v = V(self.fn(b.ap), b.keys, b.excl)
        v.base = b
        return v

    def put(self, v):
        self.base.put(v.base)


class Scope:
    def __init__(self):
        self.items = []

    def take(self, pool):
        v = pool.get()
        self.items.append((pool, v))
        return v

    def __enter__(self):
        return self

    def __exit__(self, *a):
        for pool, v in self.items:
            pool.put(v)
        self.items = []
        return False


MIXN = ("gla", "rwkv", "ssd", "mlstm")
NCH = 21


def build_program(n_tiles=4, depth=2, mixers=MIXN, debug=False, upto='all', limit=None):
    nc = bass.Bass("TRN2", target_bir_lowering=False)
    nc.dge_precook = False
    x_d = nc.dram_tensor("x", [SEQ, D], F32, kind="ExternalInput").ap()
    cst_d = nc.dram_tensor("cst", [128, NCST], F32, kind="ExternalInput").ap()
    win_d, wout_d, prm_d, gkw_d, rw2_d, ra2_d = [], [], [], [], [], []
    for l in range(2):
        win_d.append(nc.dram_tensor("win%d" % l, [D, NIN[l]], F32R, kind="ExternalInput").ap())
        wout_d.append(nc.dram_tensor("wout%d" % l, [D, D], F32R, kind="ExternalInput").ap())
        prm_d.append(nc.dram_tensor("prm%d" % l, [128, NPRM], F32, kind="ExternalInput").ap())
        gkw_d.append(nc.dram_tensor("gkw%d" % l, [17, 256], F32, kind="ExternalInput").ap())
        rw2_d.append(nc.dram_tensor("rw2%d" % l, [65, 512], F32, kind="ExternalInput").ap())
        ra2_d.append(nc.dram_tensor("ra2%d" % l, [65, 1024], F32, kind="ExternalInput").ap())
    out_d = nc.dram_tensor("out", [SEQ, D], F32, kind="ExternalOutput").ap()
    dbg_d = nc.dram_tensor("dbg", [2, D, SEQ], F32, kind="ExternalOutput").ap() if debug else None

    P = Prog(nc)
    P.limit = limit
    cms = []

    def sb(name, shape, dt=F32):
        cm = nc.sbuf_tensor("sb_" + name, shape, dt)
        cms.append(cm)
        return cm.__enter__()

    def ps(name, shape, dt=F32):
        cm = nc.psum_tensor("ps_" + name, shape, dt)
        cms.append(cm)
        return cm.__enter__()

    cst = sb("cst", [128, NCST])
    prm = [sb("prm%d" % l, [128, NPRM]) for l in range(2)]
    gkw1 = sb("gkw", [17, 256])
    rw21 = sb("rw2", [65, 512])
    ra21 = sb("ra2", [65, 1024])
    gkw, rw2, ra2 = [gkw1, gkw1], [rw21, rw21], [ra21, ra21]
    xT = sb("xT", [128, 16, T])
    hT = sb("hT", [128, 16, T], F32R)
    NW = 4
    WB = 256
    wbuf = [sb("wbuf%d" % i, [128, 4 * WB], F32R) for i in range(NW)]
    cbuf = sb("cbuf", [128, NCH, CW])
    yT = sb("yT", [128, 4, T], F32R)
    vfirst = sb("vfirst", [128, 4, T])
    bd = [sb("bd%d" % q, [128, 512], BF16) for q in range(5)]
    idnb = sb("idnb", [128, 128], BF16)
    negA = [sb("negA%d" % l, [8, 1]) for l in range(2)]
    Sgla = [sb("Sgla%d" % l, [128, 256]) for l in range(2)]
    Sssd = [sb("Sssd%d" % l, [128, 512]) for l in range(2)]
    Sssdb = [sb("Sssdb%d" % l, [128, 512], BF16) for l in range(2)]
    Cml = [sb("Cml%d" % l, [128, 512]) for l in range(2)]
    Cmlb = [sb("Cmlb%d" % l, [128, 512], BF16) for l in range(2)]
    nml = [sb("nml%d" % l, [128, 4]) for l in range(2)]
    nmlb = [sb("nmlb%d" % l, [128, 4], BF16) for l in range(2)]
    onesb = sb("onesb", [128, 128], BF16)
    Trw = [sb("Trw%d" % l, [128, 512]) for l in range(2)]
    Trwb = [sb("Trwb%d" % l, [128, 512], BF16) for l in range(2)]
    car_s = [sb("car_s%d" % l, [128, 8, 4]) for l in range(2)]
    car_m = [sb("car_m%d" % l, [128, 8, 4]) for l in range(2)]
    car_r = [sb("car_r%d" % l, [128, 15, 4]) for l in range(2)]
    NBIG, NSM, NU = 6, 24, 16
    arena = sb("arena", [128, NU * 512], BF16)
    BIG = Pool("big", [V(sb("big%d" % i, [128, T])[:], [("big", i)]) for i in range(NBIG)])
    SM = Pool("sm", [V(arena[:, (i // 2) * 512 + (i % 2) * 256:(i // 2) * 512 + (i % 2) * 256 + 256].bitcast(F32),
                       [("ar", i // 2)]) for i in range(NSM)])
    SMB = Pool("smb", [V(arena[:, u * 512:(u + 1) * 512], [("ar", u)]) for u in range(NU)])
    TK = Pool("tok8", [V(sb("tok8_%d" % i, [128, 16])[:], [("tok8", i)]) for i in range(6)])
    pb_t = [ps("pb%d" % i, [128, 512]) for i in range(8)]
    PB = Pool("pb", [V(pb_t[i][:], [("pb", i)], excl=True) for i in range(8)])
    PH = DerivedPool(PB, lambda a: a[:, 0:256])
    PQ = DerivedPool(PB, lambda a: a[:, 0:128])

    st = {"ev": 0}

    def evac_eng():
        st["ev"] += 1
        return "scalar" if st["ev"] % 2 else "vector"

    def C(off, n=128, p0=0, p1=128):
        return V(cst[p0:p1, off:off + n], ["cst"])

    IDN, ONES, TRI, SGT = C(C_IDN), C(C_ONES), C(C_TRI), C(C_SGT)
    MU64, MSU64, MSL64, BONES = C(C_MU64), C(C_MSU64), C(C_MSL64), C(C_BONES)

    def SEL(h, k):
        return V(cst[0:k, C_IDN + h:C_IDN + h + 1].to_broadcast([k, 128]), ["cst"])

    def prmc(l, col, p1=128, p0=0):
        return V(prm[l][p0:p1, col:col + 1], [("prm", l)])

    def cb(j, p1=128, p0=0):
        return V(cbuf[p0:p1, j, PADC:CW], [("c", j)])

    def cbs(j, s, p0=0, p1=128):
        return V(cbuf[p0:p1, j, PADC + s * 128:PADC + (s + 1) * 128], [("c", j)])

    P.dma("gpsimd", "cst", cst[:], cst_d, writes=["cst"])
    for l in range(depth):
        P.dma("gpsimd", "prm%d" % l, prm[l][:], prm_d[l], writes=[("prm", l)])
    P.memset(V(cbuf[:], [("c", j) for j in range(NCH)]), 0.0)
    for q in range(5):
        P.memset(V(bd[q][:], [("bd", q)]), 0.0)
    P.cp(V(idnb[:], ["idnb"]), IDN)
    P.memset(V(onesb[:], ["onesb"]), 1.0)
    for l in range(depth):
        P.memset(V(Sgla[l][:], [("Sgla", l)]), 0.0)
        P.memset(V(Sssd[l][:], [("Sssd", l)]), 0.0)
        P.memset(V(Sssdb[l][:], [("Sssdb", l)]), 0.0)
        P.memset(V(Cml[l][:], [("Cml", l)]), 0.0)
        P.memset(V(Cmlb[l][:], [("Cmlb", l)]), 0.0)
        P.memset(V(nml[l][:], [("nml", l)]), 0.0)
        P.memset(V(nmlb[l][:], [("nmlb", l)]), 0.0)
        P.memset(V(Trw[l][:], [("Trw", l)]), 0.0)
        P.memset(V(Trwb[l][:], [("Trwb", l)]), 0.0)
        P.memset(V(car_s[l][:], [("car_s", l, j) for j in range(8)]), 0.0)
        P.memset(V(car_m[l][:], [("car_m", l, j) for j in range(8)]), 0.0)
        P.memset(V(car_r[l][:], [("car_r", l, j) for j in range(15)]), 0.0)
        P.act(V(negA[l][:], [("negA", l)]), prmc(l, P_SALOG, 8), AF.Exp)
        P.ts(V(negA[l][:], [("negA", l)]), V(negA[l][:], [("negA", l)]), -1.0)

    def plan(l, m):
        off = col_offsets(l)
        it = []
        if m == "gla":
            for j in range(2):
                it.append((off['gla_q'] + 128 * j, 128, j, 0))
            for j in range(2):
                it.append((off['gla_k'] + 128 * j, 128, 2 + j, 0))
            for j in range(4):
                it.append((off['gla_z'] + 128 * j, 128, 4 + j, 0))
            for j in range(4):
                it.append((off['gla_v'] + 128 * j, 128, 8 + j, 0))
            it.append((off['gla_gk'], 16, 12, 0))
        elif m == "rwkv":
            b = off['rwkv_shift']
            for j in range(12):
                it.append((b + 128 * j, 128, j, 0))
            it.append((b + 1536, 64, 12, 0))
            it.append((b + 1600, 64, 13, 0))
            if l > 0:
                it.append((b + 1664, 32, 14, 0))
            for j in range(4):
                it.append((off['rwkv_z'] + 128 * j, 128, 15 + j, 0))
        elif m == "ssd":
            for j in range(8):
                it.append((off['ssd_xbc'] + 128 * j, 128, j, 0))
            for j in range(4):
                it.append((off['ssd_z'] + 128 * j, 128, 8 + j, 0))
            it.append((off['ssd_dt'], 8, 12, 0))
        elif m == "mlstm":
            for j in range(8):
                it.append((off['mlstm_qk'] + 128 * j, 128, j, 0))
            for j in range(4):
                it.append((off['mlstm_o'] + 128 * j, 128, 8 + j, 0))
            for j in range(4):
                it.append((off['mlstm_z'] + 128 * j, 128, 12 + j, 0))
            for j in range(4):
                it.append((off['mlstm_v'] + 128 * j, 128, 16 + j, 0))
            it.append((off['mlstm_i'], 8, 20, 0))
        return it

    plans = {(l, m): plan(l, m) for l in range(depth) for m in mixers}

    def groups_of(l, m):
        gs = []
        for it in plans[(l, m)]:
            c0, n = it[0], it[1]
            if gs and gs[-1][0] + gs[-1][1] == c0 and gs[-1][1] + n <= WB:
                gs[-1][2].append((gs[-1][1],) + tuple(it[1:]))
                gs[-1][1] += n
            else:
                gs.append([c0, n, [(0,) + tuple(it[1:])]])
        return gs

    gplans = {(l, m): groups_of(l, m) for l in range(depth) for m in mixers}
    wq = []
    for t in range(n_tiles):
        for l in range(depth):
            for m in mixers:
                for (c0, n, chunks) in gplans[(l, m)]:
                    for kg in range(4):
                        wq.append((win_d[l][kg * 512:(kg + 1) * 512, c0:c0 + n].rearrange("(k p) c -> p k c", p=128), (4, n)))
                mi = MIXN.index(m)
                for q in range(2048 // WB):
                    wq.append((wout_d[l][mi * 512:(mi + 1) * 512, q * WB:(q + 1) * WB]
                               .rearrange("(k p) c -> p k c", p=128), (4, WB)))
    wstate = {"issued": 0, "used": 0}

    def w_issue():
        i = wstate["issued"]
        src, (k, n) = wq[i]
        slot = i % NW
        dst = wbuf[slot][:, 0:k * n].rearrange("p (k n) -> p k n", k=k)
        P.dma("sync", "w%d" % slot, dst, src, writes=[("wbuf", slot)])
        wstate["issued"] += 1

    def w_next():
        i = wstate["used"]
        while wstate["issued"] < min(i + NW, len(wq)):
            w_issue()
        wstate["used"] += 1
        return wbuf[i % NW], ("wbuf", i % NW)

    def inproj(l, m):
        hooks = post_hooks(l, m)
        for (c0, n, chunks) in gplans[(l, m)]:
            with Scope() as sc:
                pps = [sc.take(PB)(lambda a, pb=pb, cn=cn: a[pb:pb + cn, :]) for (o, cn, j, pb) in chunks]
                for kg in range(4):
                    wt, wkey = w_next()
                    w3 = wt[:, 0:4 * n].rearrange("p (k n) -> p k n", k=4)
                    for ci, (o, cn, j, pb) in enumerate(chunks):
                        for k in range(4):
                            P.mm(pps[ci], V(w3[:, k, o:o + cn], [wkey]), V(hT[:, kg * 4 + k, :], [("hT", kg * 4 + k)]),
                                 start=(kg == 0 and k == 0), stop=(kg == 3 and k == 3))
                for ci, (o, cn, j, pb) in enumerate(chunks):
                    P.cp(cb(j, p0=pb, p1=pb + cn), pps[ci], eng=evac_eng())
            for ci, (o, cn, j, pb) in enumerate(chunks):
                if j in hooks:
                    hooks[j]()

    def outproj(l, m):
        for q in range(2048 // WB):
            wt, wkey = w_next()
            w3 = wt[:, 0:4 * WB].rearrange("p (k n) -> p k n", k=4)
            for oc in range(WB // 128):
                with Scope() as sc:
                    pp = sc.take(PB)
                    for k in range(4):
                        P.mm(pp, V(w3[:, k, oc * 128:(oc + 1) * 128], [wkey]), V(yT[:, k, :], [("y", k)]),
                             start=(k == 0), stop=(k == 3))
                    j = q * (WB // 128) + oc
                    xv = V(xT[:, j, :], [("xT", j)])
                    P.tt(xv, xv, pp, ALU.add)

    def rmsnorm(gcol, l, dst_fn):
        with Scope() as sc:
            pp = sc.take(PB)
            for j in range(16):
                with Scope() as s2:
                    sq = s2.take(BIG)
                    P.act(sq, V(xT[:, j, :], [("xT", j)]), AF.Square)
                    P.mm(pp, ONES, sq, start=(j == 0), stop=(j == 15))
            rs = sc.take(BIG)
            P.act(rs, pp, AF.Ln, bias=EPS, scale=1.0 / D)
            P.act(rs, rs, AF.Exp, scale=-0.5)
            for j in range(16):
                P.stt(dst_fn(j), V(xT[:, j, :], [("xT", j)]), prmc(l, gcol + j), rs, ALU.mult, ALU.mult)

    def conv_one(l, car, cname, wcol, bcol, j):
        ck = (cname, l, j)
        with Scope() as sc:
            P.cp(V(cbuf[:, j, 1:4], [("c", j)]), V(car[l][:, j, 0:3], [ck]))
            acc = sc.take(BIG)
            P.ts(acc, V(cbuf[:, j, 1:1 + T], [("c", j)]), prmc(l, wcol + j))
            for tap in range(1, 4):
                P.stt(acc, V(cbuf[:, j, 1 + tap:1 + tap + T], [("c", j)]), prmc(l, wcol + tap * 8 + j), acc,
                      ALU.mult, ALU.add)
            P.cp(V(car[l][:, j, 0:3], [ck]), V(cbuf[:, j, T + 1:T + 4], [("c", j)]), eng="scalar")
            P.act(cb(j), acc, AF.Silu, bias=prmc(l, bcol + j))

    def shift_one(l, j):
        ck = ("car_r", l, j)
        p1 = 128 if j < 12 else (64 if j < 14 else 32)
        with Scope() as sc:
            P.cp(V(cbuf[0:p1, j, 3:4], [("c", j)]), V(car_r[l][0:p1, j, 0:1], [ck]), eng="scalar")
            d = sc.take(BIG)(lambda a: a[0:p1, :])
            P.tt(d, V(cbuf[0:p1, j, 3:3 + T], [("c", j)]), V(cbuf[0:p1, j, 4:4 + T], [("c", j)]), ALU.subtract)
            P.cp(V(car_r[l][0:p1, j, 0:1], [ck]), V(cbuf[0:p1, j, T + 3:T + 4], [("c", j)]), eng="scalar")
            P.stt(cb(j, p1=p1), d, prmc(l, P_RMU + j, p1), cb(j, p1=p1), ALU.mult, ALU.add)

    def post_hooks(l, m):
        h = {}
        if m == "gla":
            for j in range(4):
                h[4 + j] = lambda j=j: P.act(cb(4 + j), cb(4 + j), AF.Silu)
        elif m == "ssd":
            for j in range(8):
                h[j] = lambda j=j: conv_one(l, car_s, "car_s", P_SCW, P_SCB, j)
            for j in range(4):
                h[8 + j] = lambda j=j: P.act(cb(8 + j), cb(8 + j), AF.Silu)
        elif m == "mlstm":
            for j in range(8):
                h[j] = lambda j=j: conv_one(l, car_m, "car_m", P_MCW, P_MCB, j)
            for j in range(4):
                h[8 + j] = lambda j=j: P.act(cb(8 + j), cb(8 + j), AF.Sigmoid)
                h[12 + j] = lambda j=j: P.act(cb(12 + j), cb(12 + j), AF.Silu)
            h[20] = lambda: P.ts(cb(20, p1=8), cb(20, p1=8), prmc(l, P_MIFB, 8), None, ALU.add)
        elif m == "rwkv":
            for j in range(15):
                h[j] = lambda j=j: shift_one(l, j)

            def h12():
                shift_one(l, 12)
                P.act(cb(12, p1=64), cb(12, p1=64), AF.Tanh)
                P.memset(V(cbuf[64:65, 12, :], [("c", 12)]), 1.0)

            def h13():
                shift_one(l, 13)
                P.memset(V(cbuf[64:65, 13, :], [("c", 13)]), 1.0)
            h[12], h[13] = h12, h13
            for j in range(4):
                h[15 + j] = lambda j=j: P.act(cb(15 + j), cb(15 + j), AF.Silu)
        return h

    def gla_pre(l):
        P.memset(V(cbuf[0:32, 12, :], [("c", 12)]), 1.0)

    def gla(l):
        g4 = lambda v: v(lambda a: a.rearrange("p (g t) -> p g t", g=4))
        g2 = lambda v: v(lambda a: a.rearrange("p (g t) -> p g t", g=2))
        Sv = g2(V(Sgla[l][:], [("Sgla", l)]))
        for s in range(NSUB):
            sl = slice(s * 128, (s + 1) * 128)
            csl = slice(PADC + s * 128, PADC + (s + 1) * 128)
            ck = lambda j0, n: [("c", j0 + i) for i in range(n)]
            qv = V(cbuf[:, 0:2, csl], ck(0, 2))
            kv = V(cbuf[:, 2:4, csl], ck(2, 2))
            zv = V(cbuf[:, 4:8, csl], ck(4, 4))
            vv = V(cbuf[:, 8:12, csl], ck(8, 4))
            with Scope() as sc:
                vt = g4(sc.take(SMB))
                with Scope() as s2:
                    pvt = g4(s2.take(PB))
                    for h in range(4):
                        P.tr(pvt(lambda a: a[:, h, :]), vv(lambda a: a[:, h, :]), IDN)
                    P.cp(vt, pvt, eng="scalar")
                nb = sc.take(BIG)
                nlsv = nb(lambda a: a[:, 0:256])
                ercv = nb(lambda a: a[:, 256:512])
                with Scope() as s2:
                    pg = s2.take(PH)
                    P.mm(pg, cbs(12, s, 0, 17), V(gkw[l][:], ["gkw"]))
                    P.act(nlsv, pg, AF.Exp, scale=-1.0)
                    P.act(nlsv, nlsv, AF.Ln, bias=1.0)
                eb = sc.take(BIG)
                eq = g2(eb(lambda a: a[:, 0:256]))
                gb = sc.take(BIG)
                qg = g2(gb(lambda a: a[:, 0:256]))
                kg = g2(gb(lambda a: a[:, 256:512]))
                with Scope() as s2:
                    pc = g2(s2.take(PH))
                    for j in range(2):
                        P.mm(pc(lambda a: a[:, j, :]), nlsv(lambda a: a[:, j * 128:(j + 1) * 128]), TRI)
                    ek = g2(eb(lambda a: a[:, 256:512]))
                    P.act(eq, pc, AF.Exp, scale=-1.0 / 16)
                    P.act(ek, pc, AF.Exp, scale=1.0 / 16)
                    P.stt(qg, qv, 0.125, eq, ALU.mult, ALU.mult)
                    P.tt(kg, kv, ek, ALU.mult)
                kb = sc.take(SMB)
                kdv = kb(lambda a: a[:, 0:256])
                with Scope() as s2:
                    pr = s2.take(PH)
                    P.mm(pr, SGT, nlsv)
                    P.act(ercv, pr, AF.Exp, scale=-1.0 / 16)
                    pt = s2.take(PH)
                    for j in range(2):
                        P.tr(pt(lambda a: a[:, j * 128:(j + 1) * 128]), kv(lambda a: a[:, j, :]), IDN)
                    P.tt(kdv, pt, ercv, ALU.mult)
                attm = g4(sc.take(SMB))
                with Scope() as s2:
                    pa2 = [g2(s2.take(PH)), g2(s2.take(PH))]
                    for h in range(4):
                        j, pb = h // 2, (h % 2) * 64
                        P.mm(pa2[h % 2](lambda a: a[:, j, :]), kg(lambda a: a[pb:pb + 64, j, :]), qg(lambda a: a[pb:pb + 64, j, :]))
                    for hh in range(2):
                        P.tt(attm(lambda a: a[:, hh::2, :]), pa2[hh], TRI(lambda a: a.unsqueeze(1).to_broadcast([128, 2, 128])), ALU.mult)
                with Scope() as s2:
                    po4 = g4(s2.take(PB))
                    for h in range(4):
                        j, pb = h // 2, (h % 2) * 64
                        P.mm(po4(lambda a: a[:, h, :]), vt(lambda a: a[:, h, :]), attm(lambda a: a[:, h, :]), start=True, stop=False)
                        P.mm(po4(lambda a: a[:, h, :]), Sv(lambda a: a[pb:pb + 64, j, :]), qg(lambda a: a[pb:pb + 64, j, :]),
                             start=False, stop=True)
                    osq = g4(s2.take(BIG))
                    P.act(osq, po4, AF.Square)
                    pss = g4(s2.take(PB))
                    for h in range(4):
                        P.mm(pss(lambda a: a[:, h, :]), ONES, osq(lambda a: a[:, h, :]))
                    P.act(osq, pss, AF.Ln, bias=EPS, scale=1.0 / 128)
                    P.act(osq, osq, AF.Exp, scale=-0.5)
                    P.tt(osq, po4, osq, ALU.mult)
                    P.tt(osq, osq, V(prm[l][:, P_GLAG:P_GLAG + 4].unsqueeze(2).to_broadcast([128, 4, 128]), [("prm", l)]), ALU.mult)
                    P.tt(V(yT[:, :, sl], [("y", k) for k in range(4)]), osq, zv, ALU.mult)
                with Scope() as s2:
                    pi = s2.take(PB)
                    for j in range(2):
                        P.mm(pi(lambda a: a[:, j * 256:(j + 1) * 256]), kdv(lambda a: a[:, j * 128:(j + 1) * 128]),
                             vt(lambda a: a[:, 2 * j:2 * j + 2, :].rearrange("p g t -> p (g t)")))
                    pi3 = pi(lambda a: a.rearrange("p (j w) -> p j w", j=2))
                    for hh in range(2):
                        pb = hh * 64
                        sv = Sv(lambda a: a[pb:pb + 64, :, :])
                        P.tt(sv, sv, eq(lambda a: a[pb:pb + 64, :, 127:128].to_broadcast([64, 2, 128])), ALU.mult)
                        P.tt(sv, sv, pi3(lambda a: a[pb:pb + 64, :, hh * 128:(hh + 1) * 128]), ALU.add)

    def ssd(l):
        g4 = lambda v: v(lambda a: a.rearrange("p (g t) -> p g t", g=4))
        Sv = V(Sssd[l][:], [("Sssd", l)])
        Sb = V(Sssdb[l][:], [("Sssdb", l)])

        def bc4(v):
            return v(lambda a: a.unsqueeze(1).to_broadcast([128, 4, 128]))

        with Scope() as st_:
            dt_ = st_.take(BIG)(lambda a: a[0:8, :])
            dta_ = st_.take(BIG)(lambda a: a[0:8, :])
            P.act(dt_, cb(12, p1=8), AF.Exp, bias=prmc(l, P_SDTB, 8))
            P.act(dt_, dt_, AF.Ln, bias=1.0)
            P.ts(dta_, dt_, V(negA[l][:], [("negA", l)]))
            for s in range(NSUB):
                sl = slice(s * 128, (s + 1) * 128)
                csl = slice(PADC + s * 128, PADC + (s + 1) * 128)
                ck = lambda j0, n: [("c", j0 + i) for i in range(n)]
                xv = V(cbuf[:, 0:4, csl], ck(0, 4))
                zv = V(cbuf[:, 8:12, csl], ck(8, 4))
                with Scope() as sc:
                    dtk = sc.take(TK)
                    with Scope() as s2:
                        p1 = s2.take(PQ)
                        P.tr(p1(lambda a: a[:, 0:8]), dt_(lambda a: a[:, sl]), C(C_IDN, 8, 0, 8))
                        P.tr(p1(lambda a: a[:, 8:16]), dta_(lambda a: a[:, sl]), C(C_IDN, 8, 0, 8))
                        P.cp(dtk, p1(lambda a: a[:, 0:16]))
                    dt_tok = dtk(lambda a: a[:, 0:8])
                    dta_tok = dtk(lambda a: a[:, 8:16])
                    cumT = sc.take(BIG)(lambda a: a[0:8, 0:128])
                    ex = sc.take(TK)
                    with Scope() as s2:
                        p2 = s2.take(PQ)
                        P.mm(p2(lambda a: a[0:8, :]), dta_tok, TRI)
                        P.cp(cumT, p2(lambda a: a[0:8, :]), eng="scalar")
                        p3 = s2.take(PQ)
                        P.mm(p3(lambda a: a[:, 0:8]), ONES, dta_tok)
                        P.mm(p3(lambda a: a[:, 8:16]), SGT, dta_tok)
                        P.act(ex, p3(lambda a: a[:, 0:16]), AF.Exp)
                    dec, erev = ex(lambda a: a[:, 0:8]), ex(lambda a: a[:, 8:16])
                    xdtv, xdtwv = sc.take(SMB), sc.take(SMB)
                    h8 = lambda v: v(lambda a: a.rearrange("p (h e) -> p h e", h=8))
                    with Scope() as s2:
                        pt = s2.take(PB)
                        for jc in range(4):
                            P.tr(pt(lambda a: a[:, jc * 128:(jc + 1) * 128]), cbs(jc, s), IDN)
                        P.tt(h8(xdtv), h8(pt), dt_tok(lambda a: a.unsqueeze(2).to_broadcast([128, 8, 64])), ALU.mult)
                    P.tt(h8(xdtwv), h8(xdtv), erev(lambda a: a.unsqueeze(2).to_broadcast([128, 8, 64])), ALU.mult)
                    bc_t = sc.take(SMB)
                    btok = bc_t(lambda a: a[:, 0:256].rearrange("p (g t) -> p g t", g=2))
                    cbm = bc_t(lambda a: a[:, 256:512].rearrange("p (g t) -> p g t", g=2))
                    with Scope() as s2:
                        pt = s2.take(PH)(lambda a: a.rearrange("p (g t) -> p g t", g=2))
                        pc = s2.take(PH)(lambda a: a.rearrange("p (g t) -> p g t", g=2))
                        for g in range(2):
                            P.tr(pt(lambda a: a[:, g, :]), cbs(4 + g, s), IDN)
                            P.mm(pc(lambda a: a[:, g, :]), cbs(4 + g, s), cbs(6 + g, s))
                        P.cp(btok, pt, eng="scalar")
                        P.tt(cbm, pc, TRI(lambda a: a.unsqueeze(1).to_broadcast([128, 2, 128])), ALU.mult)
                    pyb = sc.take(PB)
                    py4 = g4(pyb)
                    for g in range(2):
                        with Scope() as s2:
                            R = g4(s2.take(BIG))
                            P.tt(R, bc4(SGT), dta_tok(lambda a: a[:, 4 * g:4 * g + 4].unsqueeze(2).to_broadcast([128, 4, 128])), ALU.mult)
                            WT = g4(s2.take(SMB))
                            with Scope() as s3:
                                pg = g4(s3.take(PB))
                                for hl in range(4):
                                    P.mm(pg(lambda a: a[:, hl, :]), R(lambda a: a[:, hl, :]), TRI)
                                LT = g4(s3.take(BIG))
                                P.act(LT, pg, AF.Exp)
                                P.tt(WT, LT, cbm(lambda a: a[:, g:g + 1, :].to_broadcast([128, 4, 128])), ALU.mult)
                            Cw = g4(s2.take(SMB))
                            with Scope() as s3:
                                pbc = g4(s3.take(PB))
                                for hl in range(4):
                                    P.mm(pbc(lambda a: a[:, hl, :]), SEL(4 * g + hl, 8), cumT)
                                ec = g4(s3.take(BIG))
                                P.act(ec, pbc, AF.Exp)
                                P.tt(Cw, cbs(6 + g, s)(lambda a: a.unsqueeze(1).to_broadcast([128, 4, 128])), ec, ALU.mult)
                            for hl in range(4):
                                h = 4 * g + hl
                                jc, pb = h // 2, (h % 2) * 64
                                po = py4(lambda a: a[pb:pb + 64, jc, :])
                                P.mm(po, xdtv(lambda a: a[:, h * 64:(h + 1) * 64]), WT(lambda a: a[:, hl, :]), start=True, stop=False)
                                P.mm(po, Sb(lambda a: a[:, h * 64:(h + 1) * 64]), Cw(lambda a: a[:, hl, :]), start=False, stop=True)
                    with Scope() as s2:
                        y2 = g4(s2.take(BIG))
                        sq = g4(s2.take(BIG))
                        P.tt(y2, xv, V(prm[l][:, P_SDCH:P_SDCH + 4].unsqueeze(2).to_broadcast([128, 4, 128]), [("prm", l)]), ALU.mult)
                        P.tt(y2, y2, py4, ALU.add)
                        P.tt(y2, y2, zv, ALU.mult)
                        P.act(sq, y2, AF.Square)
                        pss = s2.take(PQ)
                        for jc in range(4):
                            P.mm(pss, ONES, sq(lambda a: a[:, jc, :]), start=(jc == 0), stop=(jc == 3))
                        rs = s2.take(BIG)(lambda a: a[:, 0:128])
                        P.act(rs, pss, AF.Ln, bias=EPS, scale=1.0 / 512)
                        P.act(rs, rs, AF.Exp, scale=-0.5)
                        P.tt(y2, y2, V(prm[l][:, P_SNG:P_SNG + 4].unsqueeze(2).to_broadcast([128, 4, 128]), [("prm", l)]), ALU.mult)
                        P.tt(V(yT[:, :, sl], [("y", k) for k in range(4)]), y2, bc4(rs), ALU.mult)
                    with Scope() as s2:
                        pi = s2.take(PB)
                        for g in range(2):
                            P.mm(pi(lambda a: a[:, g * 256:(g + 1) * 256]), btok(lambda a: a[:, g, :]),
                                 xdtwv(lambda a: a[:, g * 256:(g + 1) * 256]))
                        P.tt(h8(Sv), h8(Sv), dec(lambda a: a.unsqueeze(2).to_broadcast([128, 8, 64])), ALU.mult)
                        P.tt(Sv, Sv, pi, ALU.add)
                        P.cp(Sb, Sv, eng="scalar")

    def mlstm(l):
        isq = float(1.0 / np.sqrt(128.0))
        g4 = lambda v: v(lambda a: a.rearrange("p (g t) -> p g t", g=4))
        C4 = g4(V(Cml[l][:], [("Cml", l)]))
        C4b = g4(V(Cmlb[l][:], [("Cmlb", l)]))
        n4 = V(nml[l][:], [("nml", l)])
        n4b = V(nmlb[l][:], [("nmlb", l)])
        ONESb = V(onesb[:], ["onesb"])

        def bc4(v):
            return v(lambda a: a.unsqueeze(1).to_broadcast([128, 4, 128]))

        def bch(v):
            return v(lambda a: a.unsqueeze(2).to_broadcast([128, 4, 128]))

        def pbank(sc):
            return g4(sc.take(PB))

        for s in range(NSUB):
            sl = slice(s * 128, (s + 1) * 128)
            csl = slice(PADC + s * 128, PADC + (s + 1) * 128)
            ck4 = lambda j0: [("c", j0 + i) for i in range(4)]
            qv = V(cbuf[:, 0:4, csl], ck4(0))
            kv = V(cbuf[:, 4:8, csl], ck4(4))
            ov = V(cbuf[:, 8:12, csl], ck4(8))
            zv = V(cbuf[:, 12:16, csl], ck4(12))
            vv = V(cbuf[:, 16:20, csl], ck4(16))
            with Scope() as sc:
                tk = sc.take(TK)
                with Scope() as s2:
                    p0 = s2.take(PQ)
                    P.tr(p0(lambda a: a[:, 0:8]), cbs(20, s, 0, 8), C(C_IDN, 8, 0, 8))
                    P.cp(tk(lambda a: a[:, 0:4]), p0(lambda a: a[:, 0:4]))
                    P.act(tk(lambda a: a[:, 4:8]), p0(lambda a: a[:, 4:8]), AF.Exp, scale=-1.0)
                    P.act(tk(lambda a: a[:, 4:8]), tk(lambda a: a[:, 4:8]), AF.Ln, bias=1.0)
                li = tk(lambda a: a[:, 0:4])
                nl = tk(lambda a: a[:, 4:8])
                cumT = sc.take(BIG)(lambda a: a[0:4, 0:128])
                with Scope() as s2:
                    p1 = s2.take(PQ)
                    P.mm(p1(lambda a: a[0:4, :]), nl, TRI)
                    P.ts(cumT, p1(lambda a: a[0:4, :]), -1.0)
                    p2 = s2.take(PQ)
                    P.mm(p2(lambda a: a[:, 0:4]), ONES, nl)
                    P.mm(p2(lambda a: a[:, 4:8]), SGT, nl)
                    P.act(tk(lambda a: a[:, 8:12]), p2(lambda a: a[:, 0:4]), AF.Exp, scale=-1.0)
                    gt = s2.take(TK)
                    P.stt(gt(lambda a: a[:, 0:4]), p2(lambda a: a[:, 4:8]), -1.0, li, ALU.mult, ALU.add)
                    P.act(tk(lambda a: a[:, 12:16]), gt(lambda a: a[:, 0:4]), AF.Exp)
                dec, wend = tk(lambda a: a[:, 8:12]), tk(lambda a: a[:, 12:16])
                vt = g4(sc.take(SMB))
                with Scope() as s2:
                    pvt = pbank(s2)
                    for h in range(4):
                        P.tr(pvt(lambda a: a[:, h, :]), vv(lambda a: a[:, h, :]), IDN)
                    P.cp(vt, pvt, eng="scalar")
                ed = g4(sc.take(SMB))
                with Scope() as s2:
                    t_ = g4(s2.take(BIG))
                    rh = g4(s2.take(BIG))
                    P.stt(t_, bc4(SGT), -1.0, bch(nl), ALU.mult, ALU.mult)
                    P.tt(rh, bc4(IDN), bch(li), ALU.mult)
                    P.tt(rh, rh, t_, ALU.add)
                    pl = pbank(s2)
                    for h in range(4):
                        P.mm(pl(lambda a: a[:, h, :]), rh(lambda a: a[:, h, :]), TRI)
                    P.act(ed, pl, AF.Exp)
                wq_ = g4(sc.take(SMB))
                with Scope() as s2:
                    pk = pbank(s2)
                    for h in range(4):
                        P.mm(pk(lambda a: a[:, h, :]), kv(lambda a: a[:, h, :]), qv(lambda a: a[:, h, :]))
                    m1 = g4(s2.take(SMB))
                    P.stt(m1, pk, isq, bc4(TRI), ALU.mult, ALU.mult)
                    P.tt(wq_, m1, ed, ALU.mult)
                qw = g4(sc.take(SMB))
                with Scope() as s2:
                    pbc = pbank(s2)
                    for h in range(4):
                        P.mm(pbc(lambda a: a[:, h, :]), SEL(h, 4), cumT)
                    wi = g4(s2.take(BIG))
                    P.act(wi, pbc, AF.Exp)
                    P.tt(qw, qv, wi, ALU.mult)
                hh_ = g4(sc.take(BIG))
                with Scope() as s2:
                    da = g4(s2.take(BIG))
                    pd = pbank(s2)
                    for h in range(4):
                        P.mm(pd(lambda a: a[:, h, :]), ONESb, wq_(lambda a: a[:, h, :]), start=True, stop=False)
                        P.mm(pd(lambda a: a[:, h, :]), n4b(lambda a: a[:, h:h + 1].to_broadcast([128, 128])),
                             qw(lambda a: a[:, h, :]), start=False, stop=True)
                    P.act(da, pd, AF.Abs)
                    P.ts(da, da, 1.0, None, ALU.max)
                    P.recip(da, da)
                    pn = pbank(s2)
                    for h in range(4):
                        P.mm(pn(lambda a: a[:, h, :]), vt(lambda a: a[:, h, :]), wq_(lambda a: a[:, h, :]), start=True, stop=False)
                        P.mm(pn(lambda a: a[:, h, :]), C4b(lambda a: a[:, h, :]), qw(lambda a: a[:, h, :]), start=False, stop=True)
                    P.tt(hh_, pn, da, ALU.mult)
                P.tt(hh_, hh_, ov, ALU.mult)
                with Scope() as s2:
                    hc = g4(s2.take(BIG))
                    sq = g4(s2.take(BIG))
                    pm = pbank(s2)
                    for h in range(4):
                        P.mm(pm(lambda a: a[:, h, :]), ONES, hh_(lambda a: a[:, h, :]))
                    P.stt(hc, pm, -1.0 / 128, hh_, ALU.mult, ALU.add)
                    P.act(sq, hc, AF.Square)
                    pv = pbank(s2)
                    for h in range(4):
                        P.mm(pv(lambda a: a[:, h, :]), ONES, sq(lambda a: a[:, h, :]))
                    P.act(sq, pv, AF.Ln, bias=EPS, scale=1.0 / 128)
                    P.act(sq, sq, AF.Exp, scale=-0.5)
                    P.tt(hc, hc, sq, ALU.mult)
                    P.tt(hc, hc, V(prm[l][:, P_MNG:P_MNG + 4].unsqueeze(2).to_broadcast([128, 4, 128]), [("prm", l)]), ALU.mult)
                    P.tt(V(yT[:, :, sl], [("y", k) for k in range(4)]), hc, zv, ALU.mult)
                with Scope() as s2:
                    kw = g4(s2.take(SMB))
                    pt = pbank(s2)
                    for h in range(4):
                        P.tr(pt(lambda a: a[:, h, :]), kv(lambda a: a[:, h, :]), IDN)
                    P.stt(kw, pt, isq, bch(wend), ALU.mult, ALU.mult)
                    pi = pbank(s2)
                    pi2 = s2.take(PQ)
                    for h in range(4):
                        P.mm(pi(lambda a: a[:, h, :]), kw(lambda a: a[:, h, :]), vt(lambda a: a[:, h, :]))
                        P.mm(pi2(lambda a: a[:, h:h + 1]), kw(lambda a: a[:, h, :]), ONESb(lambda a: a[:, 0:1]))
                    P.tt(C4, C4, bch(dec), ALU.mult)
                    P.tt(C4, C4, pi, ALU.add)
                    P.cp(C4b, C4, eng="scalar")
                    P.tt(n4, n4, dec, ALU.mult)
                    P.tt(n4, n4, pi2(lambda a: a[:, 0:4]), ALU.add)
                    P.cp(n4b, n4, eng="scalar")

    def rwkv_pre(l):
        pass

    def rwkv(l):
        vfv = lambda jc: V(vfirst[:, jc, :], [("vfirst", jc)])
        for jc in range(4):
            if l == 0:
                P.cp(vfv(jc), cb(8 + jc), eng="scalar")
            else:
                with Scope() as sc:
                    pv = sc.take(PB)
                    P.mm(pv, V(ra2[l][0:32, 512 + jc * 128:512 + (jc + 1) * 128], ["ra2"]), cb(14, p1=32))
                    sg = sc.take(BIG)
                    P.act(sg, pv, AF.Sigmoid, bias=prmc(l, P_RV0 + jc))
                    d = sc.take(BIG)
                    P.tt(d, vfv(jc), cb(8 + jc), ALU.subtract)
                    P.tt(d, d, sg, ALU.mult)
                    P.tt(cb(8 + jc), cb(8 + jc), d, ALU.add)
        LD = 0.6065306597126334
        g4 = lambda v: v(lambda a: a.rearrange("p (g t) -> p g t", g=4))
        AtT, RtT, BtT, KtT, VTb = [g4(V(bd[q][:], [("bd", q)])) for q in range(5)]
        IDNb = V(idnb[:], ["idnb"])
        Tv = g4(V(Trw[l][:], [("Trw", l)]))
        Tvb = g4(V(Trwb[l][:], [("Trwb", l)]))

        def bc4(v):
            return v(lambda a: a.unsqueeze(1).to_broadcast([128, 4, 128]))

        def prmb(col):
            return V(prm[l][:, col:col + 4].unsqueeze(2).to_broadcast([128, 4, 128]), [("prm", l)])

        def pbank(sc, bf=False):
            pp = sc.take(PB)
            if bf:
                return pp(lambda a: a.bitcast(BF16)[:, 0:512].rearrange("p (g t) -> p g t", g=4))
            return g4(pp)

        for s in range(NSUB):
            sl = slice(s * 128, (s + 1) * 128)
            csl = slice(PADC + s * 128, PADC + (s + 1) * 128)
            ck4 = lambda j0: [("c", j0 + i) for i in range(4)]
            rview = V(cbuf[:, 0:4, csl], ck4(0))
            kview = V(cbuf[:, 4:8, csl], ck4(4))
            vview = V(cbuf[:, 8:12, csl], ck4(8))
            zview = V(cbuf[:, 15:19, csl], ck4(15))
            with Scope() as so:
              yr = g4(so.take(BIG))
              with Scope() as sc:
                sgv = sc.take(BIG)
                with Scope() as s2:
                    pw = s2.take(PB)
                    P.mm(pw, cbs(12, s, 0, 65), V(rw2[l][:], ["rw2"]))
                    P.act(sgv, pw, AF.Sigmoid)
                a_ = g4(sc.take(BIG))
                kk = g4(sc.take(BIG))
                with Scope() as s2:
                    pa = pbank(s2)
                    for jc in range(4):
                        P.mm(pa(lambda a: a[:, jc, :]), V(ra2[l][0:65, jc * 128:(jc + 1) * 128], ["ra2"]), cbs(13, s, 0, 65))
                    P.act(a_, pa, AF.Sigmoid)
                P.tt(kk, kview, prmb(P_RKK), ALU.mult)
                with Scope() as s2:
                    sq = g4(s2.take(BIG))
                    P.act(sq, kk, AF.Square)
                    pn = pbank(s2)
                    for jc in range(4):
                        P.mm(pn(lambda a: a[:, jc, :]), BONES, sq(lambda a: a[:, jc, :]))
                    P.ts(sq, pn, 1e-24, None, ALU.max)
                    P.act(sq, sq, AF.Ln)
                    P.act(sq, sq, AF.Exp, scale=-0.5)
                    P.tt(kk, kk, sq, ALU.mult)
                    P.ts(sq, a_, -1.0, None, ALU.add)
                    P.tt(sq, sq, prmb(P_RKA), ALU.mult)
                    P.stt(kview, sq, 1.0, kview, ALU.add, ALU.mult)
                P.tt(a_, kk, a_, ALU.mult)
                Wt = g4(sc.take(BIG))
                iW, Wx = g4(sc.take(SMB)), g4(sc.take(SMB))
                with Scope() as s2:
                    pc1, pc2 = pbank(s2), pbank(s2)
                    for jc in range(4):
                        P.mm(pc1(lambda a: a[:, jc, :]), sgv(lambda a: a[:, jc * 128:(jc + 1) * 128]), MU64)
                        P.mm(pc2(lambda a: a[:, jc, :]), sgv(lambda a: a[:, jc * 128:(jc + 1) * 128]), MSU64)
                    P.act(Wt, pc1, AF.Exp, scale=-LD)
                    P.act(iW, pc1, AF.Exp, scale=LD)
                    P.act(Wx, pc2, AF.Exp, scale=-LD)
                for cc in range(2):
                    wsl = slice(cc * 64, (cc + 1) * 64)
                    with Scope() as sk:
                        for hh in range(2):
                            pr = slice(hh * 64, (hh + 1) * 64)
                            src = lambda v: v(lambda a: a[pr, :, wsl])
                            dst = lambda v: v(lambda a: a[pr, :, hh * 64:(hh + 1) * 64])
                            P.stt(dst(AtT), src(kk), -1.0, src(Wx), ALU.mult, ALU.mult)
                            P.tt(dst(RtT), src(rview), src(Wt), ALU.mult)
                            P.tt(dst(BtT), src(a_), src(iW), ALU.mult)
                            P.tt(dst(KtT), src(kview), src(iW), ALU.mult)
                            P.cp(dst(VTb), src(vview), eng="scalar")

                        def trans(srcv):
                            o = g4(sk.take(SMB))
                            with Scope() as s2:
                                pt = pbank(s2, bf=True)
                                for jc in range(4):
                                    P.tr(pt(lambda a: a[:, jc, :]), srcv(lambda a: a[:, jc, :]), IDNb)
                                P.cp(o, pt, eng=evac_eng())
                            return o
                        Vb, Btk, Ktk = trans(VTb), trans(BtT), trans(KtT)

                        def mm4(pp, lhsT, rhs, **kw):
                            for jc in range(4):
                                P.mm(pp(lambda a: a[:, jc, :]), lhsT(lambda a: a[:, jc, :]), rhs(lambda a: a[:, jc, :]), **kw)

                        def acc4(pp, terms):
                            for jc in range(4):
                                for ti, (l_, r_) in enumerate(terms):
                                    P.mm(pp(lambda a: a[:, jc, :]), l_(lambda a: a[:, jc, :]), r_(lambda a: a[:, jc, :]),
                                         start=(ti == 0), stop=(ti == len(terms) - 1))

                        inv = Scope()

                        def mmask(lhsT, rhs, mask, own=None):
                            o = g4((own or sk).take(SMB))
                            with Scope() as s2:
                                pp = pbank(s2)
                                mm4(pp, lhsT, rhs)
                                P.tt(o, pp, bc4(mask), ALU.mult)
                            return o
                        X = mmask(BtT, AtT, MSU64, own=inv)
                        Xt = mmask(AtT, BtT, MSL64, own=inv)
                        AakT = mmask(KtT, AtT, MSU64)
                        RbT = mmask(BtT, RtT, MU64)
                        RkT = mmask(KtT, RtT, MU64)
                        N = g4(inv.take(SMB))
                        P.tt(N, bc4(IDN), X, ALU.add)
                        Pm, Pt = X, Xt
                        for k in range(1, 6):
                            nxt = Scope()
                            with Scope() as s2:
                                P2 = None
                                if k < 5:
                                    pp = pbank(s2)
                                    mm4(pp, Pt, Pm)
                                    P2 = g4(nxt.take(SMB))
                                    P.cp(P2, pp, eng=evac_eng())
                                pp2 = pbank(s2)
                                mm4(pp2, Pm, Pt)
                                Pt2 = g4(nxt.take(SMB))
                                P.cp(Pt2, pp2, eng=evac_eng())
                                pn_ = pbank(s2)
                                mm4(pn_, Pt2, N)
                                N2 = g4(nxt.take(SMB))
                                P.tt(N2, N, pn_, ALU.add)
                            inv.__exit__()
                            inv = nxt
                            Pm, Pt, N = P2, Pt2, N2
                        with Scope() as s2:
                            pr0 = pbank(s2)
                            acc4(pr0, [(AtT, Tvb), (AakT, Vb)])
                            R0 = g4(sk.take(SMB))
                            P.cp(R0, pr0, eng="scalar")
                            pu = pbank(s2)
                            mm4(pu, N, R0)
                            U = g4(sk.take(SMB))
                            P.cp(U, pu, eng="scalar")
                        inv.__exit__()
                        with Scope() as s2:
                            py = pbank(s2)
                            acc4(py, [(Tvb, RtT), (U, RbT), (Vb, RkT)])
                            for hh in range(2):
                                pr = slice(hh * 64, (hh + 1) * 64)
                                P.cp(yr(lambda a: a[pr, :, wsl]), py(lambda a: a[pr, :, hh * 64:(hh + 1) * 64]), eng="scalar")
                            pS = pbank(s2)
                            acc4(pS, [(Btk, U), (Ktk, Vb)])
                            P.tt(Tv, Tv, pS, ALU.add)
                            P.tt(Tv, Tv, Wt(lambda a: a[:, :, cc * 64 + 63:cc * 64 + 64].to_broadcast([128, 4, 128])), ALU.mult)
                            P.cp(Tvb, Tv, eng="scalar")
              if True:
                with Scope() as s2:
                    pm = pbank(s2)
                    for jc in range(4):
                        P.mm(pm(lambda a: a[:, jc, :]), BONES, yr(lambda a: a[:, jc, :]))
                    hc = g4(s2.take(BIG))
                    P.stt(hc, pm, -1.0 / 64, yr, ALU.mult, ALU.add)
                    sq = g4(s2.take(BIG))
                    P.act(sq, hc, AF.Square)
                    pv = pbank(s2)
                    for jc in range(4):
                        P.mm(pv(lambda a: a[:, jc, :]), BONES, sq(lambda a: a[:, jc, :]))
                    P.act(sq, pv, AF.Ln, bias=64e-5, scale=1.0 / 64)
                    P.act(sq, sq, AF.Exp, scale=-0.5)
                    P.tt(hc, hc, sq, ALU.mult)
                    P.tt(hc, hc, prmb(P_RLNG), ALU.mult)
                    P.tt(hc, hc, prmb(P_RLNB), ALU.add)
                    P.tt(sq, rview, prmb(P_RRK), ALU.mult)
                    P.tt(sq, sq, kview, ALU.mult)
                    pb_ = pbank(s2)
                    for jc in range(4):
                        P.mm(pb_(lambda a: a[:, jc, :]), BONES, sq(lambda a: a[:, jc, :]))
                    P.tt(sq, pb_, vview, ALU.mult)
                    P.tt(hc, hc, sq, ALU.add)
                    P.tt(V(yT[:, :, sl], [("y", k) for k in range(4)]), hc, zview, ALU.mult)

    MIX = {"gla": gla, "rwkv": rwkv, "ssd": ssd, "mlstm": mlstm}

    def finish():
        fin = {}
        for key, cnt in P.dma_cnt.items():
            if key[1].startswith("out_") or key[1].startswith("dbg_"):
                fin[key] = cnt
        P._emit_waits("gpsimd", fin)
        P.emit()
        P.close()
        for cm in reversed(cms):
            cm.__exit__(None, None, None)
        return nc, P

    if upto == 'init':
        return finish()
    for tile in range(n_tiles):
        tok0 = tile * T
        for s in range(NSUB):
            for jq in range(4):
                with Scope() as sc:
                    xi = sc.take(BIG)
                    r0 = tok0 + s * 128
                    P.dma("gpsimd", "xin_%s" % str(xi.keys[0][1]), xi.ap, x_d[r0:r0 + 128, jq * 512:(jq + 1) * 512],
                          writes=xi.keys)
                    pp = sc.take(PB)
                    for q in range(4):
                        P.tr(pp(lambda a: a[:, q * 128:(q + 1) * 128]), xi(lambda a: a[:, q * 128:(q + 1) * 128]), IDN)
                    P.cp(V(xT[:, jq * 4:(jq + 1) * 4, s * 128:(s + 1) * 128], [("xT", jq * 4 + q) for q in range(4)]),
                         pp(lambda a: a.rearrange("p (q t) -> p q t", q=4)), eng=evac_eng())
        if upto == 'xload':
            return finish()
        for l in range(depth):
            P.dma("gpsimd", "gkw", gkw1[:], gkw_d[l], writes=["gkw"])
            P.dma("gpsimd", "rw2", rw21[:], rw2_d[l], writes=["rw2"])
            P.dma("gpsimd", "ra2", ra21[:], ra2_d[l], writes=["ra2"])
            rmsnorm(P_NG, l, lambda j: V(hT[:, j, :], [("hT", j)]))
            for m in mixers:
                if m == "gla":
                    gla_pre(l)
                inproj(l, m)
                if upto == 'inproj':
                    return finish()
                MIX[m](l)
                if debug:
                    mi = MIXN.index(m)
                    for k in range(4):
                        with Scope() as sc:
                            yk = sc.take(BIG)
                            P.cp(yk, V(yT[:, k, :], [("y", k)]))
                            P.dma("gpsimd", "dbg_%s" % str(yk.keys[0][1]),
                                  dbg_d[l, mi * 512 + k * 128:mi * 512 + (k + 1) * 128, tok0:tok0 + T],
                                  yk.ap, reads=yk.keys, writes=["dbgout"])
                outproj(l, m)
        if upto == 'norm':
            return finish()
        rmsnorm(P_FNG, depth - 1, lambda j: cb(j))
        if upto == 'fnorm':
            return finish()
        for s in range(NSUB):
            for jq in range(4):
                with Scope() as sc:
                    pp = sc.take(PB)
                    for q in range(4):
                        P.tr(pp(lambda a: a[:, q * 128:(q + 1) * 128]), cbs(jq * 4 + q, s), IDN)
                    xo = sc.take(BIG)
                    P.cp(xo, pp, eng=evac_eng())
                    r0 = tok0 + s * 128
                    P.dma("gpsimd", "out_%s" % str(xo.keys[0][1]), out_d[r0:r0 + 128, jq * 512:(jq + 1) * 512], xo.ap,
                          reads=xo.keys, writes=["outd"])
    return finish()


def make_in_maps(inputs, batch_of_core):
    cst = make_consts()
    shared = {"cst": cst}
    for l in range(2):
        g = lambda k: np.ascontiguousarray(np.asarray(inputs["%s_%d" % (k, l)], np.float32))
        shared["win%d" % l] = g("w_in")
        shared["wout%d" % l] = g("w_out")
        shared["prm%d" % l] = pack_params(inputs, l)
        shared["gkw%d" % l] = np.concatenate([g("gla_gk_w2"), g("gla_gk_b")[None, :]], axis=0)
        shared["rw2%d" % l] = np.concatenate([g("rwkv_w2"), g("rwkv_w0")[None, :]], axis=0)
        shared["ra2%d" % l] = np.concatenate([np.concatenate([g("rwkv_a2"), g("rwkv_a0")[None, :]], axis=0), np.concatenate([g("rwkv_v2") if l > 0 else np.zeros((32, 512), np.float32), np.zeros((33, 512), np.float32)], axis=0)], axis=1)
    x = np.asarray(inputs["x"], np.float32)
    maps = []
    for b in batch_of_core:
        d = dict(shared)
        d["x"] = np.ascontiguousarray(x[b])
        maps.append(d)
    return maps


_INPUT_NAMES = (
    'x',
    'norm_g_0',
    'w_in_0',
    'w_out_0',
    'gla_gk_w2_0',
    'gla_gk_b_0',
    'gla_norm_g_0',
    'rwkv_mu_0',
    'rwkv_w0_0',
    'rwkv_w2_0',
    'rwkv_a0_0',
    'rwkv_a2_0',
    'rwkv_k_k_0',
    'rwkv_k_a_0',
    'rwkv_r_k_0',
    'rwkv_ln_g_0',
    'rwkv_ln_b_0',
    'ssd_conv_w_0',
    'ssd_conv_b_0',
    'ssd_dt_bias_0',
    'ssd_a_log_0',
    'ssd_d_0',
    'ssd_norm_g_0',
    'mlstm_conv_w_0',
    'mlstm_conv_b_0',
    'mlstm_ig_b_0',
    'mlstm_fg_b_0',
    'mlstm_norm_g_0',
    'norm_g_1',
    'w_in_1',
    'w_out_1',
    'gla_gk_w2_1',
    'gla_gk_b_1',
    'gla_norm_g_1',
    'rwkv_mu_1',
    'rwkv_w0_1',
    'rwkv_w2_1',
    'rwkv_a0_1',
    'rwkv_a2_1',
    'rwkv_v0_1',
    'rwkv_v2_1',
    'rwkv_k_k_1',
    'rwkv_k_a_1',
    'rwkv_r_k_1',
    'rwkv_ln_g_1',
    'rwkv_ln_b_1',
    'ssd_conv_w_1',
    'ssd_conv_b_1',
    'ssd_dt_bias_1',
    'ssd_a_log_1',
    'ssd_d_1',
    'ssd_norm_g_1',
    'mlstm_conv_w_1',
    'mlstm_conv_b_1',
    'mlstm_ig_b_1',
    'mlstm_fg_b_1',
    'mlstm_norm_g_1',
    'final_norm_g',
)


_CACHE = {}


def kernel(**inputs):
    inputs = {n: inputs[n] for n in _INPUT_NAMES}
    if "nc" not in _CACHE:
        _CACHE["nc"] = build_program()[0]
    nc = _CACHE["nc"]
    batch_of_core = [i // 2 for i in range(8)]
    maps = make_in_maps(inputs, batch_of_core)
    res = run_bass_kernel_spmd(nc, maps, core_ids=list(range(8)))
    out = np.stack([res.results[2 * b]["out"] for b in range(4)], axis=0)
    return out.astype(np.float32)
```

```python
import numpy as np
import concourse.bass as bass
import concourse.mybir as mybir
from concourse.bass_utils import run_bass_kernel_spmd

F32 = mybir.dt.float32
F32R = mybir.dt.float32r
BF16 = mybir.dt.bfloat16
AF = mybir.ActivationFunctionType
ALU = mybir.AluOpType

D = 2048
SEQ = 2048
T = 512
NSUB = T // 128
PADC = 4
CW = T + PADC
GW = 512
NIN = (7840, 7872)
EPS = 1e-6


def in_layout(l):
    rs = 3 * GW + 64 + 64 + (32 if l > 0 else 0)
    return (('gla_q', 256), ('gla_k', 256), ('gla_v', 512), ('gla_gk', 16), ('gla_z', 512),
            ('rwkv_shift', rs), ('rwkv_z', 512), ('ssd_xbc', 1024), ('ssd_dt', 8), ('ssd_z', 512),
            ('mlstm_qk', 1024), ('mlstm_v', 512), ('mlstm_i', 4), ('mlstm_f', 4), ('mlstm_o', 512),
            ('mlstm_z', 512))


def col_offsets(l):
    off = {}
    c = 0
    for n, w in in_layout(l):
        off[n] = c
        c += w
    return off


_o = [0]


def _al(n):
    r = _o[0]
    _o[0] += n
    return r


P_NG = _al(16); P_GLAG = _al(4); P_RMU = _al(15); P_RA0 = _al(4); P_RKK = _al(4); P_RKA = _al(4); P_RRK = _al(4)
P_RLNG = _al(4); P_RLNB = _al(4); P_RV0 = _al(4); P_SCW = _al(32); P_SCB = _al(8); P_SDTB = _al(1); P_SALOG = _al(1)
P_SDCH = _al(4); P_SNG = _al(4); P_MCW = _al(32); P_MCB = _al(8); P_MIFB = _al(1); P_MNG = _al(4); P_FNG = _al(16)
NPRM = _o[0]
C_IDN, C_ONES, C_TRI, C_SGT, C_MU64, C_MSU64, C_MSL64, C_BONES, NCST = 0, 128, 256, 384, 512, 640, 768, 896, 1024


def cols128(v):
    v = np.asarray(v, np.float32).reshape(-1)
    n = (len(v) + 127) // 128
    o = np.zeros((n * 128,), np.float32)
    o[:len(v)] = v
    return o.reshape(n, 128).T


def pack_params(inp, l):
    g = lambda k: np.asarray(inp["%s_%d" % (k, l)], np.float32)
    prm = np.zeros((128, NPRM), np.float32)
    prm[:, P_NG:P_NG + 16] = cols128(g("norm_g"))
    prm[:, P_GLAG:P_GLAG + 4] = cols128(g("gla_norm_g"))
    mu = g("rwkv_mu")
    prm[:, P_RMU:P_RMU + 12] = cols128(mu[:1536])
    prm[:64, P_RMU + 12] = mu[1536:1600]
    prm[:64, P_RMU + 13] = mu[1600:1664]
    if l > 0:
        prm[:32, P_RMU + 14] = mu[1664:1696]
        prm[:, P_RV0:P_RV0 + 4] = cols128(g("rwkv_v0"))
    prm[:, P_RA0:P_RA0 + 4] = cols128(g("rwkv_a0"))
    prm[:, P_RKK:P_RKK + 4] = cols128(g("rwkv_k_k"))
    prm[:, P_RKA:P_RKA + 4] = cols128(g("rwkv_k_a"))
    prm[:, P_RRK:P_RRK + 4] = cols128(g("rwkv_r_k"))
    prm[:, P_RLNG:P_RLNG + 4] = cols128(g("rwkv_ln_g"))
    prm[:, P_RLNB:P_RLNB + 4] = cols128(g("rwkv_ln_b"))
    cw = g("ssd_conv_w")
    for j in range(4):
        prm[:, P_SCW + j * 8:P_SCW + j * 8 + 8] = cols128(cw[j])
    prm[:, P_SCB:P_SCB + 8] = cols128(g("ssd_conv_b"))
    prm[:8, P_SDTB] = g("ssd_dt_bias")
    prm[:8, P_SALOG] = g("ssd_a_log")
    prm[:, P_SDCH:P_SDCH + 4] = cols128(np.repeat(g("ssd_d"), 64))
    prm[:, P_SNG:P_SNG + 4] = cols128(g("ssd_norm_g"))
    cw = g("mlstm_conv_w")
    for j in range(4):
        prm[:, P_MCW + j * 8:P_MCW + j * 8 + 8] = cols128(cw[j])
    prm[:, P_MCB:P_MCB + 8] = cols128(g("mlstm_conv_b"))
    prm[:4, P_MIFB] = g("mlstm_ig_b")
    prm[4:8, P_MIFB] = g("mlstm_fg_b")
    prm[:, P_MNG:P_MNG + 4] = cols128(g("mlstm_norm_g"))
    prm[:, P_FNG:P_FNG + 16] = cols128(np.asarray(inp["final_norm_g"], np.float32))
    return prm


def make_consts():
    c = np.zeros((128, NCST), np.float32)
    i = np.arange(128)
    r, cc = i[:, None], i[None, :]
    same = (r // 64) == (cc // 64)
    c[:, C_IDN:C_IDN + 128] = (r == cc)
    c[:, C_ONES:C_ONES + 128] = 1.0
    c[:, C_TRI:C_TRI + 128] = (r <= cc)
    c[:, C_SGT:C_SGT + 128] = (r > cc)
    c[:, C_MU64:C_MU64 + 128] = same & (r <= cc)
    c[:, C_MSU64:C_MSU64 + 128] = same & (r < cc)
    c[:, C_MSL64:C_MSL64 + 128] = same & (r > cc)
    c[:, C_BONES:C_BONES + 128] = same
    return c


class V:
    def __init__(self, ap, keys, excl=False):
        self.ap = ap
        self.keys = tuple(keys)
        self.excl = excl

    def __call__(self, f):
        v = V(f(self.ap), self.keys, self.excl)
        if hasattr(self, "base"):
            v.base = self.base
        return v


class Prog:
    ENGS = ("sync", "scalar", "vector", "gpsimd", "tensor")

    def __init__(self, nc):
        self.nc = nc
        self.ops = {e: [] for e in self.ENGS}
        self.count = {e: 0 for e in self.ENGS}
        self.waited = {e: {} for e in self.ENGS}
        self.lastw = {}
        self.readers = {}
        self.dma_cnt = {}
        self.sems = {}
        self.ctx = []
        self.nops = 0
        self.limit = None
        self.log = []

    def sem(self, key):
        if key not in self.sems:
            nm = "s_" + "".join(ch for ch in str(key) if ch.isalnum())
            cm = self.nc.semaphore(nm)
            self.sems[key] = cm.__enter__()
            self.ctx.append(cm)
        return self.sems[key]

    def _deps(self, reads, writes):
        deps = {}

        def add(d):
            if d is not None and deps.get(d[0], 0) < d[1]:
                deps[d[0]] = d[1]
        for k in reads:
            add(self.lastw.get(k))
        for k in writes:
            add(self.lastw.get(k))
            for r in self.readers.get(k, ()):
                add(r)
        return deps

    def _emit_waits(self, eng, deps):
        w = self.waited[eng]
        for k, v in deps.items():
            if eng == "tensor" and k == "tensor":
                continue
            if w.get(k, 0) >= v:
                continue
            w[k] = v
            s = self.sem(k)
            self.ops[eng].append(lambda e, s=s, v=v: e.wait_ge(s, v))

    def _commit(self, reads, writes, tok):
        for k in reads:
            self.readers.setdefault(k, []).append(tok)
        for k in writes:
            self.lastw[k] = tok
            self.readers[k] = []

    def op(self, eng, fn, reads=(), writes=()):
        if self.limit is not None and self.nops >= self.limit:
            return
        self._emit_waits(eng, self._deps(reads, writes))
        self.count[eng] += 1
        tok = (eng, self.count[eng])
        s = self.sem(eng)
        self.ops[eng].append(lambda e, fn=fn, s=s: fn(e).then_inc(s, 1))
        self._commit(reads, writes, tok)
        self.nops += 1
        if self.limit is not None:
            import sys as _s
            f = _s._getframe(1)
            ln = []
            while f is not None and len(ln) < 3:
                ln.append(f.f_lineno)
                f = f.f_back
            self.log.append((self.nops, eng, ln))

    def dma(self, eng, slot, out, in_, reads=(), writes=(), **kw):
        if self.limit is not None and self.nops >= self.limit:
            return
        self._emit_waits(eng, self._deps(reads, writes))
        key = ("dma", slot)
        self.dma_cnt[key] = self.dma_cnt.get(key, 0) + 16
        tok = (key, self.dma_cnt[key])
        s = self.sem(key)
        self.ops[eng].append(lambda e, s=s: e.dma_start(out=out, in_=in_, **kw).then_inc(s, 16))
        self._commit(reads, writes, tok)

    def wait_all(self, eng, keys):
        deps = {}
        for k in keys:
            d = self.lastw.get(k)
            if d is not None and deps.get(d[0], 0) < d[1]:
                deps[d[0]] = d[1]
        self._emit_waits(eng, deps)

    def emit(self):
        with self.nc.Block() as block:
            def mk(name):
                def body(e):
                    for f in self.ops[name]:
                        f(e)
                return body
            block.sync(mk("sync"))
            block.scalar(mk("scalar"))
            block.vector(mk("vector"))
            block.gpsimd(mk("gpsimd"))
            block.tensor(mk("tensor"))

    def close(self):
        for cm in reversed(self.ctx):
            cm.__exit__(None, None, None)

    @staticmethod
    def _rw(out, ins):
        r, w = (), out.keys
        for v in ins:
            if isinstance(v, V):
                if v.excl:
                    w = w + v.keys
                else:
                    r = r + v.keys
        return r, w

    @staticmethod
    def _a(x):
        return x.ap if isinstance(x, V) else x

    def mm(self, out, lhsT, rhs, start=True, stop=True):
        r, w = self._rw(out, (lhsT, rhs))
        self.op("tensor", lambda e: e.matmul(out.ap, lhsT.ap, rhs.ap, start=start, stop=stop), reads=r, writes=w)

    def tr(self, out, in_, idn):
        r, w = self._rw(out, (in_, idn))
        self.op("tensor", lambda e: e.transpose(out.ap, in_.ap, idn.ap), reads=r, writes=w)

    def act(self, out, in_, func, bias=None, scale=None):
        kw = {}
        if bias is not None:
            kw["bias"] = self._a(bias)
        if scale is not None:
            kw["scale"] = scale
        r, w = self._rw(out, (in_, bias))
        self.op("scalar", lambda e: e.activation(out.ap, in_.ap, func, **kw), reads=r, writes=w)

    def tt(self, out, a, b, op, eng="vector"):
        r, w = self._rw(out, (a, b))
        self.op(eng, lambda e: e.tensor_tensor(out.ap, a.ap, b.ap, op), reads=r, writes=w)

    def ts(self, out, a, s1, s2=None, op0=ALU.mult, op1=None, eng="vector"):
        r, w = self._rw(out, (a, s1, s2))
        s1, s2 = self._a(s1), self._a(s2)
        if op1 is None:
            self.op(eng, lambda e: e.tensor_scalar(out.ap, a.ap, s1, None, op0), reads=r, writes=w)
        else:
            self.op(eng, lambda e: e.tensor_scalar(out.ap, a.ap, s1, s2, op0, op1), reads=r, writes=w)

    def stt(self, out, a, s, b, op0, op1, eng="vector"):
        r, w = self._rw(out, (a, s, b))
        s = self._a(s)
        self.op(eng, lambda e: e.scalar_tensor_tensor(out.ap, a.ap, s, b.ap, op0, op1), reads=r, writes=w)

    def cp(self, out, a, eng="vector"):
        r, w = self._rw(out, (a,))
        if eng == "scalar":
            self.op("scalar", lambda e: e.copy(out.ap, a.ap), reads=r, writes=w)
        else:
            self.op(eng, lambda e: e.tensor_copy(out.ap, a.ap), reads=r, writes=w)

    def recip(self, out, a):
        r, w = self._rw(out, (a,))
        self.op("vector", lambda e: e.reciprocal(out.ap, a.ap), reads=r, writes=w)

    def memset(self, out, val, eng="vector"):
        self.op(eng, lambda e: e.memset(out.ap, val), writes=out.keys)


class Pool:
    def __init__(self, name, views):
        self.name = name
        self.free = list(views)

    def get(self):
        if not self.free:
            raise RuntimeError("pool %s exhausted" % self.name)
        return self.free.pop(0)

    def put(self, v):
        self.free.append(v)


class DerivedPool:
    def __init__(self, base, fn):
        self.base = base
        self.fn = fn
        self.name = base.name

    def get(self):
        b = self.base.get()
        v = V(self.fn(b.ap), b.keys, b.excl)
        v.base = b
        return v

    def put(self, v):
        self.base.put(v.base)


class Scope:
    def __init__(self):
        self.items = []

    def take(self, pool):
        v = pool.get()
        self.items.append((pool, v))
        return v

    def __enter__(self):
        return self

    def __exit__(self, *a):
        for pool, v in self.items:
            pool.put(v)
        self.items = []
        return False


MIXN = ("gla", "rwkv", "ssd", "mlstm")
NCH = 21


def build_program(n_tiles=4, depth=2, mixers=MIXN, debug=False, upto='all', limit=None):
    nc = bass.Bass("TRN2", target_bir_lowering=False)
    nc.dge_precook = False
    x_d = nc.dram_tensor("x", [SEQ, D], F32, kind="ExternalInput").ap()
    cst_d = nc.dram_tensor("cst", [128, NCST], F32, kind="ExternalInput").ap()
    win_d, wout_d, prm_d, gkw_d, rw2_d, ra2_d = [], [], [], [], [], []
    for l in range(2):
        win_d.append(nc.dram_tensor("win%d" % l, [D, NIN[l]], F32R, kind="ExternalInput").ap())
        wout_d.append(nc.dram_tensor("wout%d" % l, [D, D], F32R, kind="ExternalInput").ap())
        prm_d.append(nc.dram_tensor("prm%d" % l, [128, NPRM], F32, kind="ExternalInput").ap())
        gkw_d.append(nc.dram_tensor("gkw%d" % l, [17, 256], F32, kind="ExternalInput").ap())
        rw2_d.append(nc.dram_tensor("rw2%d" % l, [65, 512], F32, kind="ExternalInput").ap())
        ra2_d.append(nc.dram_tensor("ra2%d" % l, [65, 1024], F32, kind="ExternalInput").ap())
    out_d = nc.dram_tensor("out", [SEQ, D], F32, kind="ExternalOutput").ap()
    dbg_d = nc.dram_tensor("dbg", [2, D, SEQ], F32, kind="ExternalOutput").ap() if debug else None

    P = Prog(nc)
    P.limit = limit
    cms = []

    def sb(name, shape, dt=F32):
        cm = nc.sbuf_tensor("sb_" + name, shape, dt)
        cms.append(cm)
        return cm.__enter__()

    def ps(name, shape, dt=F32):
        cm = nc.psum_tensor("ps_" + name, shape, dt)
        cms.append(cm)
        return cm.__enter__()

    cst = sb("cst", [128, NCST])
    prm = [sb("prm%d" % l, [128, NPRM]) for l in range(2)]
    gkw1 = sb("gkw", [17, 256])
    rw21 = sb("rw2", [65, 512])
    ra21 = sb("ra2", [65, 1024])
    gkw, rw2, ra2 = [gkw1, gkw1], [rw21, rw21], [ra21, ra21]
    xT = sb("xT", [128, 16, T])
    hT = sb("hT", [128, 16, T], F32R)
    NW = 4
    WB = 256
    wbuf = [sb("wbuf%d" % i, [128, 4 * WB], F32R) for i in range(NW)]
    cbuf = sb("cbuf", [128, NCH, CW])
    yT = sb("yT", [128, 4, T], F32R)
    vfirst = sb("vfirst", [128, 4, T])
    bd = [sb("bd%d" % q, [128, 512], BF16) for q in range(5)]
    idnb = sb("idnb", [128, 128], BF16)
    negA = [sb("negA%d" % l, [8, 1]) for l in range(2)]
    Sgla = [sb("Sgla%d" % l, [128, 256]) for l in range(2)]
    Sssd = [sb("Sssd%d" % l, [128, 512]) for l in range(2)]
    Sssdb = [sb("Sssdb%d" % l, [128, 512], BF16) for l in range(2)]
    Cml = [sb("Cml%d" % l, [128, 512]) for l in range(2)]
    Cmlb = [sb("Cmlb%d" % l, [128, 512], BF16) for l in range(2)]
    nml = [sb("nml%d" % l, [128, 4]) for l in range(2)]
    nmlb = [sb("nmlb%d" % l, [128, 4], BF16) for l in range(2)]
    onesb = sb("onesb", [128, 128], BF16)
    Trw = [sb("Trw%d" % l, [128, 512]) for l in range(2)]
    Trwb = [sb("Trwb%d" % l, [128, 512], BF16) for l in range(2)]
    car_s = [sb("car_s%d" % l, [128, 8, 4]) for l in range(2)]
    car_m = [sb("car_m%d" % l, [128, 8, 4]) for l in range(2)]
    car_r = [sb("car_r%d" % l, [128, 15, 4]) for l in range(2)]
    NBIG, NSM, NU = 6, 24, 16
    arena = sb("arena", [128, NU * 512], BF16)
    BIG = Pool("big", [V(sb("big%d" % i, [128, T])[:], [("big", i)]) for i in range(NBIG)])
    SM = Pool("sm", [V(arena[:, (i // 2) * 512 + (i % 2) * 256:(i // 2) * 512 + (i % 2) * 256 + 256].bitcast(F32),
                       [("ar", i // 2)]) for i in range(NSM)])
    SMB = Pool("smb", [V(arena[:, u * 512:(u + 1) * 512], [("ar", u)]) for u in range(NU)])
    TK = Pool("tok8", [V(sb("tok8_%d" % i, [128, 16])[:], [("tok8", i)]) for i in range(6)])
    pb_t = [ps("pb%d" % i, [128, 512]) for i in range(8)]
    PB = Pool("pb", [V(pb_t[i][:], [("pb", i)], excl=True) for i in range(8)])
    PH = DerivedPool(PB, lambda a: a[:, 0:256])
    PQ = DerivedPool(PB, lambda a: a[:, 0:128])

    st = {"ev": 0}

    def evac_eng():
        st["ev"] += 1
        return "scalar" if st["ev"] % 2 else "vector"

    def C(off, n=128, p0=0, p1=128):
        return V(cst[p0:p1, off:off + n], ["cst"])

    IDN, ONES, TRI, SGT = C(C_IDN), C(C_ONES), C(C_TRI), C(C_SGT)
    MU64, MSU64, MSL64, BONES = C(C_MU64), C(C_MSU64), C(C_MSL64), C(C_BONES)

    def SEL(h, k):
        return V(cst[0:k, C_IDN + h:C_IDN + h + 1].to_broadcast([k, 128]), ["cst"])

    def prmc(l, col, p1=128, p0=0):
        return V(prm[l][p0:p1, col:col + 1], [("prm", l)])

    def cb(j, p1=128, p0=0):
        return V(cbuf[p0:p1, j, PADC:CW], [("c", j)])

    def cbs(j, s, p0=0, p1=128):
        return V(cbuf[p0:p1, j, PADC + s * 128:PADC + (s + 1) * 128], [("c", j)])

    P.dma("gpsimd", "cst", cst[:], cst_d, writes=["cst"])
    for l in range(depth):
        P.dma("gpsimd", "prm%d" % l, prm[l][:], prm_d[l], writes=[("prm", l)])
    P.memset(V(cbuf[:], [("c", j) for j in range(NCH)]), 0.0)
    for q in range(5):
        P.memset(V(bd[q][:], [("bd", q)]), 0.0)
    P.cp(V(idnb[:], ["idnb"]), IDN)
    P.memset(V(onesb[:], ["onesb"]), 1.0)
    for l in range(depth):
        P.memset(V(Sgla[l][:], [("Sgla", l)]), 0.0)
        P.memset(V(Sssd[l][:], [("Sssd", l)]), 0.0)
        P.memset(V(Sssdb[l][:], [("Sssdb", l)]), 0.0)
        P.memset(V(Cml[l][:], [("Cml", l)]), 0.0)
        P.memset(V(Cmlb[l][:], [("Cmlb", l)]), 0.0)
        P.memset(V(nml[l][:], [("nml", l)]), 0.0)
        P.memset(V(nmlb[l][:], [("nmlb", l)]), 0.0)
        P.memset(V(Trw[l][:], [("Trw", l)]), 0.0)
        P.memset(V(Trwb[l][:], [("Trwb", l)]), 0.0)
        P.memset(V(car_s[l][:], [("car_s", l, j) for j in range(8)]), 0.0)
        P.memset(V(car_m[l][:], [("car_m", l, j) for j in range(8)]), 0.0)
        P.memset(V(car_r[l][:], [("car_r", l, j) for j in range(15)]), 0.0)
        P.act(V(negA[l][:], [("negA", l)]), prmc(l, P_SALOG, 8), AF.Exp)
        P.ts(V(negA[l][:], [("negA", l)]), V(negA[l][:], [("negA", l)]), -1.0)

    def plan(l, m):
        off = col_offsets(l)
        it = []
        if m == "gla":
            for j in range(2):
                it.append((off['gla_q'] + 128 * j, 128, j, 0))
            for j in range(2):
                it.append((off['gla_k'] + 128 * j, 128, 2 + j, 0))
            for j in range(4):
                it.append((off['gla_z'] + 128 * j, 128, 4 + j, 0))
            for j in range(4):
                it.append((off['gla_v'] + 128 * j, 128, 8 + j, 0))
            it.append((off['gla_gk'], 16, 12, 0))
        elif m == "rwkv":
            b = off['rwkv_shift']
            for j in range(12):
                it.append((b + 128 * j, 128, j, 0))
            it.append((b + 1536, 64, 12, 0))
            it.append((b + 1600, 64, 13, 0))
            if l > 0:
                it.append((b + 1664, 32, 14, 0))
            for j in range(4):
                it.append((off['rwkv_z'] + 128 * j, 128, 15 + j, 0))
        elif m == "ssd":
            for j in range(8):
                it.append((off['ssd_xbc'] + 128 * j, 128, j, 0))
            for j in range(4):
                it.append((off['ssd_z'] + 128 * j, 128, 8 + j, 0))
            it.append((off['ssd_dt'], 8, 12, 0))
        elif m == "mlstm":
            for j in range(8):
                it.append((off['mlstm_qk'] + 128 * j, 128, j, 0))
            for j in range(4):
                it.append((off['mlstm_o'] + 128 * j, 128, 8 + j, 0))
            for j in range(4):
                it.append((off['mlstm_z'] + 128 * j, 128, 12 + j, 0))
            for j in range(4):
                it.append((off['mlstm_v'] + 128 * j, 128, 16 + j, 0))
            it.append((off['mlstm_i'], 8, 20, 0))
        return it

    plans = {(l, m): plan(l, m) for l in range(depth) for m in mixers}

    def groups_of(l, m):
        gs = []
        for it in plans[(l, m)]:
            c0, n = it[0], it[1]
            if gs and gs[-1][0] + gs[-1][1] == c0 and gs[-1][1] + n <= WB:
                gs[-1][2].append((gs[-1][1],) + tuple(it[1:]))
                gs[-1][1] += n
            else:
                gs.append([c0, n, [(0,) + tuple(it[1:])]])
        return gs

    gplans = {(l, m): groups_of(l, m) for l in range(depth) for m in mixers}
    wq = []
    for t in range(n_tiles):
        for l in range(depth):
            for m in mixers:
                for (c0, n, chunks) in gplans[(l, m)]:
                    for kg in range(4):
                        wq.append((win_d[l][kg * 512:(kg + 1) * 512, c0:c0 + n].rearrange("(k p) c -> p k c", p=128), (4, n)))
                mi = MIXN.index(m)
                for q in range(2048 // WB):
                    wq.append((wout_d[l][mi * 512:(mi + 1) * 512, q * WB:(q + 1) * WB]
                               .rearrange("(k p) c -> p k c", p=128), (4, WB)))
    wstate = {"issued": 0, "used": 0}

    def w_issue():
        i = wstate["issued"]
        src, (k, n) = wq[i]
        slot = i % NW
        dst = wbuf[slot][:, 0:k * n].rearrange("p (k n) -> p k n", k=k)
        P.dma("sync", "w%d" % slot, dst, src, writes=[("wbuf", slot)])
        wstate["issued"] += 1

    def w_next():
        i = wstate["used"]
        while wstate["issued"] < min(i + NW, len(wq)):
            w_issue()
        wstate["used"] += 1
        return wbuf[i % NW], ("wbuf", i % NW)

    def inproj(l, m):
        hooks = post_hooks(l, m)
        for (c0, n, chunks) in gplans[(l, m)]:
            with Scope() as sc:
                pps = [sc.take(PB)(lambda a, pb=pb, cn=cn: a[pb:pb + cn, :]) for (o, cn, j, pb) in chunks]
                for kg in range(4):
                    wt, wkey = w_next()
                    w3 = wt[:, 0:4 * n].rearrange("p (k n) -> p k n", k=4)
                    for ci, (o, cn, j, pb) in enumerate(chunks):
                        for k in range(4):
                            P.mm(pps[ci], V(w3[:, k, o:o + cn], [wkey]), V(hT[:, kg * 4 + k, :], [("hT", kg * 4 + k)]),
                                 start=(kg == 0 and k == 0), stop=(kg == 3 and k == 3))
                for ci, (o, cn, j, pb) in enumerate(chunks):
                    P.cp(cb(j, p0=pb, p1=pb + cn), pps[ci], eng=evac_eng())
            for ci, (o, cn, j, pb) in enumerate(chunks):
                if j in hooks:
                    hooks[j]()

    def outproj(l, m):
        for q in range(2048 // WB):
            wt, wkey = w_next()
            w3 = wt[:, 0:4 * WB].rearrange("p (k n) -> p k n", k=4)
            for oc in range(WB // 128):
                with Scope() as sc:
                    pp = sc.take(PB)
                    for k in range(4):
                        P.mm(pp, V(w3[:, k, oc * 128:(oc + 1) * 128], [wkey]), V(yT[:, k, :], [("y", k)]),
                             start=(k == 0), stop=(k == 3))
                    j = q * (WB // 128) + oc
                    xv = V(xT[:, j, :], [("xT", j)])
                    P.tt(xv, xv, pp, ALU.add)

    def rmsnorm(gcol, l, dst_fn):
        with Scope() as sc:
            pp = sc.take(PB)
            for j in range(16):
                with Scope() as s2:
                    sq = s2.take(BIG)
                    P.act(sq, V(xT[:, j, :], [("xT", j)]), AF.Square)
                    P.mm(pp, ONES, sq, start=(j == 0), stop=(j == 15))
            rs = sc.take(BIG)
            P.act(rs, pp, AF.Ln, bias=EPS, scale=1.0 / D)
            P.act(rs, rs, AF.Exp, scale=-0.5)
            for j in range(16):
                P.stt(dst_fn(j), V(xT[:, j, :], [("xT", j)]), prmc(l, gcol + j), rs, ALU.mult, ALU.mult)

    def conv_one(l, car, cname, wcol, bcol, j):
        ck = (cname, l, j)
        with Scope() as sc:
            P.cp(V(cbuf[:, j, 1:4], [("c", j)]), V(car[l][:, j, 0:3], [ck]))
            acc = sc.take(BIG)
            P.ts(acc, V(cbuf[:, j, 1:1 + T], [("c", j)]), prmc(l, wcol + j))
            for tap in range(1, 4):
                P.stt(acc, V(cbuf[:, j, 1 + tap:1 + tap + T], [("c", j)]), prmc(l, wcol + tap * 8 + j), acc,
                      ALU.mult, ALU.add)
            P.cp(V(car[l][:, j, 0:3], [ck]), V(cbuf[:, j, T + 1:T + 4], [("c", j)]), eng="scalar")
            P.act(cb(j), acc, AF.Silu, bias=prmc(l, bcol + j))

    def shift_one(l, j):
        ck = ("car_r", l, j)
        p1 = 128 if j < 12 else (64 if j < 14 else 32)
        with Scope() as sc:
            P.cp(V(cbuf[0:p1, j, 3:4], [("c", j)]), V(car_r[l][0:p1, j, 0:1], [ck]), eng="scalar")
            d = sc.take(BIG)(lambda a: a[0:p1, :])
            P.tt(d, V(cbuf[0:p1, j, 3:3 + T], [("c", j)]), V(cbuf[0:p1, j, 4:4 + T], [("c", j)]), ALU.subtract)
            P.cp(V(car_r[l][0:p1, j, 0:1], [ck]), V(cbuf[0:p1, j, T + 3:T + 4], [("c", j)]), eng="scalar")
            P.stt(cb(j, p1=p1), d, prmc(l, P_RMU + j, p1), cb(j, p1=p1), ALU.mult, ALU.add)

    def post_hooks(l, m):
        h = {}
        if m == "gla":
            for j in range(4):
                h[4 + j] = lambda j=j: P.act(cb(4 + j), cb(4 + j), AF.Silu)
        elif m == "ssd":
            for j in range(8):
                h[j] = lambda j=j: conv_one(l, car_s, "car_s", P_SCW, P_SCB, j)
            for j in range(4):
                h[8 + j] = lambda j=j: P.act(cb(8 + j), cb(8 + j), AF.Silu)
        elif m == "mlstm":
            for j in range(8):
                h[j] = lambda j=j: conv_one(l, car_m, "car_m", P_MCW, P_MCB, j)
            for j in range(4):
                h[8 + j] = lambda j=j: P.act(cb(8 + j), cb(8 + j), AF.Sigmoid)
                h[12 + j] = lambda j=j: P.act(cb(12 + j), cb(12 + j), AF.Silu)
            h[20] = lambda: P.ts(cb(20, p1=8), cb(20, p1=8), prmc(l, P_MIFB, 8), None, ALU.add)
        elif m == "rwkv":
            for j in range(15):
                h[j] = lambda j=j: shift_one(l, j)

            def h12():
                shift_one(l, 12)
                P.act(cb(12, p1=64), cb(12, p1=64), AF.Tanh)
                P.memset(V(cbuf[64:65, 12, :], [("c", 12)]), 1.0)

            def h13():
                shift_one(l, 13)
                P.memset(V(cbuf[64:65, 13, :], [("c", 13)]), 1.0)
            h[12], h[13] = h12, h13
            for j in range(4):
                h[15 + j] = lambda j=j: P.act(cb(15 + j), cb(15 + j), AF.Silu)
        return h

    def gla_pre(l):
        P.memset(V(cbuf[0:32, 12, :], [("c", 12)]), 1.0)

    def gla(l):
        g4 = lambda v: v(lambda a: a.rearrange("p (g t) -> p g t", g=4))
        g2 = lambda v: v(lambda a: a.rearrange("p (g t) -> p g t", g=2))
        Sv = g2(V(Sgla[l][:], [("Sgla", l)]))
        for s in range(NSUB):
            sl = slice(s * 128, (s + 1) * 128)
            csl = slice(PADC + s * 128, PADC + (s + 1) * 128)
            ck = lambda j0, n: [("c", j0 + i) for i in range(n)]
            qv = V(cbuf[:, 0:2, csl], ck(0, 2))
            kv = V(cbuf[:, 2:4, csl], ck(2, 2))
            zv = V(cbuf[:, 4:8, csl], ck(4, 4))
            vv = V(cbuf[:, 8:12, csl], ck(8, 4))
            with Scope() as sc:
                vt = g4(sc.take(SMB))
                with Scope() as s2:
                    pvt = g4(s2.take(PB))
                    for h in range(4):
                        P.tr(pvt(lambda a: a[:, h, :]), vv(lambda a: a[:, h, :]), IDN)
                    P.cp(vt, pvt, eng="scalar")
                nb = sc.take(BIG)
                nlsv = nb(lambda a: a[:, 0:256])
                ercv = nb(lambda a: a[:, 256:512])
                with Scope() as s2:
                    pg = s2.take(PH)
                    P.mm(pg, cbs(12, s, 0, 17), V(gkw[l][:], ["gkw"]))
                    P.act(nlsv, pg, AF.Exp, scale=-1.0)
                    P.act(nlsv, nlsv, AF.Ln, bias=1.0)
                eb = sc.take(BIG)
                eq = g2(eb(lambda a: a[:, 0:256]))
                gb = sc.take(BIG)
                qg = g2(gb(lambda a: a[:, 0:256]))
                kg = g2(gb(lambda a: a[:, 256:512]))
                with Scope() as s2:
                    pc = g2(s2.take(PH))
                    for j in range(2):
                        P.mm(pc(lambda a: a[:, j, :]), nlsv(lambda a: a[:, j * 128:(j + 1) * 128]), TRI)
                    ek = g2(eb(lambda a: a[:, 256:512]))
                    P.act(eq, pc, AF.Exp, scale=-1.0 / 16)
                    P.act(ek, pc, AF.Exp, scale=1.0 / 16)
                    P.stt(qg, qv, 0.125, eq, ALU.mult, ALU.mult)
                    P.tt(kg, kv, ek, ALU.mult)
                kb = sc.take(SMB)
                kdv = kb(lambda a: a[:, 0:256])
                with Scope() as s2:
                    pr = s2.take(PH)
                    P.mm(pr, SGT, nlsv)
                    P.act(ercv, pr, AF.Exp, scale=-1.0 / 16)
                    pt = s2.take(PH)
                    for j in range(2):
                        P.tr(pt(lambda a: a[:, j * 128:(j + 1) * 128]), kv(lambda a: a[:, j, :]), IDN)
                    P.tt(kdv, pt, ercv, ALU.mult)
                attm = g4(sc.take(SMB))
                with Scope() as s2:
                    pa2 = [g2(s2.take(PH)), g2(s2.take(PH))]
                    for h in range(4):
                        j, pb = h // 2, (h % 2) * 64
                        P.mm(pa2[h % 2](lambda a: a[:, j, :]), kg(lambda a: a[pb:pb + 64, j, :]), qg(lambda a: a[pb:pb + 64, j, :]))
                    for hh in range(2):
                        P.tt(attm(lambda a: a[:, hh::2, :]), pa2[hh], TRI(lambda a: a.unsqueeze(1).to_broadcast([128, 2, 128])), ALU.mult)
                with Scope() as s2:
                    po4 = g4(s2.take(PB))
                    for h in range(4):
                        j, pb = h // 2, (h % 2) * 64
                        P.mm(po4(lambda a: a[:, h, :]), vt(lambda a: a[:, h, :]), attm(lambda a: a[:, h, :]), start=True, stop=False)
                        P.mm(po4(lambda a: a[:, h, :]), Sv(lambda a: a[pb:pb + 64, j, :]), qg(lambda a: a[pb:pb + 64, j, :]),
                             start=False, stop=True)
                    osq = g4(s2.take(BIG))
                    P.act(osq, po4, AF.Square)
                    pss = g4(s2.take(PB))
                    for h in range(4):
                        P.mm(pss(lambda a: a[:, h, :]), ONES, osq(lambda a: a[:, h, :]))
                    P.act(osq, pss, AF.Ln, bias=EPS, scale=1.0 / 128)
                    P.act(osq, osq, AF.Exp, scale=-0.5)
                    P.tt(osq, po4, osq, ALU.mult)
                    P.tt(osq, osq, V(prm[l][:, P_GLAG:P_GLAG + 4].unsqueeze(2).to_broadcast([128, 4, 128]), [("prm", l)]), ALU.mult)
                    P.tt(V(yT[:, :, sl], [("y", k) for k in range(4)]), osq, zv, ALU.mult)
                with Scope() as s2:
                    pi = s2.take(PB)
                    for j in range(2):
                        P.mm(pi(lambda a: a[:, j * 256:(j + 1) * 256]), kdv(lambda a: a[:, j * 128:(j + 1) * 128]),
                             vt(lambda a: a[:, 2 * j:2 * j + 2, :].rearrange("p g t -> p (g t)")))
                    pi3 = pi(lambda a: a.rearrange("p (j w) -> p j w", j=2))
                    for hh in range(2):
                        pb = hh * 64
                        sv = Sv(lambda a: a[pb:pb + 64, :, :])
                        P.tt(sv, sv, eq(lambda a: a[pb:pb + 64, :, 127:128].to_broadcast([64, 2, 128])), ALU.mult)
                        P.tt(sv, sv, pi3(lambda a: a[pb:pb + 64, :, hh * 128:(hh + 1) * 128]), ALU.add)

    def ssd(l):
        g4 = lambda v: v(lambda a: a.rearrange("p (g t) -> p g t", g=4))
        Sv = V(Sssd[l][:], [("Sssd", l)])
        Sb = V(Sssdb[l][:], [("Sssdb", l)])

        def bc4(v):
            return v(lambda a: a.unsqueeze(1).to_broadcast([128, 4, 128]))

        with Scope() as st_:
            dt_ = st_.take(BIG)(lambda a: a[0:8, :])
            dta_ = st_.take(BIG)(lambda a: a[0:8, :])
            P.act(dt_, cb(12, p1=8), AF.Exp, bias=prmc(l, P_SDTB, 8))
            P.act(dt_, dt_, AF.Ln, bias=1.0)
            P.ts(dta_, dt_, V(negA[l][:], [("negA", l)]))
            for s in range(NSUB):
                sl = slice(s * 128, (s + 1) * 128)
                csl = slice(PADC + s * 128, PADC + (s + 1) * 128)
                ck = lambda j0, n: [("c", j0 + i) for i in range(n)]
                xv = V(cbuf[:, 0:4, csl], ck(0, 4))
                zv = V(cbuf[:, 8:12, csl], ck(8, 4))
                with Scope() as sc:
                    dtk = sc.take(TK)
                    with Scope() as s2:
                        p1 = s2.take(PQ)
                        P.tr(p1(lambda a: a[:, 0:8]), dt_(lambda a: a[:, sl]), C(C_IDN, 8, 0, 8))
                        P.tr(p1(lambda a: a[:, 8:16]), dta_(lambda a: a[:, sl]), C(C_IDN, 8, 0, 8))
                        P.cp(dtk, p1(lambda a: a[:, 0:16]))
                    dt_tok = dtk(lambda a: a[:, 0:8])
                    dta_tok = dtk(lambda a: a[:, 8:16])
                    cumT = sc.take(BIG)(lambda a: a[0:8, 0:128])
                    ex = sc.take(TK)
                    with Scope() as s2:
                        p2 = s2.take(PQ)
                        P.mm(p2(lambda a: a[0:8, :]), dta_tok, TRI)
                        P.cp(cumT, p2(lambda a: a[0:8, :]), eng="scalar")
                        p3 = s2.take(PQ)
                        P.mm(p3(lambda a: a[:, 0:8]), ONES, dta_tok)
                        P.mm(p3(lambda a: a[:, 8:16]), SGT, dta_tok)
                        P.act(ex, p3(lambda a: a[:, 0:16]), AF.Exp)
                    dec, erev = ex(lambda a: a[:, 0:8]), ex(lambda a: a[:, 8:16])
                    xdtv, xdtwv = sc.take(SMB), sc.take(SMB)
                    h8 = lambda v: v(lambda a: a.rearrange("p (h e) -> p h e", h=8))
                    with Scope() as s2:
                        pt = s2.take(PB)
                        for jc in range(4):
                            P.tr(pt(lambda a: a[:, jc * 128:(jc + 1) * 128]), cbs(jc, s), IDN)
                        P.tt(h8(xdtv), h8(pt), dt_tok(lambda a: a.unsqueeze(2).to_broadcast([128, 8, 64])), ALU.mult)
                    P.tt(h8(xdtwv), h8(xdtv), erev(lambda a: a.unsqueeze(2).to_broadcast([128, 8, 64])), ALU.mult)
                    bc_t = sc.take(SMB)
                    btok = bc_t(lambda a: a[:, 0:256].rearrange("p (g t) -> p g t", g=2))
                    cbm = bc_t(lambda a: a[:, 256:512].rearrange("p (g t) -> p g t", g=2))
                    with Scope() as s2:
                        pt = s2.take(PH)(lambda a: a.rearrange("p (g t) -> p g t", g=2))
                        pc = s2.take(PH)(lambda a: a.rearrange("p (g t) -> p g t", g=2))
                        for g in range(2):
                            P.tr(pt(lambda a: a[:, g, :]), cbs(4 + g, s), IDN)
                            P.mm(pc(lambda a: a[:, g, :]), cbs(4 + g, s), cbs(6 + g, s))
                        P.cp(btok, pt, eng="scalar")
                        P.tt(cbm, pc, TRI(lambda a: a.unsqueeze(1).to_broadcast([128, 2, 128])), ALU.mult)
                    pyb = sc.take(PB)
                    py4 = g4(pyb)
                    for g in range(2):
                        with Scope() as s2:
                            R = g4(s2.take(BIG))
                            P.tt(R, bc4(SGT), dta_tok(lambda a: a[:, 4 * g:4 * g + 4].unsqueeze(2).to_broadcast([128, 4, 128])), ALU.mult)
                            WT = g4(s2.take(SMB))
                            with Scope() as s3:
                                pg = g4(s3.take(PB))
                                for hl in range(4):
                                    P.mm(pg(lambda a: a[:, hl, :]), R(lambda a: a[:, hl, :]), TRI)
                                LT = g4(s3.take(BIG))
                                P.act(LT, pg, AF.Exp)
                                P.tt(WT, LT, cbm(lambda a: a[:, g:g + 1, :].to_broadcast([128, 4, 128])), ALU.mult)
                            Cw = g4(s2.take(SMB))
                            with Scope() as s3:
                                pbc = g4(s3.take(PB))
                                for hl in range(4):
                                    P.mm(pbc(lambda a: a[:, hl, :]), SEL(4 * g + hl, 8), cumT)
                                ec = g4(s3.take(BIG))
                                P.act(ec, pbc, AF.Exp)
                                P.tt(Cw, cbs(6 + g, s)(lambda a: a.unsqueeze(1).to_broadcast([128, 4, 128])), ec, ALU.mult)
                            for hl in range(4):
                                h = 4 * g + hl
                                jc, pb = h // 2, (h % 2) * 64
                                po = py4(lambda a: a[pb:pb + 64, jc, :])
                                P.mm(po, xdtv(lambda a: a[:, h * 64:(h + 1) * 64]), WT(lambda a: a[:, hl, :]), start=True, stop=False)
                                P.mm(po, Sb(lambda a: a[:, h * 64:(h + 1) * 64]), Cw(lambda a: a[:, hl, :]), start=False, stop=True)
                    with Scope() as s2:
                        y2 = g4(s2.take(BIG))
                        sq = g4(s2.take(BIG))
                        P.tt(y2, xv, V(prm[l][:, P_SDCH:P_SDCH + 4].unsqueeze(2).to_broadcast([128, 4, 128]), [("prm", l)]), ALU.mult)
                        P.tt(y2, y2, py4, ALU.add)
                        P.tt(y2, y2, zv, ALU.mult)
                        P.act(sq, y2, AF.Square)
                        pss = s2.take(PQ)
                        for jc in range(4):
                            P.mm(pss, ONES, sq(lambda a: a[:, jc, :]), start=(jc == 0), stop=(jc == 3))
                        rs = s2.take(BIG)(lambda a: a[:, 0:128])
                        P.act(rs, pss, AF.Ln, bias=EPS, scale=1.0 / 512)
                        P.act(rs, rs, AF.Exp, scale=-0.5)
                        P.tt(y2, y2, V(prm[l][:, P_SNG:P_SNG + 4].unsqueeze(2).to_broadcast([128, 4, 128]), [("prm", l)]), ALU.mult)
                        P.tt(V(yT[:, :, sl], [("y", k) for k in range(4)]), y2, bc4(rs), ALU.mult)
                    with Scope() as s2:
                        pi = s2.take(PB)
                        for g in range(2):
                            P.mm(pi(lambda a: a[:, g * 256:(g + 1) * 256]), btok(lambda a: a[:, g, :]),
                                 xdtwv(lambda a: a[:, g * 256:(g + 1) * 256]))
                        P.tt(h8(Sv), h8(Sv), dec(lambda a: a.unsqueeze(2).to_broadcast([128, 8, 64])), ALU.mult)
                        P.tt(Sv, Sv, pi, ALU.add)
                        P.cp(Sb, Sv, eng="scalar")

    def mlstm(l):
        isq = float(1.0 / np.sqrt(128.0))
        g4 = lambda v: v(lambda a: a.rearrange("p (g t) -> p g t", g=4))
        C4 = g4(V(Cml[l][:], [("Cml", l)]))
        C4b = g4(V(Cmlb[l][:], [("Cmlb", l)]))
        n4 = V(nml[l][:], [("nml", l)])
        n4b = V(nmlb[l][:], [("nmlb", l)])
        ONESb = V(onesb[:], ["onesb"])

        def bc4(v):
            return v(lambda a: a.unsqueeze(1).to_broadcast([128, 4, 128]))

        def bch(v):
            return v(lambda a: a.unsqueeze(2).to_broadcast([128, 4, 128]))

        def pbank(sc):
            return g4(sc.take(PB))

        for s in range(NSUB):
            sl = slice(s * 128, (s + 1) * 128)
            csl = slice(PADC + s * 128, PADC + (s + 1) * 128)
            ck4 = lambda j0: [("c", j0 + i) for i in range(4)]
            qv = V(cbuf[:, 0:4, csl], ck4(0))
            kv = V(cbuf[:, 4:8, csl], ck4(4))
            ov = V(cbuf[:, 8:12, csl], ck4(8))
            zv = V(cbuf[:, 12:16, csl], ck4(12))
            vv = V(cbuf[:, 16:20, csl], ck4(16))
            with Scope() as sc:
                tk = sc.take(TK)
                with Scope() as s2:
                    p0 = s2.take(PQ)
                    P.tr(p0(lambda a: a[:, 0:8]), cbs(20, s, 0, 8), C(C_IDN, 8, 0, 8))
                    P.cp(tk(lambda a: a[:, 0:4]), p0(lambda a: a[:, 0:4]))
                    P.act(tk(lambda a: a[:, 4:8]), p0(lambda a: a[:, 4:8]), AF.Exp, scale=-1.0)
                    P.act(tk(lambda a: a[:, 4:8]), tk(lambda a: a[:, 4:8]), AF.Ln, bias=1.0)
                li = tk(lambda a: a[:, 0:4])
                nl = tk(lambda a: a[:, 4:8])
                cumT = sc.take(BIG)(lambda a: a[0:4, 0:128])
                with Scope() as s2:
                    p1 = s2.take(PQ)
                    P.mm(p1(lambda a: a[0:4, :]), nl, TRI)
                    P.ts(cumT, p1(lambda a: a[0:4, :]), -1.0)
                    p2 = s2.take(PQ)
                    P.mm(p2(lambda a: a[:, 0:4]), ONES, nl)
                    P.mm(p2(lambda a: a[:, 4:8]), SGT, nl)
                    P.act(tk(lambda a: a[:, 8:12]), p2(lambda a: a[:, 0:4]), AF.Exp, scale=-1.0)
                    gt = s2.take(TK)
                    P.stt(gt(lambda a: a[:, 0:4]), p2(lambda a: a[:, 4:8]), -1.0, li, ALU.mult, ALU.add)
                    P.act(tk(lambda a: a[:, 12:16]), gt(lambda a: a[:, 0:4]), AF.Exp)
                dec, wend = tk(lambda a: a[:, 8:12]), tk(lambda a: a[:, 12:16])
                vt = g4(sc.take(SMB))
                with Scope() as s2:
                    pvt = pbank(s2)
                    for h in range(4):
                        P.tr(pvt(lambda a: a[:, h, :]), vv(lambda a: a[:, h, :]), IDN)
                    P.cp(vt, pvt, eng="scalar")
                ed = g4(sc.take(SMB))
                with Scope() as s2:
                    t_ = g4(s2.take(BIG))
                    rh = g4(s2.take(BIG))
                    P.stt(t_, bc4(SGT), -1.0, bch(nl), ALU.mult, ALU.mult)
                    P.tt(rh, bc4(IDN), bch(li), ALU.mult)
                    P.tt(rh, rh, t_, ALU.add)
                    pl = pbank(s2)
                    for h in range(4):
                        P.mm(pl(lambda a: a[:, h, :]), rh(lambda a: a[:, h, :]), TRI)
                    P.act(ed, pl, AF.Exp)
                wq_ = g4(sc.take(SMB))
                with Scope() as s2:
                    pk = pbank(s2)
                    for h in range(4):
                        P.mm(pk(lambda a: a[:, h, :]), kv(lambda a: a[:, h, :]), qv(lambda a: a[:, h, :]))
                    m1 = g4(s2.take(SMB))
                    P.stt(m1, pk, isq, bc4(TRI), ALU.mult, ALU.mult)
                    P.tt(wq_, m1, ed, ALU.mult)
                qw = g4(sc.take(SMB))
                with Scope() as s2:
                    pbc = pbank(s2)
                    for h in range(4):
                        P.mm(pbc(lambda a: a[:, h, :]), SEL(h, 4), cumT)
                    wi = g4(s2.take(BIG))
                    P.act(wi, pbc, AF.Exp)
                    P.tt(qw, qv, wi, ALU.mult)
                hh_ = g4(sc.take(BIG))
                with Scope() as s2:
                    da = g4(s2.take(BIG))
                    pd = pbank(s2)
                    for h in range(4):
                        P.mm(pd(lambda a: a[:, h, :]), ONESb, wq_(lambda a: a[:, h, :]), start=True, stop=False)
                        P.mm(pd(lambda a: a[:, h, :]), n4b(lambda a: a[:, h:h + 1].to_broadcast([128, 128])),
                             qw(lambda a: a[:, h, :]), start=False, stop=True)
                    P.act(da, pd, AF.Abs)
                    P.ts(da, da, 1.0, None, ALU.max)
                    P.act(da, da, AF.Ln)
                    P.act(da, da, AF.Exp, scale=-1.0)
                    pn = pbank(s2)
                    for h in range(4):
                        P.mm(pn(lambda a: a[:, h, :]), vt(lambda a: a[:, h, :]), wq_(lambda a: a[:, h, :]), start=True, stop=False)
                        P.mm(pn(lambda a: a[:, h, :]), C4b(lambda a: a[:, h, :]), qw(lambda a: a[:, h, :]), start=False, stop=True)
                    P.tt(hh_, pn, da, ALU.mult)
                P.tt(hh_, hh_, ov, ALU.mult)
                with Scope() as s2:
                    hc = g4(s2.take(BIG))
                    sq = g4(s2.take(BIG))
                    pm = pbank(s2)
                    for h in range(4):
                        P.mm(pm(lambda a: a[:, h, :]), ONES, hh_(lambda a: a[:, h, :]))
                    P.stt(hc, pm, -1.0 / 128, hh_, ALU.mult, ALU.add)
                    P.act(sq, hc, AF.Square)
                    pv = pbank(s2)
                    for h in range(4):
                        P.mm(pv(lambda a: a[:, h, :]), ONES, sq(lambda a: a[:, h, :]))
                    P.act(sq, pv, AF.Ln, bias=EPS, scale=1.0 / 128)
                    P.act(sq, sq, AF.Exp, scale=-0.5)
                    P.tt(hc, hc, sq, ALU.mult)
                    P.tt(hc, hc, V(prm[l][:, P_MNG:P_MNG + 4].unsqueeze(2).to_broadcast([128, 4, 128]), [("prm", l)]), ALU.mult)
                    P.tt(V(yT[:, :, sl], [("y", k) for k in range(4)]), hc, zv, ALU.mult)
                with Scope() as s2:
                    kw = g4(s2.take(SMB))
                    pt = pbank(s2)
                    for h in range(4):
                        P.tr(pt(lambda a: a[:, h, :]), kv(lambda a: a[:, h, :]), IDN)
                    P.stt(kw, pt, isq, bch(wend), ALU.mult, ALU.mult)
                    pi = pbank(s2)
                    pi2 = s2.take(PQ)
                    for h in range(4):
                        P.mm(pi(lambda a: a[:, h, :]), kw(lambda a: a[:, h, :]), vt(lambda a: a[:, h, :]))
                        P.mm(pi2(lambda a: a[:, h:h + 1]), kw(lambda a: a[:, h, :]), ONESb(lambda a: a[:, 0:1]))
                    P.tt(C4, C4, bch(dec), ALU.mult)
                    P.tt(C4, C4, pi, ALU.add)
                    P.cp(C4b, C4, eng="scalar")
                    P.tt(n4, n4, dec, ALU.mult)
                    P.tt(n4, n4, pi2(lambda a: a[:, 0:4]), ALU.add)
                    P.cp(n4b, n4, eng="scalar")

    def rwkv_pre(l):
        pass

    def rwkv(l):
        vfv = lambda jc: V(vfirst[:, jc, :], [("vfirst", jc)])
        for jc in range(4):
            if l == 0:
                P.cp(vfv(jc), cb(8 + jc), eng="scalar")
            else:
                with Scope() as sc:
                    pv = sc.take(PB)
                    P.mm(pv, V(ra2[l][0:32, 512 + jc * 128:512 + (jc + 1) * 128], ["ra2"]), cb(14, p1=32))
                    sg = sc.take(BIG)
                    P.act(sg, pv, AF.Sigmoid, bias=prmc(l, P_RV0 + jc))
                    d = sc.take(BIG)
                    P.tt(d, vfv(jc), cb(8 + jc), ALU.subtract)
                    P.tt(d, d, sg, ALU.mult)
                    P.tt(cb(8 + jc), cb(8 + jc), d, ALU.add)
        LD = 0.6065306597126334
        g4 = lambda v: v(lambda a: a.rearrange("p (g t) -> p g t", g=4))
        AtT, RtT, BtT, KtT, VTb = [g4(V(bd[q][:], [("bd", q)])) for q in range(5)]
        IDNb = V(idnb[:], ["idnb"])
        Tv = g4(V(Trw[l][:], [("Trw", l)]))
        Tvb = g4(V(Trwb[l][:], [("Trwb", l)]))

        def bc4(v):
            return v(lambda a: a.unsqueeze(1).to_broadcast([128, 4, 128]))

        def prmb(col):
            return V(prm[l][:, col:col + 4].unsqueeze(2).to_broadcast([128, 4, 128]), [("prm", l)])

        def pbank(sc, bf=False):
            pp = sc.take(PB)
            if bf:
                return pp(lambda a: a.bitcast(BF16)[:, 0:512].rearrange("p (g t) -> p g t", g=4))
            return g4(pp)

        for s in range(NSUB):
            sl = slice(s * 128, (s + 1) * 128)
            csl = slice(PADC + s * 128, PADC + (s + 1) * 128)
            ck4 = lambda j0: [("c", j0 + i) for i in range(4)]
            rview = V(cbuf[:, 0:4, csl], ck4(0))
            kview = V(cbuf[:, 4:8, csl], ck4(4))
            vview = V(cbuf[:, 8:12, csl], ck4(8))
            zview = V(cbuf[:, 15:19, csl], ck4(15))
            with Scope() as so:
              yr = g4(so.take(BIG))
              with Scope() as sc:
                sgv = sc.take(BIG)
                with Scope() as s2:
                    pw = s2.take(PB)
                    P.mm(pw, cbs(12, s, 0, 65), V(rw2[l][:], ["rw2"]))
                    P.act(sgv, pw, AF.Sigmoid)
                a_ = g4(sc.take(BIG))
                kk = g4(sc.take(BIG))
                with Scope() as s2:
                    pa = pbank(s2)
                    for jc in range(4):
                        P.mm(pa(lambda a: a[:, jc, :]), V(ra2[l][0:65, jc * 128:(jc + 1) * 128], ["ra2"]), cbs(13, s, 0, 65))
                    P.act(a_, pa, AF.Sigmoid)
                P.tt(kk, kview, prmb(P_RKK), ALU.mult)
                with Scope() as s2:
                    sq = g4(s2.take(BIG))
                    P.act(sq, kk, AF.Square)
                    pn = pbank(s2)
                    for jc in range(4):
                        P.mm(pn(lambda a: a[:, jc, :]), BONES, sq(lambda a: a[:, jc, :]))
                    P.ts(sq, pn, 1e-24, None, ALU.max)
                    P.act(sq, sq, AF.Ln)
                    P.act(sq, sq, AF.Exp, scale=-0.5)
                    P.tt(kk, kk, sq, ALU.mult)
                    P.ts(sq, a_, -1.0, None, ALU.add)
                    P.tt(sq, sq, prmb(P_RKA), ALU.mult)
                    P.stt(kview, sq, 1.0, kview, ALU.add, ALU.mult)
                P.tt(a_, kk, a_, ALU.mult)
                Wt = g4(sc.take(BIG))
                iW, Wx = g4(sc.take(SMB)), g4(sc.take(SMB))
                with Scope() as s2:
                    pc1, pc2 = pbank(s2), pbank(s2)
                    for jc in range(4):
                        P.mm(pc1(lambda a: a[:, jc, :]), sgv(lambda a: a[:, jc * 128:(jc + 1) * 128]), MU64)
                        P.mm(pc2(lambda a: a[:, jc, :]), sgv(lambda a: a[:, jc * 128:(jc + 1) * 128]), MSU64)
                    P.act(Wt, pc1, AF.Exp, scale=-LD)
                    P.act(iW, pc1, AF.Exp, scale=LD)
                    P.act(Wx, pc2, AF.Exp, scale=-LD)
                for cc in range(2):
                    wsl = slice(cc * 64, (cc + 1) * 64)
                    with Scope() as sk:
                        for hh in range(2):
                            pr = slice(hh * 64, (hh + 1) * 64)
                            src = lambda v: v(lambda a: a[pr, :, wsl])
                            dst = lambda v: v(lambda a: a[pr, :, hh * 64:(hh + 1) * 64])
                            P.stt(dst(AtT), src(kk), -1.0, src(Wx), ALU.mult, ALU.mult)
                            P.tt(dst(RtT), src(rview), src(Wt), ALU.mult)
                            P.tt(dst(BtT), src(a_), src(iW), ALU.mult)
                            P.tt(dst(KtT), src(kview), src(iW), ALU.mult)
                            P.cp(dst(VTb), src(vview), eng="scalar")

                        def trans(srcv):
                            o = g4(sk.take(SMB))
                            with Scope() as s2:
                                pt = pbank(s2, bf=True)
                                for jc in range(4):
                                    P.tr(pt(lambda a: a[:, jc, :]), srcv(lambda a: a[:, jc, :]), IDNb)
                                P.cp(o, pt, eng=evac_eng())
                            return o
                        Vb, Btk, Ktk = trans(VTb), trans(BtT), trans(KtT)

                        def mm4(pp, lhsT, rhs, **kw):
                            for jc in range(4):
                                P.mm(pp(lambda a: a[:, jc, :]), lhsT(lambda a: a[:, jc, :]), rhs(lambda a: a[:, jc, :]), **kw)

                        def acc4(pp, terms):
                            for jc in range(4):
                                for ti, (l_, r_) in enumerate(terms):
                                    P.mm(pp(lambda a: a[:, jc, :]), l_(lambda a: a[:, jc, :]), r_(lambda a: a[:, jc, :]),
                                         start=(ti == 0), stop=(ti == len(terms) - 1))

                        inv = Scope()

                        def mmask(lhsT, rhs, mask, own=None):
                            o = g4((own or sk).take(SMB))
                            with Scope() as s2:
                                pp = pbank(s2)
                                mm4(pp, lhsT, rhs)
                                P.tt(o, pp, bc4(mask), ALU.mult)
                            return o
                        X = mmask(BtT, AtT, MSU64, own=inv)
                        Xt = mmask(AtT, BtT, MSL64, own=inv)
                        AakT = mmask(KtT, AtT, MSU64)
                        RbT = mmask(BtT, RtT, MU64)
                        RkT = mmask(KtT, RtT, MU64)
                        N = g4(inv.take(SMB))
                        P.tt(N, bc4(IDN), X, ALU.add)
                        Pm, Pt = X, Xt
                        for k in range(1, 6):
                            nxt = Scope()
                            with Scope() as s2:
                                P2 = None
                                if k < 5:
                                    pp = pbank(s2)
                                    mm4(pp, Pt, Pm)
                                    P2 = g4(nxt.take(SMB))
                                    P.cp(P2, pp, eng=evac_eng())
                                pp2 = pbank(s2)
                                mm4(pp2, Pm, Pt)
                                Pt2 = g4(nxt.take(SMB))
                                P.cp(Pt2, pp2, eng=evac_eng())
                                pn_ = pbank(s2)
                                mm4(pn_, Pt2, N)
                                N2 = g4(nxt.take(SMB))
                                P.tt(N2, N, pn_, ALU.add)
                            inv.__exit__()
                            inv = nxt
                            Pm, Pt, N = P2, Pt2, N2
                        with Scope() as s2:
                            pr0 = pbank(s2)
                            acc4(pr0, [(AtT, Tvb), (AakT, Vb)])
                            R0 = g4(sk.take(SMB))
                            P.cp(R0, pr0, eng="scalar")
                            pu = pbank(s2)
                            mm4(pu, N, R0)
                            U = g4(sk.take(SMB))
                            P.cp(U, pu, eng="scalar")
                        inv.__exit__()
                        with Scope() as s2:
                            py = pbank(s2)
                            acc4(py, [(Tvb, RtT), (U, RbT), (Vb, RkT)])
                            for hh in range(2):
                                pr = slice(hh * 64, (hh + 1) * 64)
                                P.cp(yr(lambda a: a[pr, :, wsl]), py(lambda a: a[pr, :, hh * 64:(hh + 1) * 64]), eng="scalar")
                            pS = pbank(s2)
                            acc4(pS, [(Btk, U), (Ktk, Vb)])
                            P.tt(Tv, Tv, pS, ALU.add)
                            P.tt(Tv, Tv, Wt(lambda a: a[:, :, cc * 64 + 63:cc * 64 + 64].to_broadcast([128, 4, 128])), ALU.mult)
                            P.cp(Tvb, Tv, eng="scalar")
              if True:
                with Scope() as s2:
                    pm = pbank(s2)
                    for jc in range(4):
                        P.mm(pm(lambda a: a[:, jc, :]), BONES, yr(lambda a: a[:, jc, :]))
                    hc = g4(s2.take(BIG))
                    P.stt(hc, pm, -1.0 / 64, yr, ALU.mult, ALU.add)
                    sq = g4(s2.take(BIG))
                    P.act(sq, hc, AF.Square)
                    pv = pbank(s2)
                    for jc in range(4):
                        P.mm(pv(lambda a: a[:, jc, :]), BONES, sq(lambda a: a[:, jc, :]))
                    P.act(sq, pv, AF.Ln, bias=64e-5, scale=1.0 / 64)
                    P.act(sq, sq, AF.Exp, scale=-0.5)
                    P.tt(hc, hc, sq, ALU.mult)
                    P.tt(hc, hc, prmb(P_RLNG), ALU.mult)
                    P.tt(hc, hc, prmb(P_RLNB), ALU.add)
                    P.tt(sq, rview, prmb(P_RRK), ALU.mult)
                    P.tt(sq, sq, kview, ALU.mult)
                    pb_ = pbank(s2)
                    for jc in range(4):
                        P.mm(pb_(lambda a: a[:, jc, :]), BONES, sq(lambda a: a[:, jc, :]))
                    P.tt(sq, pb_, vview, ALU.mult)
                    P.tt(hc, hc, sq, ALU.add)
                    P.tt(V(yT[:, :, sl], [("y", k) for k in range(4)]), hc, zview, ALU.mult)

    MIX = {"gla": gla, "rwkv": rwkv, "ssd": ssd, "mlstm": mlstm}

    def finish():
        fin = {}
        for key, cnt in P.dma_cnt.items():
            if key[1].startswith("out_") or key[1].startswith("dbg_"):
                fin[key] = cnt
        P._emit_waits("sync", fin)
        P.emit()
        P.close()
        for cm in reversed(cms):
            cm.__exit__(None, None, None)
        return nc, P

    if upto == 'init':
        return finish()
    for tile in range(n_tiles):
        tok0 = tile * T
        for s in range(NSUB):
            for jq in range(4):
                with Scope() as sc:
                    xi = sc.take(BIG)
                    r0 = tok0 + s * 128
                    P.dma("sync", "xin_%s" % str(xi.keys[0][1]), xi.ap, x_d[r0:r0 + 128, jq * 512:(jq + 1) * 512],
                          writes=xi.keys)
                    pp = sc.take(PB)
                    for q in range(4):
                        P.tr(pp(lambda a: a[:, q * 128:(q + 1) * 128]), xi(lambda a: a[:, q * 128:(q + 1) * 128]), IDN)
                    P.cp(V(xT[:, jq * 4:(jq + 1) * 4, s * 128:(s + 1) * 128], [("xT", jq * 4 + q) for q in range(4)]),
                         pp(lambda a: a.rearrange("p (q t) -> p q t", q=4)), eng=evac_eng())
        if upto == 'xload':
            return finish()
        for l in range(depth):
            P.dma("gpsimd", "gkw", gkw1[:], gkw_d[l], writes=["gkw"])
            P.dma("gpsimd", "rw2", rw21[:], rw2_d[l], writes=["rw2"])
            P.dma("gpsimd", "ra2", ra21[:], ra2_d[l], writes=["ra2"])
            rmsnorm(P_NG, l, lambda j: V(hT[:, j, :], [("hT", j)]))
            for m in mixers:
                if m == "gla":
                    gla_pre(l)
                inproj(l, m)
                if upto == 'inproj':
                    return finish()
                MIX[m](l)
                if debug:
                    mi = MIXN.index(m)
                    for k in range(4):
                        with Scope() as sc:
                            yk = sc.take(BIG)
                            P.cp(yk, V(yT[:, k, :], [("y", k)]))
                            P.dma("gpsimd", "dbg_%s" % str(yk.keys[0][1]),
                                  dbg_d[l, mi * 512 + k * 128:mi * 512 + (k + 1) * 128, tok0:tok0 + T],
                                  yk.ap, reads=yk.keys, writes=["dbgout"])
                outproj(l, m)
        if upto == 'norm':
            return finish()
        rmsnorm(P_FNG, depth - 1, lambda j: cb(j))
        if upto == 'fnorm':
            return finish()
        for s in range(NSUB):
            for jq in range(4):
                with Scope() as sc:
                    pp = sc.take(PB)
                    for q in range(4):
                        P.tr(pp(lambda a: a[:, q * 128:(q + 1) * 128]), cbs(jq * 4 + q, s), IDN)
                    xo = sc.take(BIG)
                    P.cp(xo, pp, eng=evac_eng())
                    r0 = tok0 + s * 128
                    P.dma("sync", "out_%s" % str(xo.keys[0][1]), out_d[r0:r0 + 128, jq * 512:(jq + 1) * 512], xo.ap,
                          reads=xo.keys, writes=["outd"])
    return finish()


def make_in_maps(inputs, batch_of_core):
    cst = make_consts()
    shared = {"cst": cst}
    for l in range(2):
        g = lambda k: np.ascontiguousarray(np.asarray(inputs["%s_%d" % (k, l)], np.float32))
        shared["win%d" % l] = g("w_in")
        shared["wout%d" % l] = g("w_out")
        shared["prm%d" % l] = pack_params(inputs, l)
        shared["gkw%d" % l] = np.concatenate([g("gla_gk_w2"), g("gla_gk_b")[None, :]], axis=0)
        shared["rw2%d" % l] = np.concatenate([g("rwkv_w2"), g("rwkv_w0")[None, :]], axis=0)
        shared["ra2%d" % l] = np.concatenate([np.concatenate([g("rwkv_a2"), g("rwkv_a0")[None, :]], axis=0), np.concatenate([g("rwkv_v2") if l > 0 else np.zeros((32, 512), np.float32), np.zeros((33, 512), np.float32)], axis=0)], axis=1)
    x = np.asarray(inputs["x"], np.float32)
    maps = []
    for b in batch_of_core:
        d = dict(shared)
        d["x"] = np.ascontiguousarray(x[b])
        maps.append(d)
    return maps


_INPUT_NAMES = (
    'x',
    'norm_g_0',
    'w_in_0',
    'w_out_0',
    'gla_gk_w2_0',
    'gla_gk_b_0',
    'gla_norm_g_0',
    'rwkv_mu_0',
    'rwkv_w0_0',
    'rwkv_w2_0',
    'rwkv_a0_0',
    'rwkv_a2_0',
    'rwkv_k_k_0',
    'rwkv_k_a_0',
    'rwkv_r_k_0',
    'rwkv_ln_g_0',
    'rwkv_ln_b_0',
    'ssd_conv_w_0',
    'ssd_conv_b_0',
    'ssd_dt_bias_0',
    'ssd_a_log_0',
    'ssd_d_0',
    'ssd_norm_g_0',
    'mlstm_conv_w_0',
    'mlstm_conv_b_0',
    'mlstm_ig_b_0',
    'mlstm_fg_b_0',
    'mlstm_norm_g_0',
    'norm_g_1',
    'w_in_1',
    'w_out_1',
    'gla_gk_w2_1',
    'gla_gk_b_1',
    'gla_norm_g_1',
    'rwkv_mu_1',
    'rwkv_w0_1',
    'rwkv_w2_1',
    'rwkv_a0_1',
    'rwkv_a2_1',
    'rwkv_v0_1',
    'rwkv_v2_1',
    'rwkv_k_k_1',
    'rwkv_k_a_1',
    'rwkv_r_k_1',
    'rwkv_ln_g_1',
    'rwkv_ln_b_1',
    'ssd_conv_w_1',
    'ssd_conv_b_1',
    'ssd_dt_bias_1',
    'ssd_a_log_1',
    'ssd_d_1',
    'ssd_norm_g_1',
    'mlstm_conv_w_1',
    'mlstm_conv_b_1',
    'mlstm_ig_b_1',
    'mlstm_fg_b_1',
    'mlstm_norm_g_1',
    'final_norm_g',
)


_CACHE = {}


def kernel(**inputs):
    inputs = {n: inputs[n] for n in _INPUT_NAMES}
    if "nc" not in _CACHE:
        _CACHE["nc"] = build_program()[0]
    nc = _CACHE["nc"]
    batch_of_core = [i // 2 for i in range(8)]
    maps = make_in_maps(inputs, batch_of_core)
    res = run_bass_kernel_spmd(nc, maps, core_ids=list(range(8)))
    out = np.stack([res.results[2 * b]["out"] for b in range(4)], axis=0)
    return out.astype(np.float32)
```
